# Optimizing a Trainium2 kernel written in Bass

```python
import functools
import jax, jax.numpy as jnp
from jax import lax
import numpy as np

D_MODEL = 1024
BATCH = 2
SEQ = 8192
DEPTH = 4
DEC_BATCH = 128
DEC_SEQ = 1
PAST_LEN = 8192
PAGE_SIZE = 128

N_HEADS_A = 8
N_KV_A = 2
HEAD_DIM_A = 64
GROUP_A = N_HEADS_A // N_KV_A
WINDOW = 128
BLOCK = WINDOW
N_HEADS_R = 4
DK_R = 128
DV_R = 256
CHUNK = 128
D_FF = 2816
ALPHA = (2.0 * DEPTH) ** 0.25
BETA = (8.0 * DEPTH) ** -0.25
LN_EPS = 1e-5
GN_EPS = 1e-6
NEG = -1e30

QA = N_HEADS_A * HEAD_DIM_A
KA = N_KV_A * HEAD_DIM_A
QR = N_HEADS_R * DK_R
VR = N_HEADS_R * DV_R
IN_COLS = QA + 2 * KA + 2 * QR + 2 * VR + 2 * D_MODEL
IN_SPLITS = (QA, QA + KA, QA + 2 * KA, QA + 2 * KA + QR, QA + 2 * KA + 2 * QR,
             QA + 2 * KA + 2 * QR + VR, QA + 2 * KA + 2 * QR + 2 * VR)

kernel_name = 'swa_sink_gqa_retention_macaron_deepnorm'


def layer_norm(x, g, b):
    xf = x.astype(jnp.float32)
    mu = jnp.mean(xf, axis=-1, keepdims=True)
    var = jnp.mean(jnp.square(xf - mu), axis=-1, keepdims=True)
    return ((xf - mu) * lax.rsqrt(var + LN_EPS) * g + b).astype(x.dtype)


def swiglu_half_step(x, w_gu, w_dn, g, b):
    a, u = jnp.split(x @ w_gu, 2, axis=-1)
    return layer_norm(ALPHA * x + 0.5 * ((jax.nn.silu(a) * u) @ w_dn), g, b)


def alibi_slopes():
    return 2.0 ** (-8.0 * jnp.arange(1, N_HEADS_A + 1, dtype=jnp.float32) / N_HEADS_A)


def retention_log_decay():
    return jnp.log1p(-(2.0 ** (-5.0 - jnp.arange(N_HEADS_R, dtype=jnp.float32))))


def sink_attend(q, k, v, dist, valid, sinks):
    s = jnp.einsum('...qkgd,...skd->...kgqs', q.astype(jnp.float32), k.astype(jnp.float32)) * (HEAD_DIM_A ** -0.5)
    slopes = alibi_slopes().reshape(N_KV_A, GROUP_A, 1, 1)
    s = jnp.where(valid, s - slopes * dist, NEG)
    sink = sinks.astype(jnp.float32).reshape(N_KV_A, GROUP_A, 1, 1)
    m = jnp.maximum(jnp.max(s, axis=-1, keepdims=True), sink)
    p = jnp.exp(s - m)
    p = p / (jnp.sum(p, axis=-1, keepdims=True) + jnp.exp(sink - m))
    return jnp.einsum('...kgqs,...skd->...qkgd', p, v.astype(jnp.float32))


def swa_prompt(q, k, v, sinks):
    B, T = q.shape[0], q.shape[1]
    nb = T // BLOCK
    qb = q.reshape(B, nb, BLOCK, N_KV_A, GROUP_A, HEAD_DIM_A)
    kb = k.reshape(B, nb, BLOCK, N_KV_A, HEAD_DIM_A)
    vb = v.reshape(B, nb, BLOCK, N_KV_A, HEAD_DIM_A)

    def with_prev(xb):
        prev = jnp.pad(xb[:, :-1], ((0, 0), (1, 0), (0, 0), (0, 0), (0, 0)))
        return jnp.concatenate([prev, xb], axis=2)

    i = jnp.arange(BLOCK)[:, None]
    j = jnp.arange(2 * BLOCK)[None, :]
    dist = i + BLOCK - j
    blk = jnp.arange(nb)[:, None, None]
    valid = (dist >= 0) & (dist <= WINDOW) & (blk * BLOCK - BLOCK + j >= 0)
    o = sink_attend(qb, with_prev(kb), with_prev(vb), dist.astype(jnp.float32), valid[:, None, None], sinks)
    return o.reshape(B, T, QA).astype(q.dtype), (k[:, -WINDOW:], v[:, -WINDOW:])


def swa_sample(q, k, v, sinks, cache_k, cache_v):
    DB, Tq = q.shape[0], q.shape[1]
    keys = jnp.concatenate([cache_k.astype(k.dtype), k], axis=1)
    vals = jnp.concatenate([cache_v.astype(v.dtype), v], axis=1)
    i = jnp.arange(Tq)[:, None]
    j = jnp.arange(WINDOW + Tq)[None, :]
    dist = i + WINDOW - j
    valid = (dist >= 0) & (dist <= WINDOW)
    o = sink_attend(q.reshape(DB, Tq, N_KV_A, GROUP_A, HEAD_DIM_A), keys, vals,
                    dist.astype(jnp.float32), valid, sinks)
    return o.reshape(DB, Tq, QA).astype(q.dtype), (keys[:, -WINDOW:], vals[:, -WINDOW:])


def retention_chunk(q, k, v, s0):
    lg = retention_log_decay()
    C = q.shape[1]
    pos = jnp.arange(C, dtype=jnp.float32)
    diff = pos[:, None] - pos[None, :]
    decay = jnp.where(diff >= 0, jnp.exp(jnp.maximum(diff, 0.0) * lg[:, None, None]), 0.0)
    qf, kf, vf = q.astype(jnp.float32), k.astype(jnp.float32), v.astype(jnp.float32)
    scores = jnp.einsum('bihd,bjhd->bhij', qf, kf) * decay
    o = jnp.einsum('bhij,bjhe->bihe', scores, vf)
    o = o + jnp.einsum('bihd,bhde->bihe', qf, s0) * jnp.exp((pos + 1.0)[:, None] * lg)[None, :, :, None]
    k_dec = kf * jnp.exp((C - 1.0 - pos)[:, None] * lg)[None, :, :, None]
    s_new = jnp.exp(C * lg)[None, :, None, None] * s0 + jnp.einsum('bjhd,bjhe->bhde', k_dec, vf)
    return o, s_new


def retention_prompt(q, k, v):
    B, T = q.shape[0], q.shape[1]
    nc = T // CHUNK

    def to_chunks(a):
        return jnp.transpose(a.reshape(B, nc, CHUNK, a.shape[2], a.shape[3]), (1, 0, 2, 3, 4))

    def step(s, qkv):
        o, s_new = retention_chunk(qkv[0], qkv[1], qkv[2], s)
        return s_new, o

    s0 = jnp.zeros((B, N_HEADS_R, DK_R, DV_R), jnp.float32)
    s_fin, o = lax.scan(step, s0, (to_chunks(q), to_chunks(k), to_chunks(v)))
    o = jnp.transpose(o, (1, 0, 2, 3, 4)).reshape(B, T, N_HEADS_R, DV_R)
    return o, s_fin


def retention_sample(q, k, v, state):
    return retention_chunk(q, k, v, state.astype(jnp.float32))


def mixing_sublayer(x, w_in, sinks, gn_g, w_br_a, w_br_r, w_o, g, b, attend, retain):
    Bx, T = x.shape[0], x.shape[1]
    z = x @ w_in
    q_a, k_a, v_a, q_r, k_r, v_r, g_r, gate = jnp.split(z, IN_SPLITS, axis=-1)
    o_a, win = attend(q_a.reshape(Bx, T, N_HEADS_A, HEAD_DIM_A),
                      k_a.reshape(Bx, T, N_KV_A, HEAD_DIM_A),
                      v_a.reshape(Bx, T, N_KV_A, HEAD_DIM_A), sinks)
    o_r, s_new = retain(q_r.reshape(Bx, T, N_HEADS_R, DK_R),
                        k_r.reshape(Bx, T, N_HEADS_R, DK_R) * (DK_R ** -0.5),
                        v_r.reshape(Bx, T, N_HEADS_R, DV_R))
    mu = jnp.mean(o_r, axis=-1, keepdims=True)
    var = jnp.mean(jnp.square(o_r - mu), axis=-1, keepdims=True)
    o_r = ((o_r - mu) * lax.rsqrt(var + GN_EPS)).reshape(Bx, T, VR) * gn_g
    o_r = (o_r * jax.nn.silu(g_r.astype(jnp.float32))).astype(x.dtype)
    g_a, g_b = jnp.split(jax.nn.sigmoid(gate), 2, axis=-1)
    merged = g_a * (o_a @ w_br_a) + g_b * (o_r @ w_br_r)
    return layer_norm(ALPHA * x + merged @ w_o, g, b), win, s_new


def setup_inputs(seed: int = 0) -> dict:
    key = jax.random.key(seed)
    ks = jax.random.split(key, 24)
    f32 = jnp.float32

    def nrm(k, shape, scale):
        return jax.random.normal(k, shape, f32) * scale

    col_scale = jnp.concatenate([
        jnp.ones((QA + KA,), f32), jnp.full((KA,), BETA, f32),
        jnp.ones((2 * QR,), f32), jnp.full((VR,), BETA, f32),
        jnp.ones((VR + 2 * D_MODEL,), f32)])
    return {
        'x_prompt': nrm(ks[0], (BATCH, SEQ, D_MODEL), 1.0),
        'x_sample': nrm(ks[1], (DEC_BATCH, DEC_SEQ, D_MODEL), 1.0),
        'cache_win_k': nrm(ks[2], (DEPTH, DEC_BATCH, WINDOW, N_KV_A, HEAD_DIM_A), 1.0),
        'cache_win_v': nrm(ks[3], (DEPTH, DEC_BATCH, WINDOW, N_KV_A, HEAD_DIM_A), BETA),
        'state_ret': nrm(ks[4], (DEPTH, DEC_BATCH, N_HEADS_R, DK_R, DV_R), 0.1),
        'w_ff1_gu': nrm(ks[5], (DEPTH, D_MODEL, 2 * D_FF), BETA * D_MODEL ** -0.5),
        'w_ff1_dn': nrm(ks[6], (DEPTH, D_FF, D_MODEL), BETA * D_FF ** -0.5),
        'ln1_g': 1.0 + nrm(ks[7], (DEPTH, D_MODEL), 0.02),
        'ln1_b': nrm(ks[8], (DEPTH, D_MODEL), 0.02),
        'w_in': nrm(ks[9], (DEPTH, D_MODEL, IN_COLS), D_MODEL ** -0.5) * col_scale,
        'attn_sinks': nrm(ks[10], (DEPTH, N_HEADS_A), 0.5),
        'ret_gn_g': 1.0 + nrm(ks[11], (DEPTH, VR), 0.02),
        'w_br_a': nrm(ks[12], (DEPTH, QA, D_MODEL), BETA * QA ** -0.5),
        'w_br_r': nrm(ks[13], (DEPTH, VR, D_MODEL), BETA * VR ** -0.5),
        'w_o': nrm(ks[14], (DEPTH, D_MODEL, D_MODEL), BETA * D_MODEL ** -0.5),
        'ln2_g': 1.0 + nrm(ks[15], (DEPTH, D_MODEL), 0.02),
        'ln2_b': nrm(ks[16], (DEPTH, D_MODEL), 0.02),
        'w_ff2_gu': nrm(ks[17], (DEPTH, D_MODEL, 2 * D_FF), BETA * D_MODEL ** -0.5),
        'w_ff2_dn': nrm(ks[18], (DEPTH, D_FF, D_MODEL), BETA * D_FF ** -0.5),
        'ln3_g': 1.0 + nrm(ks[19], (DEPTH, D_MODEL), 0.02),
        'ln3_b': nrm(ks[20], (DEPTH, D_MODEL), 0.02),
    }


def reference(x_prompt, x_sample, cache_win_k, cache_win_v, state_ret,
              w_ff1_gu, w_ff1_dn, ln1_g, ln1_b, w_in, attn_sinks, ret_gn_g,
              w_br_a, w_br_r, w_o, ln2_g, ln2_b, w_ff2_gu, w_ff2_dn, ln3_g, ln3_b):
    xp, xs = x_prompt, x_sample
    pk, pv, pr, sk, sv, sr = [], [], [], [], [], []
    for l in range(DEPTH):
        mix_w = (w_in[l], attn_sinks[l], ret_gn_g[l], w_br_a[l], w_br_r[l], w_o[l], ln2_g[l], ln2_b[l])
        xp = swiglu_half_step(xp, w_ff1_gu[l], w_ff1_dn[l], ln1_g[l], ln1_b[l])
        xs = swiglu_half_step(xs, w_ff1_gu[l], w_ff1_dn[l], ln1_g[l], ln1_b[l])
        xp, (k_new, v_new), s_new = mixing_sublayer(xp, *mix_w, swa_prompt, retention_prompt)
        pk.append(k_new)
        pv.append(v_new)
        pr.append(s_new)
        xs, (k_new, v_new), s_new = mixing_sublayer(
            xs, *mix_w,
            functools.partial(swa_sample, cache_k=cache_win_k[l], cache_v=cache_win_v[l]),
            functools.partial(retention_sample, state=state_ret[l]))
        sk.append(k_new)
        sv.append(v_new)
        sr.append(s_new)
        xp = swiglu_half_step(xp, w_ff2_gu[l], w_ff2_dn[l], ln3_g[l], ln3_b[l])
        xs = swiglu_half_step(xs, w_ff2_gu[l], w_ff2_dn[l], ln3_g[l], ln3_b[l])
    return (xp, xs, jnp.stack(pk), jnp.stack(pv), jnp.stack(pr), jnp.stack(sk), jnp.stack(sv), jnp.stack(sr))
```

```python
import os
import numpy as np
import concourse.bass as bass
import concourse.mybir as mybir
from concourse.bass_utils import run_bass_kernel_spmd

F32 = mybir.dt.float32
BF16 = mybir.dt.bfloat16
AF = mybir.ActivationFunctionType
ALU = mybir.AluOpType
AX = mybir.AxisListType

D = 1024
DFF = 2816
KD = 8
KF = 22
INC = 5888
INC2 = 6016
C_KAS = 5888
DEPTH_FULL = 4
ALPHA = (2.0 * DEPTH_FULL) ** 0.25
LN_EPS = 1e-5
GN_EPS = 1e-6
NH_A = 8
NH_R = 4
NSAMP = 16
TT = 256
NB = 2
C_QA, C_KA, C_VA, C_QR, C_KR, C_VR, C_GR, C_GA, C_GB = 0, 512, 640, 768, 1280, 1792, 2816, 3840, 4864
EPOCH = int(os.environ.get('KEPOCH', '8000'))
DEPOCH = int(os.environ.get('KDEPOCH', '1500'))
CINC = int(os.environ.get('KINC', '1'))


class Sched:
    def __init__(self):
        self.ops = []
        self.last_w = {}
        self.readers = {}
        self.seg = "pre"
        self.loopvar = {}

    def add(self, eng, fn, reads=(), writes=(), dma=False):
        deps = set()
        for r in reads:
            w = self.last_w.get(r)
            if w is not None:
                deps.add(w)
            if isinstance(r, tuple) and r[0] in ("ps", "pt"):
                deps.update(self.readers.get(r, ()))
        for r in writes:
            w = self.last_w.get(r)
            if w is not None:
                deps.add(w)
            deps.update(self.readers.get(r, ()))
        idx = len(self.ops)
        self.ops.append(dict(eng=eng, fn=fn, deps=deps, dma=dma, seg=self.seg))
        for r in reads:
            self.readers.setdefault(r, []).append(idx)
        for r in writes:
            self.last_w[r] = idx
            self.readers[r] = []
        return idx

    def emit(self, nc, n_iter, final_eng="sp"):
        import contextlib
        ops = self.ops
        engs = ["pe", "act", "dve", "pool", "sp"]
        segA = [i for i, op in enumerate(ops) if op["seg"] == "A"]
        segB = [i for i, op in enumerate(ops) if op["seg"] == "B"]
        assert len(segA) == len(segB)
        a2b = dict(zip(segA, segB))
        for ia, ib in a2b.items():
            assert ops[ia]["eng"] == ops[ib]["eng"] and ops[ia]["dma"] == ops[ib]["dma"]
        NBODY_C = 2
        NBODY_D = 12
        sem_of = {}
        counters = {}
        body_cnt = {e: sum(1 for i in segB if ops[i]["eng"] == e and not ops[i]["dma"]) for e in engs}
        body_seen = {}
        dma_rr = {}
        for i, op in enumerate(ops):
            e, sg = op["eng"], op["seg"]
            if sg == "A":
                continue
            if sg == "B":
                if op["dma"]:
                    k = dma_rr.get(("B", e), 0)
                    dma_rr[("B", e)] = k + 1
                    key = ("bd", e, k % NBODY_D)
                    c = counters.get(key, 0) + 16
                else:
                    n = body_seen.get(e, 0)
                    body_seen[e] = n + 1
                    half = (n * NBODY_C) // max(body_cnt[e], 1)
                    key = ("bc", e, half)
                    c = counters.get(key, 0) + 1
                counters[key] = c
                sem_of[i] = (key, c)
                continue
            if op["dma"]:
                k = dma_rr.get((sg, e), 0)
                dma_rr[(sg, e)] = k + 1
                key = ("d", sg, e, k % 6, (k // 6) // DEPOCH)
                c = counters.get(key, 0) + 16
            else:
                tot = counters.get(("n", sg, e), 0)
                counters[("n", sg, e)] = tot + 1
                key = ("c", sg, e, tot // EPOCH)
                c = counters.get(key, 0) + 1
            counters[key] = c
            sem_of[i] = (key, c)
        delta = {k: v for k, v in counters.items() if k[0] in ("bc", "bd")}
        keys = sorted({k for k, _ in sem_of.values()}, key=str)
        if os.environ.get("KSTATS"):
            print("SCHED ops:", {e: sum(1 for op in ops if op["eng"] == e and op["seg"] != "A") for e in engs},
                  "body:", body_cnt, "nsems", len(keys), "delta", delta, flush=True)
        sem_handles = {}
        with contextlib.ExitStack() as st:
            for k in keys:
                sem_handles[k] = st.enter_context(nc.semaphore("s_" + "_".join(str(x) for x in k)))
            pre_final = {}
            all_final = {}
            for i, op in enumerate(ops):
                if op["seg"] == "A":
                    continue
                k, v = sem_of[i]
                if op["seg"] == "pre":
                    pre_final[k] = max(pre_final.get(k, 0), v)
                if op["seg"] == "B":
                    v = (n_iter) * delta[k] + v
                all_final[k] = max(all_final.get(k, 0), v)
            owner = {}
            for i, op in enumerate(ops):
                if op["seg"] == "B":
                    owner[sem_of[i][0]] = op["eng"]

            def run(en, handle):
                mine = [i for i, op in enumerate(ops) if op["eng"] == en]
                waited = {}
                for i in mine:
                    op = ops[i]
                    if op["seg"] != "pre":
                        continue
                    need = {}
                    for d in op["deps"]:
                        dop = ops[d]
                        if dop["eng"] == "pe" and en == "pe" and not dop["dma"]:
                            continue
                        k, v = sem_of[d]
                        if waited.get(k, 0) >= v:
                            continue
                        need[k] = max(need.get(k, 0), v)
                    for k, v in need.items():
                        handle.wait_ge(sem_handles[k], v)
                        waited[k] = v
                    ins = op["fn"](handle)
                    k, v = sem_of[i]
                    ins.then_inc(sem_handles[k], 16 if op["dma"] else 1)
                for k, eo in owner.items():
                    if eo != en:
                        continue
                    rem = delta[k]
                    while rem > 0:
                        stp = min(rem, 240)
                        handle.sem_inc(sem_handles[k], stp)
                        rem -= stp
                for k, v in pre_final.items():
                    if waited.get(k, 0) < v:
                        handle.wait_ge(sem_handles[k], v)
                        waited[k] = v
                body = [i for i in mine if ops[i]["seg"] == "B"]
                if body:
                    rtmp = nc.alloc_register(handle.engine, "wtmp_" + en)
                    with handle.Fori(0, n_iter) as it:
                        self.loopvar[en] = it
                        w_same = {}
                        w_prev = {}
                        for i in body:
                            op = ops[i]
                            need_same = {}
                            need_prev = {}
                            for d in op["deps"]:
                                dop = ops[d]
                                if dop["seg"] == "pre":
                                    continue
                                if dop["eng"] == "pe" and en == "pe" and not dop["dma"]:
                                    continue
                                if dop["seg"] == "A":
                                    k, v = sem_of[a2b[d]]
                                    if w_same.get(k, 0) >= 1 or w_prev.get(k, 0) >= v:
                                        continue
                                    need_prev[k] = max(need_prev.get(k, 0), v)
                                else:
                                    k, v = sem_of[d]
                                    if w_same.get(k, 0) >= v:
                                        continue
                                    need_same[k] = max(need_same.get(k, 0), v)
                            for k, v in need_prev.items():
                                if k in need_same:
                                    continue
                                handle.reg_mul(rtmp, it, delta[k])
                                handle.reg_add(rtmp, rtmp, v)
                                handle.wait_ge(sem_handles[k], rtmp)
                                w_prev[k] = v
                            for k, v in need_same.items():
                                handle.reg_mul(rtmp, it, delta[k])
                                handle.reg_add(rtmp, rtmp, delta[k] + v)
                                handle.wait_ge(sem_handles[k], rtmp)
                                w_same[k] = v
                            ins = op["fn"](handle)
                            k, v = sem_of[i]
                            ins.then_inc(sem_handles[k], 16 if op["dma"] else 1)
                        self.loopvar[en] = None
                waited = dict(waited)
                for i in mine:
                    op = ops[i]
                    if op["seg"] != "post":
                        continue
                    need = {}
                    for d in op["deps"]:
                        dop = ops[d]
                        if dop["seg"] == "pre":
                            continue
                        if dop["eng"] == "pe" and en == "pe" and not dop["dma"]:
                            continue
                        k, v = sem_of[d]
                        if dop["seg"] == "B":
                            v = n_iter * delta[k] + v
                        if waited.get(k, 0) >= v:
                            continue
                        need[k] = max(need.get(k, 0), v)
                    for k, v in need.items():
                        handle.wait_ge(sem_handles[k], v)
                        waited[k] = v
                    ins = op["fn"](handle)
                    k, v = sem_of[i]
                    ins.then_inc(sem_handles[k], 16 if op["dma"] else 1)
                if en == final_eng:
                    for k, v in all_final.items():
                        if waited.get(k, 0) < v:
                            handle.wait_ge(sem_handles[k], v)

            with nc.Block() as block:
                @block.tensor
                def _(h):
                    run("pe", h)

                @block.scalar
                def _(h):
                    run("act", h)

                @block.vector
                def _(h):
                    run("dve", h)

                @block.gpsimd
                def _(h):
                    run("pool", h)

                @block.sync
                def _(h):
                    run("sp", h)


def make_consts():
    lg = np.log1p(-(2.0 ** (-5.0 - np.arange(4, dtype=np.float64))))
    pos = np.arange(128, dtype=np.float64)
    c = {}
    c["ident"] = np.eye(128, dtype=np.float32)
    diff = pos[None, :] - pos[:, None]
    dec = np.zeros((128, 4, 128), np.float64)
    for h in range(4):
        dec[:, h, :] = np.where(diff >= 0, np.exp(np.maximum(diff, 0) * lg[h]), 0.0) * (128 ** -0.5)
    c["decT"] = dec.astype(np.float32)
    gq = np.zeros((128, 4, 128), np.float64)
    for h in range(4):
        gq[:, h, :] = np.exp((pos + 1.0) * lg[h])[None, :]
    c["gq"] = gq.astype(np.float32)
    ks = np.zeros((128, 4), np.float64)
    for h in range(4):
        ks[:, h] = np.exp((127.0 - pos) * lg[h]) * (128 ** -0.5)
    c["kscale"] = ks.astype(np.float32)
    slopes = 2.0 ** (-8.0 * np.arange(1, 9, dtype=np.float64) / 8)
    i = np.arange(128)[:, None]
    j = np.arange(256)[None, :]
    dist = i + 128 - j
    valid = (dist >= 0) & (dist <= 128)
    b = np.zeros((128, 8, 256), np.float64)
    for h in range(8):
        b[:, h, :] = np.where(valid, -slopes[h] * dist, -1e30)
    c["biasR"] = b.astype(np.float32)
    bs = np.zeros((128, 132), np.float64)
    for p in range(128):
        h = p % 8
        bs[p, :129] = -slopes[h] * (128 - np.arange(129))
    c["biasS"] = bs.astype(np.float32)
    e16 = np.zeros((128, 16), np.float32)
    e16[:16, :] = np.eye(16, dtype=np.float32)
    c["eye16"] = e16
    c["eyeb"] = np.tile(np.eye(16, dtype=np.float32).reshape(1, 256), (128, 1))
    gam = np.exp(lg)
    c["gamma"] = [float(g) for g in gam]
    c["gamma128"] = [float(np.exp(128.0 * l)) for l in lg]
    return c


CONST_LAYOUT = [("ident", 128), ("decT", 512), ("gq", 512), ("kscale", 4), ("biasR", 2048),
                ("biasS", 132), ("eye16", 16), ("eyeb", 256)]


def pack_consts(c):
    cols = sum(n for _, n in CONST_LAYOUT)
    out = np.zeros((128, cols), np.float32)
    o = 0
    offs = {}
    for name, n in CONST_LAYOUT:
        out[:, o:o + n] = c[name].reshape(128, n)
        offs[name] = o
        o += n
    return out, offs


def build(T, L, with_sample=True):
    NT = T // TT
    consts = make_consts()
    cpack, coff = pack_consts(consts)
    NCC = cpack.shape[1]
    nc = bass.Bass("TRN2", target_bir_lowering=False)
    S = Sched()

    def din(name, shape, dt=F32):
        return nc.dram_tensor(name, list(shape), dt, kind="ExternalInput").ap()

    def dout(name, shape, dt=F32):
        return nc.dram_tensor(name, list(shape), dt, kind="ExternalOutput").ap()

    def dscr(name, shape, dt=BF16):
        return nc.dram_tensor(name, list(shape), dt).ap()

    xp = din("xp", [T, D])
    xs = din("xs", [NSAMP, D])
    w_f = {
        "gu1": din("gu1", [L, D, 2 * DFF]), "dn1": din("dn1", [L, DFF, D]),
        "win": din("win", [L, D, INC2]), "bra": din("bra", [L, 512, D]),
        "brr": din("brr", [L, D, D]), "wo": din("wo", [L, D, D]),
        "gu2": din("gu2", [L, D, 2 * DFF]), "dn2": din("dn2", [L, DFF, D]),
    }
    w_b = {k: dscr(k + "_bf", v.shape) for k, v in w_f.items()}
    lng = din("lng", [L, 3, 128, D])
    lnb = din("lnb", [L, 3, 128, D])
    gng = din("gng", [L, 128, D])
    sinkb = din("sinkb", [L, 128, 8])
    sinkbh = din("sinkbh", [L, 128, 1])
    cst = din("cst", [128, NCC])
    bias0 = din("bias0", [NT * 128, 1024])
    ckr = din("ckr", [L, 128, 128, 64])
    cvr = din("cvr", [L, 128, 128, 64])
    ck = din("ck", [L, NSAMP, 128, 128])
    cv = din("cv", [L, NSAMP, 128, 128])
    stt = din("stt", [L, NSAMP, 4, 128, 256])
    yp = dout("yp", [T, D])
    ys = dout("ys", [NSAMP, D])
    nkp = dout("nkp", [L, 128, 128])
    nvp = dout("nvp", [L, 128, 128])
    nrp = dout("nrp", [L, 4, 128, 256])
    nks = dout("nks", [L, NSAMP, 128, 128])
    nvs = dout("nvs", [L, NSAMP, 128, 128])
    nrs = dout("nrs", [L, NSAMP, 4, 128, 256])
    zq = dscr("zq", [L, NSAMP, 512], F32)
    zk = dscr("zk", [L, NSAMP, 128], F32)
    zv = dscr("zv", [L, NSAMP, 128], F32)
    oscr = dscr("oscr", [L, 128, 64], F32)

    import contextlib
    es = contextlib.ExitStack()

    def sb(name, shape, dt=F32):
        return es.enter_context(nc.sbuf_tensor(name, list(shape), dt))

    def pst(name, shape, dt=F32):
        return es.enter_context(nc.psum_tensor(name, list(shape), dt))

    with es:
        cs = sb("cs", [128, NCC])
        identb = sb("identb", [128, 128], BF16)
        x32 = sb("x32", [128, NB, D])
        xT = sb("xT", [128, KD, TT], BF16)
        xb = sb("xb", [128, D], BF16)
        hbuf = sb("hbuf", [128, KF, TT], BF16)
        sabuf = [sb("sa%d" % i, [128, TT], BF16) for i in range(2)]
        NWS = 3
        wslot = [sb("wslot%d" % i, [128, KD, 512], BF16) for i in range(NWS)]
        lngs = sb("lngs", [128, D])
        lnbs = sb("lnbs", [128, D])
        gngs = lngs
        stats = sb("stats", [128, 4, 6])
        mv = sb("mv", [128, 4, 2])
        rstd = sb("rstd", [128, 4])
        gstats = sb("gstats", [128, 4, 6])
        gmv = sb("gmv", [128, 4, 2])
        grstd = sb("grstd", [128, 4])
        qz = sb("qz", [128, 2, 4, TT], BF16)
        kaT = sb("kaT", [128, 2, 128 + TT], BF16)
        qrT = sb("qrT", [128, 4, TT], BF16)
        krT = sb("krT", [128, 4, TT], BF16)
        gT = sb("gT", [128, 16, TT], BF16)
        vA = sb("vA", [128, NB + 1, 128], BF16)
        kdec = sb("kdec", [128, NB, 512], BF16)
        vR = sb("vR", [128, NB, D], BF16)
        gR = sb("gR", [128, NB, D], BF16)
        kv32 = sb("kv32", [128, 256])
        sc = sb("sc", [128, 8, 256])
        pb = sb("pb", [128, 8, 256], BF16)
        pT = sb("pT", [128, 16, 128], BF16)
        mx = sb("mx", [128, 8])
        negm = sb("negm", [128, 8])
        rsum = sb("rsum", [128, 8])
        esk = sb("esk", [128, 8])
        rden = sb("rden", [128, 8])
        sinks = sb("sinks", [128, 8])
        oab = sb("oab", [128, 512], BF16)
        oaT = sb("oaT", [128, 4, TT], BF16)
        scTb = sb("scTb", [128, 4, 128], BF16)
        qdec = sb("qdec", [128, 4, 128], BF16)
        S32 = sb("S32", [128, L, 4 * 256])
        Sb = sb("Sb", [128, L, 4 * 256], BF16)
        haloK = sb("haloK", [128, L, 2, 128], BF16)
        haloV = sb("haloV", [128, L, 128], BF16)
        orn = sb("orn", [128, D])
        orb = sb("orb", [128, D], BF16)
        orT = sb("orT", [128, KD, TT], BF16)
        mT = sb("mT", [128, KD, TT], BF16)
        tmpm = sb("tmpm", [128, TT])
        bias0s = sb("bias0s", [128, 1024])
        if with_sample:
            KQ = 16
            Kt = sb("Kt", [128, KQ, 64], BF16)
            Vt = sb("Vt", [128, KQ, 64], BF16)
            tmpS = sb("tmpS", [128, KQ, 64], BF16)
            qs = sb("qs", [128, 64])
            qsb = sb("qsb", [128, 64], BF16)
            kn = sb("kn", [128, 64])
            vn = sb("vn", [128, 64])
            scS = sb("scS", [128, 132])
            pS = sb("pS", [128, 132])
            pSb = sb("pSb", [128, 132], BF16)
            oS = sb("oS", [128, 64])
            oS2 = sb("oS2", [128, 64])
            smS = sb("smS", [128, 8])
            sinkS = sb("sinkS", [128, 1])
            zs32 = bias0s[0:NSAMP, 0:768]
            oa32 = sb("oa32", [NSAMP, 512])
            qr32 = sb("qr32", [NSAMP, 512])
            kr32 = sb("kr32", [NSAMP, 512])
            vr32 = sc[0:NSAMP, 0:4, :].rearrange("p a s -> p (a s)")
            osam = sc[0:NSAMP, 4:8, :].rearrange("p a s -> p (a s)")
            qsel = sb("qsel", [128, 4, NSAMP, NSAMP])
            qrT32 = sb("qrT32", [128, 4, NSAMP])
            vdiag = sb("vdiag", [NSAMP, 4, 256])
            S0 = sb("S0", [128, 4, 256])
            zt = sb("zt", [128, NSAMP])
            S1 = sb("S1", [128, 4, 256])
            qk = sb("qk", [NSAMP, 4])
            tmq = sb("tmq", [NSAMP, 512])
        ps = [pst("ps%d" % i, [128, 512]) for i in range(6)]
        pt = [pst("pt%d" % i, [128, 1024], BF16) for i in range(2)]

        def PS(i):
            return ("ps", i)

        def PT(i):
            return ("pt", i)

        def WS(i):
            return [("ws", i, 0), ("ws", i, 1)]

        def precast_all(name, l):
            src = w_f[name][l]
            dst = w_b[name][l]
            rows = src.shape[0]
            nsp = 4
            step = rows // nsp
            for i in range(nsp):
                S.add("pool", (lambda e, s=src[i * step:(i + 1) * step, :], d=dst[i * step:(i + 1) * step, :]:
                               e.dma_start(out=d, in_=s)),
                      writes=[("wb", name, l, i)], dma=True)

        def WB(name, l):
            return [("wb", name, l, i) for i in range(4)]

        S.add("sp", lambda e: e.dma_start(out=cs[:], in_=cst), writes=["cs"], dma=True)
        S.add("pool", lambda e: e.dma_start(out=identb[:], in_=cst[:, coff["ident"]:coff["ident"] + 128]),
              writes=["identb"], dma=True)
        STAGE = int(os.environ.get("KSTAGE", "99"))
        SUB = int(os.environ.get("KSUB", "99"))
        KATT = int(os.environ.get("KATT", "99"))
        for l in range(L):
            for name in ("gu1", "dn1", "win", "bra", "brr", "wo", "gu2", "dn2"):
                if STAGE >= 2:
                    precast_all(name, l)

        decT = cs[:, coff["decT"]:coff["decT"] + 512].rearrange("p (h i) -> p h i", h=4)
        gqc = cs[:, coff["gq"]:coff["gq"] + 512].rearrange("p (h i) -> p h i", h=4)
        kscale = cs[:, coff["kscale"]:coff["kscale"] + 4]
        biasR = cs[:, coff["biasR"]:coff["biasR"] + 2048].rearrange("p (h s) -> p h s", h=8)
        biasS = cs[:, coff["biasS"]:coff["biasS"] + 132]
        eyeb = cs[:, coff["eyeb"]:coff["eyeb"] + 256].rearrange("p (a b) -> p a b", a=NSAMP)

        S.add("pool", lambda e: e.memset(S32[:], 0.0), writes=[("S32", l) for l in range(L)])
        S.add("pool", lambda e: e.memset(Sb[:], 0.0), writes=[("Sb", l) for l in range(L)])
        S.add("pool", lambda e: e.memset(haloK[:], 0.0), writes=[("haloK", l) for l in range(L)])
        S.add("pool", lambda e: e.memset(haloV[:], 0.0), writes=[("haloV", l) for l in range(L)])
        S.add("pool", lambda e: e.memset(qz[:], 0.0), writes=[("qaT", i) for i in range(4)] + [("qaT", i, 1) for i in range(4)])

        if with_sample:
            S.add("pool", lambda e: e.memset(zt[:], 0.0), writes=["zt"])
        rr = {"ws": 0, "ps_a": 0, "ps_b": 0, "pt": 0}

        def load_wslot(name, l, r0, nk, c0, ncols):
            i = rr["ws"] % NWS
            rr["ws"] += 1
            src = w_b[name][l][r0 * 128:(r0 + nk) * 128, c0:c0 + ncols].rearrange("(k p) c -> p k c", p=128)
            dst = wslot[i][:, 0:nk, 0:ncols]
            S.add("sp", lambda e: e.dma_start(out=dst, in_=src), reads=WB(name, l), writes=WS(i), dma=True)
            return i

        def transpose_to(src_bf, np_, nk, dst_fn, src_res, dst_res):
            ti = rr["pt"] % 2
            rr["pt"] += 1

            def tr(e):
                ins = None
                for k in range(nk):
                    ins = e.transpose(out=pt[ti][:, k * 128:k * 128 + np_], in_=src_bf[0:np_, k * 128:(k + 1) * 128],
                                      identity=identb[0:np_, 0:np_])
                return ins
            S.add("pe", tr, reads=src_res + ["identb"], writes=[PT(ti)])
            S.add("act", lambda e: e.copy(
                out=dst_fn(), in_=pt[ti][:, 0:nk * 128].rearrange("p (k t) -> p k t", k=nk)[:, :, 0:np_]),
                reads=[PT(ti)], writes=dst_res)

        def layer_norm_blocks(nb, np_, l, which, last_out=None):
            eps = LN_EPS / (ALPHA * ALPHA)
            S.add("sp", lambda e: e.dma_start(out=lngs[:], in_=lng[l, which]), writes=["lng"], dma=True)
            S.add("sp", lambda e: e.dma_start(out=lnbs[:], in_=lnb[l, which]), writes=["lnb"], dma=True)
            for b in range(nb):
                xr = x32[0:np_, b, :]
                S.add("dve", lambda e, xr=xr: e.bn_stats(out=stats[0:np_, 0, :], in_=xr[:, 0:512]),
                      reads=[("x32", b)], writes=["stats"])
                S.add("dve", lambda e, xr=xr: e.bn_stats(out=stats[0:np_, 1, :], in_=xr[:, 512:1024]),
                      reads=[("x32", b)], writes=["stats1"])
                S.add("dve", lambda e: e.bn_aggr(out=mv[0:np_, 0, :],
                                                 in_=stats[0:np_, 0:2, :].rearrange("p a s -> p (a s)")),
                      reads=["stats", "stats1"], writes=["mv"])
                S.add("dve", lambda e: e.tensor_scalar(out=rstd[0:np_, 0:1], in0=mv[0:np_, 0, 1:2], scalar1=eps,
                                                       scalar2=None, op0=ALU.add),
                      reads=["mv"], writes=["rstd"])
                S.add("act", lambda e: e.activation(out=rstd[0:np_, 0:1], in_=rstd[0:np_, 0:1], func=AF.Sqrt),
                      reads=["rstd"], writes=["rstd"])
                S.add("dve", lambda e: e.reciprocal(out=rstd[0:np_, 0:1], in_=rstd[0:np_, 0:1]),
                      reads=["rstd"], writes=["rstd"])
                S.add("dve", lambda e, xr=xr: e.tensor_scalar(out=xr, in0=xr, scalar1=mv[0:np_, 0, 0:1],
                                                              scalar2=rstd[0:np_, 0:1], op0=ALU.subtract, op1=ALU.mult),
                      reads=[("x32", b), "mv", "rstd"], writes=[("x32", b)])
                S.add("pool", lambda e, xr=xr: e.tensor_tensor(out=xr, in0=xr, in1=lngs[0:np_, :], op=ALU.mult),
                      reads=[("x32", b), "lng"], writes=[("x32", b)])
                S.add("pool", lambda e, xr=xr: e.tensor_tensor(out=xr, in0=xr, in1=lnbs[0:np_, :], op=ALU.add),
                      reads=[("x32", b), "lnb"], writes=[("x32", b)])
                if last_out is not None:
                    S.add("sp", lambda e, xr=xr, b=b: e.dma_start(out=last_out(b), in_=xr), reads=[("x32", b)],
                          writes=[("yout", b)], dma=True)
                S.add("act", lambda e, xr=xr: e.copy(out=xb[0:np_, :], in_=xr), reads=[("x32", b)], writes=["xb"])
                transpose_to(xb, np_, KD, (lambda b=b: xT[:, :, b * 128:b * 128 + np_]), ["xb"], [("xT", b)])

        def ffn(nb, np_, l, which, last_out=None):
            ntok = (nb - 1) * 128 + np_
            gname = "gu1" if which == 0 else "gu2"
            dname = "dn1" if which == 0 else "dn2"
            xTr = [("xT", b) for b in range(nb)]
            for fp in range(KF // 2):
                i = rr["ws"] % NWS
                rr["ws"] += 1
                srca = w_b[gname][l][:, fp * 256:(fp + 1) * 256].rearrange("(k p) c -> p k c", p=128)
                srcu = w_b[gname][l][:, DFF + fp * 256:DFF + (fp + 1) * 256].rearrange("(k p) c -> p k c", p=128)
                S.add("sp", lambda e, i=i, srca=srca: e.dma_start(out=wslot[i][:, :, 0:256], in_=srca),
                      reads=WB(gname, l), writes=[("ws", i, 0)], dma=True)
                S.add("sp", lambda e, i=i, srcu=srcu: e.dma_start(out=wslot[i][:, :, 256:512], in_=srcu),
                      reads=WB(gname, l), writes=[("ws", i, 1)], dma=True)
                for j in range(2):
                    f = fp * 2 + j
                    par = f % 2
                    pa, pu = ps[2 * par], ps[2 * par + 1]

                    def mm(e, i=i, j=j, pa=pa, pu=pu):
                        ins = None
                        for k in range(KD):
                            ins = e.matmul(pa[:, 0:ntok], lhsT=wslot[i][:, k, j * 128:(j + 1) * 128],
                                           rhs=xT[:, k, 0:ntok], start=(k == 0), stop=(k == KD - 1))
                        for k in range(KD):
                            ins = e.matmul(pu[:, 0:ntok], lhsT=wslot[i][:, k, 256 + j * 128:256 + (j + 1) * 128],
                                           rhs=xT[:, k, 0:ntok], start=(k == 0), stop=(k == KD - 1))
                        return ins
                    S.add("pe", mm, reads=WS(i) + xTr, writes=[PS(2 * par), PS(2 * par + 1)])
                    S.add("act", lambda e, pa=pa, par=par: e.activation(out=sabuf[par][:, 0:ntok], in_=pa[:, 0:ntok],
                                                                        func=AF.Silu),
                          reads=[PS(2 * par)], writes=[("sa", par)])
                    S.add("dve", lambda e, pu=pu, par=par, f=f: e.tensor_tensor(
                        out=hbuf[:, f, 0:ntok], in0=pu[:, 0:ntok], in1=sabuf[par][:, 0:ntok], op=ALU.mult),
                        reads=[PS(2 * par + 1), ("sa", par)], writes=[("h", f)])
            pieces = [(0, 8), (8, 16), (16, 22)]
            for half in range(2):
                for pc, (f0, f1) in enumerate(pieces):
                    si = load_wslot(dname, l, f0, f1 - f0, half * 512, 512)
                    for b in range(nb):
                        npb = 128 if b < nb - 1 else np_
                        pi = 4 + b

                        def mm2(e, si=si, b=b, npb=npb, pi=pi, f0=f0, f1=f1):
                            ins = None
                            for f in range(f0, f1):
                                ins = e.matmul(ps[pi][0:npb, :], lhsT=hbuf[:, f, b * 128:b * 128 + npb],
                                               rhs=wslot[si][:, f - f0, :], start=(f == 0), stop=(f == KF - 1))
                            return ins
                        S.add("pe", mm2, reads=[("h", f) for f in range(f0, f1)] + WS(si), writes=[PS(pi)])
                for b in range(nb):
                    npb = 128 if b < nb - 1 else np_
                    pi = 4 + b
                    xr = x32[0:npb, b, half * 512:(half + 1) * 512]
                    S.add("dve", lambda e, xr=xr, pi=pi, npb=npb: e.scalar_tensor_tensor(
                        out=xr, in0=ps[pi][0:npb, :], scalar=0.5 / ALPHA, in1=xr, op0=ALU.mult, op1=ALU.add),
                        reads=[PS(pi), ("x32", b)], writes=[("x32", b)])
            layer_norm_blocks(nb, np_, l, 0 if which == 0 else 2, last_out=last_out)

        VR_ATOMS = [0, 256, 768]

        def vr_atoms(name, b):
            return [(name, b, o) for o in VR_ATOMS]

        def mix_proj(nb, np_, l, sample):
            ntok = (nb - 1) * 128 + np_
            xTr = [("xT", b) for b in range(nb)]
            if sample:
                fm = [(C_QR + 128 * i, "qr", i) for i in range(4)] + [(C_GA + 128 * i, "g", i) for i in range(16)]
                tm = [(0, 768, "zs"), (C_QR, C_KR, "qr32"), (C_KR, C_VR, "kr32"), (C_VR, C_GR, "vr"), (C_GR, C_GA, "gr")]
            else:
                fm = [(C_QA + 128 * i, "qa", i) for i in range(4)] + [(C_KA, "ka", 0), (C_KAS, "ka", 1)] + \
                     [(C_QR + 128 * i, "qr", i) for i in range(4)] + [(C_KR + 128 * i, "kr", i) for i in range(4)] + \
                     [(C_GA + 128 * i, "g", i) for i in range(16)]
                tm = [(C_VA, C_QR, "va"), (C_KR, C_VR, "kr"), (C_VR, C_GR, "vr"), (C_GR, C_GA, "gr"), (C_KA, C_VA, "ka32")]
            KP = os.environ.get("KPROJ", "")
            if KP:
                fm = [x for x in fm if x[1] in KP.split(",")]
                tm = [x for x in tm if x[2] in KP.split(",")]
            for g in range(12):
                c0 = g * 512
                ncols = min(512, INC2 - c0)
                si = load_wslot("win", l, 0, KD, c0, ncols)
                for (cc, kind, idx) in fm:
                    if not (c0 <= cc < c0 + ncols):
                        continue
                    pi = rr["ps_a"] % 4
                    rr["ps_a"] += 1
                    lo = cc - c0

                    def mm(e, si=si, lo=lo, pi=pi):
                        ins = None
                        for k in range(KD):
                            ins = e.matmul(ps[pi][:, 0:ntok], lhsT=wslot[si][:, k, lo:lo + 128], rhs=xT[:, k, 0:ntok],
                                           start=(k == 0), stop=(k == KD - 1))
                        return ins
                    S.add("pe", mm, reads=WS(si) + xTr, writes=[PS(pi)])
                    src = ps[pi][:, 0:ntok]
                    if kind == "qa":
                        S.add("act", lambda e, pi=pi, idx=idx: e.copy(out=qz[0:64, 0, idx, 0:ntok],
                                                                       in_=ps[pi][0:64, 0:ntok]),
                              reads=[PS(pi)], writes=[("qaT", idx)])
                        S.add("act", lambda e, pi=pi, idx=idx: e.copy(out=qz[64:128, 1, idx, 0:ntok],
                                                                       in_=ps[pi][64:128, 0:ntok]),
                              reads=[PS(pi)], writes=[("qaT", idx, 1)])
                    elif kind == "ka":
                        S.add("act", lambda e, src=src, idx=idx: e.copy(out=kaT[:, idx, 128:128 + ntok], in_=src),
                              reads=[PS(pi)], writes=[("kaT", idx)])
                    elif kind == "qr":
                        if sample:
                            S.add("act", lambda e, src=src, idx=idx: e.copy(out=qrT32[:, idx, :], in_=src),
                                  reads=[PS(pi)], writes=[("qrT32", idx)])
                        else:
                            S.add("act", lambda e, src=src, idx=idx: e.copy(out=qrT[:, idx, 0:ntok], in_=src),
                                  reads=[PS(pi)], writes=[("qrT", idx)])
                    elif kind == "kr":
                        S.add("act", lambda e, src=src, idx=idx: e.copy(out=krT[:, idx, 0:ntok], in_=src),
                              reads=[PS(pi)], writes=[("krT", idx)])
                    else:
                        S.add("act", lambda e, src=src, idx=idx: e.activation(out=gT[:, idx, 0:ntok], in_=src,
                                                                              func=AF.Sigmoid),
                              reads=[PS(pi)], writes=[("gT", idx)])
                for (a0, a1, kind) in tm:
                    lo_c = max(a0, c0)
                    hi_c = min(a1, c0 + ncols)
                    if lo_c >= hi_c:
                        continue
                    w = hi_c - lo_c
                    for b in range(nb):
                        npb = 128 if b < nb - 1 else np_
                        if kind == "ka32" and not (b == nb - 1):
                            continue
                        pi = 4 + (rr["ps_b"] % 2)
                        rr["ps_b"] += 1

                        def mm(e, si=si, lo=lo_c - c0, w=w, b=b, npb=npb, pi=pi):
                            ins = None
                            for k in range(KD):
                                ins = e.matmul(ps[pi][0:npb, 0:w], lhsT=xT[:, k, b * 128:b * 128 + npb],
                                               rhs=wslot[si][:, k, lo:lo + w], start=(k == 0), stop=(k == KD - 1))
                            return ins
                        S.add("pe", mm, reads=WS(si) + [("xT", b)], writes=[PS(pi)])
                        src = ps[pi][0:npb, 0:w]
                        o = lo_c - a0
                        if kind == "va":
                            S.add("act", lambda e, src=src, b=b: e.copy(out=vA[:, b + 1, :], in_=src),
                                  reads=[PS(pi)], writes=[("vA", b + 1)])
                            if b == nb - 1:
                                S.add("dve", lambda e, src=src: e.tensor_copy(out=kv32[:, 128:256], in_=src),
                                      reads=[PS(pi)], writes=["kv32v"])
                        elif kind == "ka32":
                            S.add("dve", lambda e, src=src: e.tensor_copy(out=kv32[:, 0:128], in_=src),
                                  reads=[PS(pi)], writes=["kv32k"])
                        elif kind == "kr":
                            for hh in range(w // 128):
                                h = (o + hh * 128) // 128
                                S.add("dve", lambda e, pi=pi, hh=hh, h=h, b=b: e.tensor_scalar(
                                    out=kdec[:, b, h * 128:(h + 1) * 128], in0=ps[pi][:, hh * 128:(hh + 1) * 128],
                                    scalar1=kscale[:, h:h + 1], scalar2=None, op0=ALU.mult),
                                    reads=[PS(pi), "cs"], writes=[("kdec", b, h)])
                        elif kind == "vr":
                            if sample:
                                S.add("act", lambda e, src=src, o=o, w=w: e.copy(out=vr32[:, o:o + w], in_=src),
                                      reads=[PS(pi)], writes=[("vr32", o)])
                            else:
                                S.add("act", lambda e, src=src, o=o, w=w, b=b: e.copy(out=vR[:, b, o:o + w], in_=src),
                                      reads=[PS(pi)], writes=[("vR", b, o)])
                        elif kind == "gr":
                            S.add("act", lambda e, src=src, o=o, w=w, b=b, npb=npb: e.activation(
                                out=gR[0:npb, b, o:o + w], in_=src, func=AF.Silu),
                                reads=[PS(pi)], writes=[("gR", b, o)])
                        elif kind == "zs":
                            S.add("act", lambda e, src=src, o=o, w=w: e.copy(out=zs32[:, o:o + w], in_=src),
                                  reads=[PS(pi)], writes=[("zs32", o)])
                        elif kind == "qr32":
                            S.add("act", lambda e, src=src, o=o, w=w: e.copy(out=qr32[:, o:o + w], in_=src),
                                  reads=[PS(pi)], writes=[("qr32", o)])
                        elif kind == "kr32":
                            S.add("act", lambda e, src=src, o=o, w=w: e.copy(out=kr32[:, o:o + w], in_=src),
                                  reads=[PS(pi)], writes=[("kr32", o)])

        def mix_out(nb, np_, l):
            ntok = (nb - 1) * 128 + np_
            oaTr = [("oaT", b) for b in range(nb)]
            orTr = [("orT", b) for b in range(nb)]
            for half in range(2):
                sa_ = load_wslot("bra", l, 0, 4, half * 512, 512)
                sr_ = load_wslot("brr", l, 0, KD, half * 512, 512)
                for mm_ in range(4):
                    m = half * 4 + mm_
                    par = m % 2
                    pa, pr = ps[2 * par], ps[2 * par + 1]

                    def mm(e, mm_=mm_, pa=pa, pr=pr, sa_=sa_, sr_=sr_):
                        ins = None
                        for k in range(4):
                            ins = e.matmul(pa[:, 0:ntok], lhsT=wslot[sa_][:, k, mm_ * 128:(mm_ + 1) * 128],
                                           rhs=oaT[:, k, 0:ntok], start=(k == 0), stop=(k == 3))
                        for k in range(KD):
                            ins = e.matmul(pr[:, 0:ntok], lhsT=wslot[sr_][:, k, mm_ * 128:(mm_ + 1) * 128],
                                           rhs=orT[:, k, 0:ntok], start=(k == 0), stop=(k == KD - 1))
                        return ins
                    S.add("pe", mm, reads=WS(sa_) + WS(sr_) + oaTr + orTr, writes=[PS(2 * par), PS(2 * par + 1)])
                    S.add("dve", lambda e, pa=pa, m=m: e.tensor_tensor(out=tmpm[:, 0:ntok], in0=pa[:, 0:ntok],
                                                                       in1=gT[:, m, 0:ntok], op=ALU.mult),
                          reads=[PS(2 * par), ("gT", m)], writes=["tmpm"])
                    S.add("dve", lambda e, pr=pr, m=m: e.tensor_tensor(out=mT[:, m, 0:ntok], in0=pr[:, 0:ntok],
                                                                       in1=gT[:, 8 + m, 0:ntok], op=ALU.mult),
                          reads=[PS(2 * par + 1), ("gT", 8 + m)], writes=[("mT", m)])
                    S.add("pool", lambda e, m=m: e.tensor_tensor(out=mT[:, m, 0:ntok], in0=mT[:, m, 0:ntok],
                                                                 in1=tmpm[:, 0:ntok], op=ALU.add),
                          reads=["tmpm", ("mT", m)], writes=[("mT", m)])
            for half in range(2):
                si = load_wslot("wo", l, 0, KD, half * 512, 512)
                for b in range(nb):
                    npb = 128 if b < nb - 1 else np_
                    pi = 4 + (rr["ps_b"] % 2)
                    rr["ps_b"] += 1

                    def mm2(e, si=si, b=b, npb=npb, pi=pi):
                        ins = None
                        for k in range(KD):
                            ins = e.matmul(ps[pi][0:npb, :], lhsT=mT[:, k, b * 128:b * 128 + npb], rhs=wslot[si][:, k, :],
                                           start=(k == 0), stop=(k == KD - 1))
                        return ins
                    S.add("pe", mm2, reads=[("mT", m) for m in range(KD)] + WS(si), writes=[PS(pi)])
                    xr = x32[0:npb, b, half * 512:(half + 1) * 512]
                    S.add("dve", lambda e, xr=xr, pi=pi, npb=npb: e.scalar_tensor_tensor(
                        out=xr, in0=ps[pi][0:npb, :], scalar=1.0 / ALPHA, in1=xr, op0=ALU.mult, op1=ALU.add),
                        reads=[PS(pi), ("x32", b)], writes=[("x32", b)])
            layer_norm_blocks(nb, np_, l, 1)

        def gn_block(np_, l, b, src2_fn, src_fn, src_res):
            for h in range(4):
                S.add("dve", lambda e, h=h: e.bn_stats(out=gstats[0:np_, h, :], in_=src_fn(h)),
                      reads=src_res, writes=[("gst", h // 2)] if h % 2 else [("gstx", h // 2)])
            for h in range(4):
                S.add("dve", lambda e, h=h: e.bn_aggr(out=gmv[0:np_, h, :], in_=gstats[0:np_, h, :]),
                      reads=[("gst", 0), ("gst", 1), ("gstx", 0), ("gstx", 1)], writes=[("mvh", h)])
            S.add("dve", lambda e: e.tensor_scalar(out=grstd[0:np_, 0:4], in0=gmv[0:np_, :, 1], scalar1=GN_EPS,
                                                   scalar2=None, op0=ALU.add),
                  reads=[("mvh", h) for h in range(4)], writes=["grstd"])
            S.add("act", lambda e: e.activation(out=grstd[0:np_, 0:4], in_=grstd[0:np_, 0:4], func=AF.Sqrt),
                  reads=["grstd"], writes=["grstd"])
            S.add("dve", lambda e: e.reciprocal(out=grstd[0:np_, 0:4], in_=grstd[0:np_, 0:4]),
                  reads=["grstd"], writes=["grstd"])
            for h in range(4):
                S.add("dve", lambda e, h=h: e.tensor_scalar(out=orn[0:np_, h * 256:(h + 1) * 256], in0=src_fn(h),
                                                            scalar1=gmv[0:np_, h, 0:1], scalar2=grstd[0:np_, h:h + 1],
                                                            op0=ALU.subtract, op1=ALU.mult),
                      reads=src_res + [("mvh", h), "grstd"], writes=[("orn", h)])
            S.add("pool", lambda e: e.tensor_tensor(out=orn[0:np_, :], in0=orn[0:np_, :], in1=gngs[0:np_, :], op=ALU.mult),
                  reads=[("orn", h) for h in range(4)] + ["lng"], writes=[("orn", h) for h in range(4)])
            S.add("pool", lambda e: e.tensor_tensor(out=orb[0:np_, :], in0=orn[0:np_, :], in1=gR[0:np_, b, :], op=ALU.mult),
                  reads=[("orn", h) for h in range(4)] + vr_atoms("gR", b), writes=["orb"])
            transpose_to(orb, np_, KD, (lambda: orT[:, :, b * 128:b * 128 + np_]), ["orb"], [("orT", b)])

        def mix_prompt(l, tile_idx, is_last_tile):
            nb = NB
            S.add("pool", lambda e: e.tensor_copy(out=kaT[:, :, 0:128], in_=haloK[:, l, :, :]), reads=[("haloK", l)],
                  writes=["kaTh"])
            S.add("pool", lambda e: e.tensor_copy(out=vA[:, 0, :], in_=haloV[:, l, :]), reads=[("haloV", l)],
                  writes=[("vA", 0)])
            S.add("sp", lambda e: e.dma_start(out=sinks[:], in_=sinkb[l]), writes=["sinks"], dma=True)
            S.add("sp", lambda e: e.dma_start(out=gngs[:], in_=gng[l]), writes=["lng"], dma=True)
            mix_proj(nb, 128, l, False)
            if SUB < 2:
                return
            S.add("pool", lambda e: e.tensor_copy(out=haloK[:, l, :, :], in_=kaT[:, :, TT:TT + 128]),
                  reads=[("kaT", 0), ("kaT", 1)], writes=[("haloK", l)])
            S.add("pool", lambda e: e.tensor_copy(out=haloV[:, l, :], in_=vA[:, NB, :]), reads=[("vA", NB)],
                  writes=[("haloV", l)])
            if is_last_tile:
                S.add("sp", lambda e: e.dma_start(out=nkp[l], in_=kv32[:, 0:128]), reads=["kv32k"], writes=[("nkp", l)],
                      dma=True)
                S.add("sp", lambda e: e.dma_start(out=nvp[l], in_=kv32[:, 128:256]), reads=["kv32v"], writes=[("nvp", l)],
                      dma=True)
            def blk(b, first, s0, ns, kcol0, nsb):
                if KATT < 1:
                    return
                for hp in range(4):
                    def mm(e, hp=hp, b=b):
                        ins = None
                        for j in range(2):
                            h = 2 * hp + j
                            kv = h // 4
                            which = 0 if kv == j else 1
                            ins = e.matmul(ps[hp][:, j * 256 + s0:(j + 1) * 256],
                                           lhsT=qz[:, j, hp, b * 128:(b + 1) * 128],
                                           rhs=kaT[:, which, kcol0:kcol0 + ns], start=True, stop=True)
                        return ins
                    S.add("pe", mm, reads=[("qaT", hp), ("qaT", hp, 1), ("kaT", 0), ("kaT", 1), "kaTh"], writes=[PS(hp)])
                    if b == 0:
                        b0v = bias0s[:, :].rearrange("p (h s) -> p h s", h=8)
                        S.add("dve", lambda e, hp=hp: e.scalar_tensor_tensor(
                            out=sc[:, 2 * hp:2 * hp + 2, 0:128],
                            in0=ps[hp][:, :].rearrange("p (j s) -> p j s", j=2)[:, :, 0:128], scalar=0.125,
                            in1=b0v[:, 2 * hp:2 * hp + 2, :], op0=ALU.mult, op1=ALU.add),
                            reads=[PS(hp), "bias0s"], writes=[("sc", hp, 0)])
                        S.add("dve", lambda e, hp=hp: e.scalar_tensor_tensor(
                            out=sc[:, 2 * hp:2 * hp + 2, 128:256],
                            in0=ps[hp][:, :].rearrange("p (j s) -> p j s", j=2)[:, :, 128:256], scalar=0.125,
                            in1=biasR[:, 2 * hp:2 * hp + 2, 128:256], op0=ALU.mult, op1=ALU.add),
                            reads=[PS(hp), "cs"], writes=[("sc", hp)])
                    else:
                        S.add("dve", lambda e, hp=hp: e.scalar_tensor_tensor(
                            out=sc[:, 2 * hp:2 * hp + 2, s0:256],
                            in0=ps[hp][:, :].rearrange("p (j s) -> p j s", j=2)[:, :, s0:256], scalar=0.125,
                            in1=biasR[:, 2 * hp:2 * hp + 2, s0:256], op0=ALU.mult, op1=ALU.add),
                            reads=[PS(hp), "cs"], writes=[("sc", hp), ("sc", hp, 0)])
                scr = [("sc", hp) for hp in range(4)] + [("sc", hp, 0) for hp in range(4)]
                if KATT < 2:
                    return
                S.add("dve", lambda e: e.tensor_reduce(out=mx[:], in_=sc[:, :, s0:256], axis=AX.X, op=ALU.max),
                      reads=scr, writes=["mx"])
                S.add("dve", lambda e: e.tensor_tensor(out=mx[:], in0=mx[:], in1=sinks[:], op=ALU.max),
                      reads=["mx", "sinks"], writes=["mx"])
                S.add("dve", lambda e: e.tensor_scalar(out=negm[:], in0=mx[:], scalar1=-1.0, scalar2=None, op0=ALU.mult),
                      reads=["mx"], writes=["negm"])
                S.add("dve", lambda e: e.tensor_tensor(out=esk[:], in0=sinks[:], in1=mx[:], op=ALU.subtract),
                      reads=["mx", "sinks"], writes=["esk"])
                S.add("act", lambda e: e.activation(out=esk[:], in_=esk[:], func=AF.Exp), reads=["esk"], writes=["esk"])
                if KATT < 3:
                    return
                for h in range(8):
                    S.add("act", lambda e, h=h: e.activation(out=pb[:, h, s0:256], in_=sc[:, h, s0:256], func=AF.Exp,
                                                             bias=negm[:, h:h + 1], scale=1.0),
                          reads=scr + ["negm"], writes=[("pb", h)])
                if KATT < 4:
                    return
                S.add("dve", lambda e: e.tensor_reduce(out=rsum[:], in_=pb[:, :, s0:256], axis=AX.X, op=ALU.add),
                      reads=[("pb", h) for h in range(8)], writes=["rsum"])
                S.add("dve", lambda e: e.tensor_tensor(out=rden[:], in0=rsum[:], in1=esk[:], op=ALU.add),
                      reads=["rsum", "esk"], writes=["rden"])
                S.add("dve", lambda e: e.reciprocal(out=rden[:], in_=rden[:]), reads=["rden"], writes=["rden"])
                if SUB < 3:
                    return
                for half in range(2):
                    ti = rr["pt"] % 2
                    rr["pt"] += 1

                    def tr(e, half=half, ti=ti):
                        ins = None
                        for hh in range(4):
                            h = half * 4 + hh
                            for sbk in range(nsb):
                                c0 = s0 + sbk * 128
                                ins = e.transpose(out=pt[ti][:, (hh * 2 + sbk) * 128:(hh * 2 + sbk + 1) * 128],
                                                  in_=pb[:, h, c0:c0 + 128], identity=identb[:])
                        return ins
                    S.add("pe", tr, reads=[("pb", half * 4 + hh) for hh in range(4)] + ["identb"], writes=[PT(ti)])
                    S.add("act", lambda e, half=half, ti=ti: e.copy(out=pT[:, half * 8:(half + 1) * 8, :],
                                                                    in_=pt[ti][:, :].rearrange("p (a t) -> p a t", a=8)),
                          reads=[PT(ti)], writes=[("pT", half)])

                def pv(e, b=b):
                    ins = None
                    for h in range(8):
                        kv = h // 4
                        for sbk in range(nsb):
                            vb = b + sbk + (1 if first else 0)
                            ins = e.matmul(ps[4][:, h * 64:(h + 1) * 64], lhsT=pT[:, h * 2 + sbk, :],
                                           rhs=vA[:, vb, kv * 64:(kv + 1) * 64], start=(sbk == 0), stop=(sbk == nsb - 1))
                    return ins
                S.add("pe", pv, reads=[("pT", 0), ("pT", 1), ("vA", b), ("vA", b + 1)], writes=[PS(4)])
                S.add("dve", lambda e: e.tensor_tensor(
                    out=oab[:, :].rearrange("p (h d) -> p h d", h=8),
                    in0=ps[4][:, :].rearrange("p (h d) -> p h d", h=8),
                    in1=rden[:, :].unsqueeze(2).broadcast_to([128, 8, 64]), op=ALU.mult),
                    reads=[PS(4), "rden"], writes=["oab"])
                transpose_to(oab, 128, 4, (lambda b=b: oaT[:, :, b * 128:(b + 1) * 128]), ["oab"], [("oaT", b)])

                if SUB < 4:
                    return

                def mmsc(e, b=b):
                    ins = None
                    for h in range(4):
                        ins = e.matmul(ps[5][:, h * 128:(h + 1) * 128], lhsT=krT[:, h, b * 128:(b + 1) * 128],
                                       rhs=qrT[:, h, b * 128:(b + 1) * 128], start=True, stop=True)
                    return ins
                S.add("pe", mmsc, reads=[("krT", h) for h in range(4)] + [("qrT", h) for h in range(4)], writes=[PS(5)])
                S.add("dve", lambda e: e.tensor_tensor(out=scTb[:], in0=ps[5][:, :].rearrange("p (h i) -> p h i", h=4),
                                                       in1=decT, op=ALU.mult), reads=[PS(5), "cs"], writes=["scTb"])
                S.add("pool", lambda e, b=b: e.tensor_tensor(out=qdec[:], in0=qrT[:, :, b * 128:(b + 1) * 128], in1=gqc,
                                                             op=ALU.mult),
                      reads=[("qrT", h) for h in range(4)] + ["cs"], writes=["qdec"])

                def mmo(e, b=b):
                    ins = None
                    for h in range(4):
                        dst = ps[h // 2][:, (h % 2) * 256:(h % 2 + 1) * 256]
                        e.matmul(dst, lhsT=scTb[:, h, :], rhs=vR[:, b, h * 256:(h + 1) * 256], start=True, stop=False)
                        ins = e.matmul(dst, lhsT=qdec[:, h, :], rhs=Sb[:, l, h * 256:(h + 1) * 256], start=False, stop=True)
                    return ins
                S.add("pe", mmo, reads=["scTb", "qdec", ("Sb", l)] + vr_atoms("vR", b), writes=[PS(0), PS(1)])

                def mmu(e, b=b):
                    ins = None
                    for h in range(4):
                        dst = ps[2 + h // 2][:, (h % 2) * 256:(h % 2 + 1) * 256]
                        ins = e.matmul(dst, lhsT=kdec[:, b, h * 128:(h + 1) * 128], rhs=vR[:, b, h * 256:(h + 1) * 256],
                                       start=True, stop=True)
                    return ins
                S.add("pe", mmu, reads=[("kdec", b, h) for h in range(4)] + vr_atoms("vR", b), writes=[PS(2), PS(3)])
                for h in range(4):
                    S.add("dve", lambda e, h=h: e.scalar_tensor_tensor(
                        out=S32[:, l, h * 256:(h + 1) * 256], in0=S32[:, l, h * 256:(h + 1) * 256],
                        scalar=consts["gamma128"][h], in1=ps[2 + h // 2][:, (h % 2) * 256:(h % 2 + 1) * 256],
                        op0=ALU.mult, op1=ALU.add), reads=[("S32", l), PS(2 + h // 2)], writes=[("S32", l)])
                S.add("act", lambda e: e.copy(out=Sb[:, l, :], in_=S32[:, l, :]), reads=[("S32", l)], writes=[("Sb", l)])
                if SUB < 5:
                    return
                gn_block(128, l, b, lambda j: ps[j][:, :].rearrange("p (a v) -> p a v", a=2),
                         lambda h: ps[h // 2][:, (h % 2) * 256:(h % 2 + 1) * 256], [PS(0), PS(1)])

            for b in range(nb):
                first = (tile_idx == 0 and b == 0)
                s0 = 128 if first else 0
                blk(b, first, s0, 256 - s0, b * 128 + s0, (256 - s0) // 128)
            if is_last_tile:
                S.add("sp", lambda e: e.dma_start(out=nrp[l].rearrange("h p v -> p h v"),
                                                  in_=S32[:, l, :].rearrange("p (h v) -> p h v", h=4)),
                      reads=[("S32", l)], writes=[("nrp", l)], dma=True)
            if SUB < 6:
                return
            mix_out(nb, 128, l)

        def mix_sample(l):
            np_ = NSAMP
            S.add("sp", lambda e: e.dma_start(out=gngs[:], in_=gng[l]), writes=["lng"], dma=True)
            S.add("sp", lambda e: e.dma_start(out=sinkS[:], in_=sinkbh[l]), writes=["sinkS"], dma=True)
            mix_proj(1, np_, l, True)
            zsr = [("zs32", 0), ("zs32", 512)]
            S.add("sp", lambda e: e.dma_start(out=zq[l], in_=zs32[:, 0:512]), reads=zsr, writes=[("zq", l)], dma=True)
            S.add("sp", lambda e: e.dma_start(out=zk[l], in_=zs32[:, 512:640]), reads=zsr, writes=[("zk", l)], dma=True)
            S.add("sp", lambda e: e.dma_start(out=zv[l], in_=zs32[:, 640:768]), reads=zsr, writes=[("zv", l)], dma=True)
            S.add("sp", lambda e: e.dma_start(out=nks[l][:, 0:127, :], in_=ck[l][:, 1:128, :]), writes=[("nks", l, 0)],
                  dma=True)
            S.add("sp", lambda e: e.dma_start(out=nvs[l][:, 0:127, :], in_=cv[l][:, 1:128, :]), writes=[("nvs", l, 0)],
                  dma=True)
            S.add("sp", lambda e: e.dma_start(out=nks[l][:, 127, :], in_=zs32[:, 512:640]), reads=zsr,
                  writes=[("nks", l, 1)], dma=True)
            S.add("sp", lambda e: e.dma_start(out=nvs[l][:, 127, :], in_=zs32[:, 640:768]), reads=zsr,
                  writes=[("nvs", l, 1)], dma=True)
            S.add("sp", lambda e: e.dma_start(out=qs[:], in_=zq[l].rearrange("b (h d) -> (b h) d", h=8)),
                  reads=[("zq", l)], writes=["qs"], dma=True)
            kview = zk[l].rearrange("b (k d) -> b k d", k=2).unsqueeze(2).broadcast_to([NSAMP, 2, 4, 64])
            vview = zv[l].rearrange("b (k d) -> b k d", k=2).unsqueeze(2).broadcast_to([NSAMP, 2, 4, 64])
            S.add("sp", lambda e: e.dma_start(out=kn[:], in_=kview), reads=[("zk", l)], writes=["kn"], dma=True)
            S.add("sp", lambda e: e.dma_start(out=vn[:], in_=vview), reads=[("zv", l)], writes=["vn"], dma=True)
            S.add("act", lambda e: e.copy(out=qsb[:], in_=qs[:]), reads=["qs"], writes=["qsb"])
            for qi in range(128 // KQ):
                S.add("pool", lambda e, qi=qi: e.dma_start(out=Kt[:], in_=ckr[l][:, qi * KQ:(qi + 1) * KQ, :]),
                      writes=["Kt"], dma=True)
                S.add("dve", lambda e: e.tensor_tensor(out=tmpS[:], in0=Kt[:],
                                                       in1=qsb[:, :].unsqueeze(1).broadcast_to([128, KQ, 64]),
                                                       op=ALU.mult), reads=["Kt", "qsb"], writes=["tmpS"])
                S.add("dve", lambda e, qi=qi: e.tensor_reduce(out=scS[:, qi * KQ:(qi + 1) * KQ], in_=tmpS[:], axis=AX.X,
                                                              op=ALU.add), reads=["tmpS"], writes=[("scS", qi)])
            S.add("dve", lambda e: e.tensor_tensor(out=oS2[:], in0=kn[:], in1=qs[:], op=ALU.mult),
                  reads=["kn", "qs"], writes=["oS2"])
            S.add("dve", lambda e: e.tensor_reduce(out=scS[:, 128:129], in_=oS2[:], axis=AX.X, op=ALU.add),
                  reads=["oS2"], writes=[("scS", 128 // KQ)])
            scr_ = [("scS", i) for i in range(128 // KQ + 1)]
            S.add("dve", lambda e: e.scalar_tensor_tensor(out=scS[:, 0:129], in0=scS[:, 0:129], scalar=0.125,
                                                          in1=biasS[:, 0:129], op0=ALU.mult, op1=ALU.add),
                  reads=scr_ + ["cs"], writes=scr_)
            S.add("dve", lambda e: e.tensor_reduce(out=smS[:, 0:1], in_=scS[:, 0:129], axis=AX.X, op=ALU.max),
                  reads=scr_, writes=["smS0"])
            S.add("dve", lambda e: e.tensor_tensor(out=smS[:, 0:1], in0=smS[:, 0:1], in1=sinkS[:], op=ALU.max),
                  reads=["smS0", "sinkS"], writes=["smS0"])
            S.add("dve", lambda e: e.tensor_scalar(out=smS[:, 1:2], in0=smS[:, 0:1], scalar1=-1.0, scalar2=None,
                                                   op0=ALU.mult), reads=["smS0"], writes=["smS1"])
            S.add("dve", lambda e: e.tensor_tensor(out=smS[:, 2:3], in0=sinkS[:], in1=smS[:, 0:1], op=ALU.subtract),
                  reads=["smS0", "sinkS"], writes=["smS2"])
            S.add("act", lambda e: e.activation(out=smS[:, 2:3], in_=smS[:, 2:3], func=AF.Exp), reads=["smS2"],
                  writes=["smS2"])
            S.add("act", lambda e: e.activation(out=pS[:, 0:129], in_=scS[:, 0:129], func=AF.Exp, bias=smS[:, 1:2],
                                                scale=1.0), reads=scr_ + ["smS1"], writes=["pS"])
            S.add("dve", lambda e: e.tensor_reduce(out=smS[:, 3:4], in_=pS[:, 0:129], axis=AX.X, op=ALU.add),
                  reads=["pS"], writes=["smS3"])
            S.add("dve", lambda e: e.tensor_tensor(out=smS[:, 4:5], in0=smS[:, 3:4], in1=smS[:, 2:3], op=ALU.add),
                  reads=["smS3", "smS2"], writes=["smS4"])
            S.add("dve", lambda e: e.reciprocal(out=smS[:, 4:5], in_=smS[:, 4:5]), reads=["smS4"], writes=["smS4"])
            S.add("act", lambda e: e.copy(out=pSb[:, 0:128], in_=pS[:, 0:128]), reads=["pS"], writes=["pSb"])
            S.add("dve", lambda e: e.tensor_scalar(out=oS[:], in0=vn[:], scalar1=pS[:, 128:129], scalar2=None,
                                                   op0=ALU.mult), reads=["vn", "pS"], writes=["oS"])
            for qi in range(128 // KQ):
                S.add("pool", lambda e, qi=qi: e.dma_start(out=Vt[:], in_=cvr[l][:, qi * KQ:(qi + 1) * KQ, :]),
                      writes=["Vt"], dma=True)
                S.add("dve", lambda e, qi=qi: e.tensor_tensor(
                    out=tmpS[:], in0=Vt[:], in1=pSb[:, qi * KQ:(qi + 1) * KQ].unsqueeze(2).broadcast_to([128, KQ, 64]),
                    op=ALU.mult), reads=["Vt", "pSb"], writes=["tmpS"])
                S.add("dve", lambda e: e.tensor_reduce(out=oS2[:], in_=tmpS[:, :, :].rearrange("p s d -> p d s"),
                                                       axis=AX.X, op=ALU.add), reads=["tmpS"], writes=["oS2"])
                S.add("dve", lambda e: e.tensor_tensor(out=oS[:], in0=oS[:], in1=oS2[:], op=ALU.add),
                      reads=["oS", "oS2"], writes=["oS"])
            S.add("dve", lambda e: e.tensor_scalar(out=oS[:], in0=oS[:], scalar1=smS[:, 4:5], scalar2=None, op0=ALU.mult),
                  reads=["oS", "smS4"], writes=["oS"])
            S.add("sp", lambda e: e.dma_start(out=oscr[l], in_=oS[:]), reads=["oS"], writes=[("oscr", l)], dma=True)
            S.add("sp", lambda e: e.dma_start(out=oa32[:], in_=oscr[l].rearrange("(b h) d -> b (h d)", h=8)),
                  reads=[("oscr", l)], writes=["oa32"], dma=True)
            S.add("act", lambda e: e.copy(out=oab[0:np_, :], in_=oa32[:]), reads=["oa32"], writes=["oab"])
            transpose_to(oab, np_, 4, (lambda: oaT[:, :, 0:np_]), ["oab"], [("oaT", 0)])
            qrr = [("qr32", 0), ("qr32", 256)]
            krr = [("kr32", 0), ("kr32", 256)]
            vrr = [("vr32", o) for o in VR_ATOMS]
            S.add("dve", lambda e: e.tensor_tensor(out=tmq[:], in0=qr32[:], in1=kr32[:], op=ALU.mult), reads=qrr + krr,
                  writes=["tmq"])
            S.add("dve", lambda e: e.tensor_reduce(out=qk[:], in_=tmq[:, :].rearrange("b (h d) -> b h d", h=4), axis=AX.X,
                                                   op=ALU.add), reads=["tmq"], writes=["qk"])
            S.add("dve", lambda e: e.tensor_scalar(out=qk[:], in0=qk[:], scalar1=128 ** -0.5, scalar2=None, op0=ALU.mult),
                  reads=["qk"], writes=["qk"])
            for h in range(4):
                S.add("pool", lambda e, h=h: e.tensor_tensor(
                    out=qsel[:, h, :, :], in0=eyeb,
                    in1=qrT32[:, h, :].unsqueeze(1).broadcast_to([128, NSAMP, NSAMP]), op=ALU.mult),
                    reads=[("qrT32", h), "cs"], writes=[("qsel", h)])
            for b in range(NSAMP):
                S.add("sp", lambda e, b=b: e.dma_start(out=S0[:], in_=stt[l, b].rearrange("h p v -> p h v")),
                      writes=["S0"], dma=True)

                def mmc(e, b=b):
                    ins = None
                    if b == 0:
                        for j in range(2):
                            e.matmul(ps[j][0:np_, :], lhsT=zt[:, :], rhs=S0[:, 2 * j:2 * j + 2, :], start=True,
                                     stop=False, skip_group_check=True)
                    for h in range(4):
                        dst = ps[h // 2][0:np_, (h % 2) * 256:(h % 2 + 1) * 256]
                        ins = e.matmul(dst, lhsT=qsel[:, h, b, :], rhs=S0[:, h, :], start=False,
                                       stop=(b == NSAMP - 1), skip_group_check=True)
                    return ins
                S.add("pe", mmc, reads=["S0", "zt"] + [("qsel", h) for h in range(4)], writes=[PS(0), PS(1)])
                S.add("pool", lambda e, b=b: e.tensor_scalar(
                    out=vdiag[:, :, :], in0=vr32[:, :].rearrange("b (h v) -> b h v", h=4),
                    scalar1=cs[0:NSAMP, coff["eye16"] + b:coff["eye16"] + b + 1], scalar2=128 ** -0.5, op0=ALU.mult,
                    op1=ALU.mult), reads=vrr + ["cs"], writes=["vdiag"])

                def mmu(e):
                    ins = None
                    for h in range(4):
                        dst = ps[2 + h // 2][:, (h % 2) * 256:(h % 2 + 1) * 256]
                        ins = e.matmul(dst, lhsT=kr32[:, h * 128:(h + 1) * 128], rhs=vdiag[:, h, :], start=True,
                                       stop=True)
                    return ins
                S.add("pe", mmu, reads=krr + ["vdiag"], writes=[PS(2), PS(3)])
                for h in range(4):
                    S.add("dve", lambda e, h=h: e.scalar_tensor_tensor(
                        out=S1[:, h, :], in0=S0[:, h, :], scalar=consts["gamma"][h],
                        in1=ps[2 + h // 2][:, (h % 2) * 256:(h % 2 + 1) * 256], op0=ALU.mult, op1=ALU.add),
                        reads=["S0", PS(2 + h // 2)], writes=["S1"])
                S.add("sp", lambda e, b=b: e.dma_start(out=nrs[l, b].rearrange("h p v -> p h v"), in_=S1[:]),
                      reads=["S1"], writes=[("nrs", l, b)], dma=True)
            for h in range(4):
                S.add("dve", lambda e, h=h: e.tensor_scalar(
                    out=osam[:, h * 256:(h + 1) * 256], in0=ps[h // 2][0:np_, (h % 2) * 256:(h % 2 + 1) * 256],
                    scalar1=consts["gamma"][h], scalar2=None, op0=ALU.mult),
                    reads=[PS(h // 2)], writes=[("osam", h)])
                S.add("dve", lambda e, h=h: e.scalar_tensor_tensor(
                    out=osam[:, h * 256:(h + 1) * 256], in0=vr32[:, h * 256:(h + 1) * 256], scalar=qk[:, h:h + 1],
                    in1=osam[:, h * 256:(h + 1) * 256], op0=ALU.mult, op1=ALU.add),
                    reads=vrr + ["qk", ("osam", h)], writes=[("osam", h)])
            if STAGE == 15:
                S.add("sp", lambda e: e.dma_start(out=ys[0:NSAMP, :], in_=osam[:, :]),
                      reads=[("osam", h) for h in range(4)], writes=[("yout", 0)], dma=True)
                return
            gn_block(np_, l, 0, lambda j: osam[:, j * 512:(j + 1) * 512].rearrange("p (a v) -> p a v", a=2),
                     lambda h: osam[:, h * 256:(h + 1) * 256], [("osam", h) for h in range(4)])
            mix_out(1, np_, l)

        def load_x(src_fn, nb, np_):
            for b in range(nb):
                npb = 128 if b < nb - 1 else np_
                S.add("sp", lambda e, b=b, npb=npb: e.dma_start(out=x32[0:npb, b, :], in_=src_fn(b, npb)),
                      writes=[("x32", b)], dma=True)
                S.add("act", lambda e, b=b, npb=npb: e.copy(out=xb[0:npb, :], in_=x32[0:npb, b, :]),
                      reads=[("x32", b)], writes=["xb"])
                transpose_to(xb, npb, KD, (lambda b=b, npb=npb: xT[:, :, b * 128:b * 128 + npb]), ["xb"], [("xT", b)])

        def tile_body():
            for k_ in rr:
                rr[k_] = 0
            load_x(lambda b, npb: xp[bass.ds(S.loopvar["sp"] * TT + b * 128, 128), :], NB, 128)
            S.add("sp", lambda e: e.dma_start(out=bias0s[:], in_=bias0[bass.ds(S.loopvar["sp"] * 128, 128), :]),
                  writes=["bias0s"], dma=True)
            for l in range(L):
                ffn(NB, 128, l, 0)
                mix_prompt(l, 1, True)
                lo = None
                if l == L - 1:
                    lo = (lambda b: yp[bass.ds(S.loopvar["sp"] * TT + b * 128, 128), :])
                ffn(NB, 128, l, 1, last_out=lo)

        S.seg = "A"
        tile_body()
        S.seg = "B"
        tile_body()
        S.seg = "post"
        for k_ in rr:
            rr[k_] = 0
        if with_sample:
            load_x(lambda b, npb: xs[0:npb, :], 1, NSAMP)
            for l in range(L):
                ffn(1, NSAMP, l, 0)
                mix_sample(l)
                lo = None
                if l == L - 1:
                    lo = (lambda b: ys[0:NSAMP, :])
                ffn(1, NSAMP, l, 1, last_out=lo)

        S.emit(nc, NT)
    return nc, cpack


def prep_core_inputs(c, inp, cpack, T, L):
    f = np.float32
    seq = c % 2
    b0 = c * NSAMP
    m = {}
    m["xp"] = np.ascontiguousarray(inp["x_prompt"][seq, :T])
    m["xs"] = np.ascontiguousarray(inp["x_sample"][b0:b0 + NSAMP, 0])
    m["gu1"] = inp["w_ff1_gu"][:L]
    m["dn1"] = inp["w_ff1_dn"][:L]
    wi = inp["w_in"][:L]
    m["win"] = np.concatenate([wi, wi[:, :, 576:640], wi[:, :, 512:576]], axis=2)
    m["bra"] = inp["w_br_a"][:L]
    m["brr"] = inp["w_br_r"][:L]
    m["wo"] = inp["w_o"][:L]
    m["gu2"] = inp["w_ff2_gu"][:L]
    m["dn2"] = inp["w_ff2_dn"][:L]
    g = np.stack([inp["ln1_g"][:L], inp["ln2_g"][:L], inp["ln3_g"][:L]], axis=1)
    b = np.stack([inp["ln1_b"][:L], inp["ln2_b"][:L], inp["ln3_b"][:L]], axis=1)
    m["lng"] = np.ascontiguousarray(np.broadcast_to(g[:, :, None, :], (L, 3, 128, D))).astype(f)
    m["lnb"] = np.ascontiguousarray(np.broadcast_to(b[:, :, None, :], (L, 3, 128, D))).astype(f)
    m["gng"] = np.ascontiguousarray(np.broadcast_to(inp["ret_gn_g"][:L, None, :], (L, 128, D))).astype(f)
    m["sinkb"] = np.ascontiguousarray(np.broadcast_to(inp["attn_sinks"][:L, None, :], (L, 128, 8))).astype(f)
    m["sinkbh"] = np.ascontiguousarray(np.tile(inp["attn_sinks"][:L], (1, NSAMP)).reshape(L, 128, 1)).astype(f)
    m["cst"] = cpack
    cc = make_consts()
    nt = T // TT
    bz = np.tile(cc["biasR"][:, :, 0:128].reshape(1, 128, 1024), (nt, 1, 1))
    bz[0] = -1e30
    m["bias0"] = bz.reshape(nt * 128, 1024)
    ckc = inp["cache_win_k"][:L, b0:b0 + NSAMP]
    cvc = inp["cache_win_v"][:L, b0:b0 + NSAMP]
    m["ckr"] = np.ascontiguousarray(np.repeat(np.transpose(ckc, (0, 1, 3, 2, 4)), 4, axis=2).reshape(L, 128, 128, 64))
    m["cvr"] = np.ascontiguousarray(np.repeat(np.transpose(cvc, (0, 1, 3, 2, 4)), 4, axis=2).reshape(L, 128, 128, 64))
    m["ck"] = np.ascontiguousarray(ckc.reshape(L, NSAMP, 128, 128))
    m["cv"] = np.ascontiguousarray(cvc.reshape(L, NSAMP, 128, 128))
    m["stt"] = np.ascontiguousarray(inp["state_ret"][:L, b0:b0 + NSAMP])
    return {k: np.ascontiguousarray(v, dtype=f) for k, v in m.items()}


def run(inp, T, L, with_sample=True):
    nc, cpack = build(T, L, with_sample)
    in_maps = [prep_core_inputs(c, inp, cpack, T, L) for c in range(8)]
    res = run_bass_kernel_spmd(nc, in_maps, core_ids=list(range(8)))
    r = res.results
    y_p = np.stack([r[0]["yp"], r[1]["yp"]], 0)
    y_s = np.concatenate([r[c]["ys"] for c in range(8)], 0)[:, None, :]
    nk_p = np.stack([r[0]["nkp"], r[1]["nkp"]], 1).reshape(L, 2, 128, 2, 64)
    nv_p = np.stack([r[0]["nvp"], r[1]["nvp"]], 1).reshape(L, 2, 128, 2, 64)
    nr_p = np.stack([r[0]["nrp"], r[1]["nrp"]], 1)
    nk_s = np.concatenate([r[c]["nks"] for c in range(8)], 1).reshape(L, 128, 128, 2, 64)
    nv_s = np.concatenate([r[c]["nvs"] for c in range(8)], 1).reshape(L, 128, 128, 2, 64)
    nr_s = np.concatenate([r[c]["nrs"] for c in range(8)], 1)
    return tuple(np.ascontiguousarray(a, dtype=np.float32) for a in (y_p, y_s, nk_p, nv_p, nr_p, nk_s, nv_s, nr_s))


def kernel(**inputs):
    inp = {k: np.asarray(v) for k, v in inputs.items()}
    return run(inp, 8192, 4, True)
```

```python
import os
import numpy as np
import concourse.bass as bass
import concourse.mybir as mybir
from concourse.bass_utils import run_bass_kernel_spmd

F32 = mybir.dt.float32
BF16 = mybir.dt.bfloat16
AF = mybir.ActivationFunctionType
ALU = mybir.AluOpType
AX = mybir.AxisListType

D = 1024
DFF = 2816
KD = 8
KF = 22
INC = 5888
INC2 = 6016
C_KAS = 5888
DEPTH_FULL = 4
ALPHA = (2.0 * DEPTH_FULL) ** 0.25
LN_EPS = 1e-5
GN_EPS = 1e-6
NH_A = 8
NH_R = 4
NSAMP = 16
TT = 256
NB = 2
C_QA, C_KA, C_VA, C_QR, C_KR, C_VR, C_GR, C_GA, C_GB = 0, 512, 640, 768, 1280, 1792, 2816, 3840, 4864
EPOCH = int(os.environ.get('KEPOCH', '8000'))
DEPOCH = int(os.environ.get('KDEPOCH', '1500'))
CINC = int(os.environ.get('KINC', '1'))


class Sched:
    def __init__(self):
        self.ops = []
        self.last_w = {}
        self.readers = {}
        self.seg = "pre"
        self.loopvar = {}

    def add(self, eng, fn, reads=(), writes=(), dma=False):
        deps = set()
        for r in reads:
            w = self.last_w.get(r)
            if w is not None:
                deps.add(w)
            if isinstance(r, tuple) and r[0] in ("ps", "pt"):
                deps.update(self.readers.get(r, ()))
        for r in writes:
            w = self.last_w.get(r)
            if w is not None:
                deps.add(w)
            deps.update(self.readers.get(r, ()))
        idx = len(self.ops)
        self.ops.append(dict(eng=eng, fn=fn, deps=deps, dma=dma, seg=self.seg))
        for r in reads:
            self.readers.setdefault(r, []).append(idx)
        for r in writes:
            self.last_w[r] = idx
            self.readers[r] = []
        return idx

    def emit(self, nc, n_iter, final_eng="sp"):
        import contextlib
        ops = self.ops
        engs = ["pe", "act", "dve", "pool", "sp"]
        segA = [i for i, op in enumerate(ops) if op["seg"] == "A"]
        segB = [i for i, op in enumerate(ops) if op["seg"] == "B"]
        assert len(segA) == len(segB)
        a2b = dict(zip(segA, segB))
        for ia, ib in a2b.items():
            assert ops[ia]["eng"] == ops[ib]["eng"] and ops[ia]["dma"] == ops[ib]["dma"]
        NBODY_C = 2
        NBODY_D = 12
        sem_of = {}
        counters = {}
        body_cnt = {e: sum(1 for i in segB if ops[i]["eng"] == e and not ops[i]["dma"]) for e in engs}
        body_seen = {}
        dma_rr = {}
        for i, op in enumerate(ops):
            e, sg = op["eng"], op["seg"]
            if sg == "A":
                continue
            if sg == "B":
                if op["dma"]:
                    k = dma_rr.get(("B", e), 0)
                    dma_rr[("B", e)] = k + 1
                    key = ("bd", e, k % NBODY_D)
                    c = counters.get(key, 0) + 16
                else:
                    n = body_seen.get(e, 0)
                    body_seen[e] = n + 1
                    half = (n * NBODY_C) // max(body_cnt[e], 1)
                    key = ("bc", e, half)
                    c = counters.get(key, 0) + 1
                counters[key] = c
                sem_of[i] = (key, c)
                continue
            if op["dma"]:
                k = dma_rr.get((sg, e), 0)
                dma_rr[(sg, e)] = k + 1
                key = ("d", sg, e, k % 6, (k // 6) // DEPOCH)
                c = counters.get(key, 0) + 16
            else:
                tot = counters.get(("n", sg, e), 0)
                counters[("n", sg, e)] = tot + 1
                key = ("c", sg, e, tot // EPOCH)
                c = counters.get(key, 0) + 1
            counters[key] = c
            sem_of[i] = (key, c)
        delta = {k: v for k, v in counters.items() if k[0] in ("bc", "bd")}
        keys = sorted({k for k, _ in sem_of.values()}, key=str)
        if os.environ.get("KSTATS"):
            print("SCHED ops:", {e: sum(1 for op in ops if op["eng"] == e and op["seg"] != "A") for e in engs},
                  "body:", body_cnt, "nsems", len(keys), "delta", delta, flush=True)
        sem_handles = {}
        with contextlib.ExitStack() as st:
            for k in keys:
                sem_handles[k] = st.enter_context(nc.semaphore("s_" + "_".join(str(x) for x in k)))
            pre_final = {}
            all_final = {}
            for i, op in enumerate(ops):
                if op["seg"] == "A":
                    continue
                k, v = sem_of[i]
                if op["seg"] == "pre":
                    pre_final[k] = max(pre_final.get(k, 0), v)
                if op["seg"] == "B":
                    v = (n_iter) * delta[k] + v
                all_final[k] = max(all_final.get(k, 0), v)
            owner = {}
            for i, op in enumerate(ops):
                if op["seg"] == "B":
                    owner[sem_of[i][0]] = op["eng"]

            def run(en, handle):
                mine = [i for i, op in enumerate(ops) if op["eng"] == en]
                waited = {}
                for i in mine:
                    op = ops[i]
                    if op["seg"] != "pre":
                        continue
                    need = {}
                    for d in op["deps"]:
                        dop = ops[d]
                        if dop["eng"] == "pe" and en == "pe" and not dop["dma"]:
                            continue
                        k, v = sem_of[d]
                        if waited.get(k, 0) >= v:
                            continue
                        need[k] = max(need.get(k, 0), v)
                    for k, v in need.items():
                        handle.wait_ge(sem_handles[k], v)
                        waited[k] = v
                    ins = op["fn"](handle)
                    k, v = sem_of[i]
                    ins.then_inc(sem_handles[k], 16 if op["dma"] else 1)
                for k, eo in owner.items():
                    if eo != en:
                        continue
                    rem = delta[k]
                    while rem > 0:
                        stp = min(rem, 240)
                        handle.sem_inc(sem_handles[k], stp)
                        rem -= stp
                for k, v in pre_final.items():
                    if waited.get(k, 0) < v:
                        handle.wait_ge(sem_handles[k], v)
                        waited[k] = v
                body = [i for i in mine if ops[i]["seg"] == "B"]
                if body:
                    rtmp = nc.alloc_register(handle.engine, "wtmp_" + en)
                    with handle.Fori(0, n_iter) as it:
                        self.loopvar[en] = it
                        w_same = {}
                        w_prev = {}
                        for i in body:
                            op = ops[i]
                            need_same = {}
                            need_prev = {}
                            for d in op["deps"]:
                                dop = ops[d]
                                if dop["seg"] == "pre":
                                    continue
                                if dop["eng"] == "pe" and en == "pe" and not dop["dma"]:
                                    continue
                                if dop["seg"] == "A":
                                    k, v = sem_of[a2b[d]]
                                    if w_same.get(k, 0) >= 1 or w_prev.get(k, 0) >= v:
                                        continue
                                    need_prev[k] = max(need_prev.get(k, 0), v)
                                else:
                                    k, v = sem_of[d]
                                    if w_same.get(k, 0) >= v:
                                        continue
                                    need_same[k] = max(need_same.get(k, 0), v)
                            for k, v in need_prev.items():
                                if k in need_same:
                                    continue
                                handle.reg_mul(rtmp, it, delta[k])
                                handle.reg_add(rtmp, rtmp, v)
                                handle.wait_ge(sem_handles[k], rtmp)
                                w_prev[k] = v
                            for k, v in need_same.items():
                                handle.reg_mul(rtmp, it, delta[k])
                                handle.reg_add(rtmp, rtmp, delta[k] + v)
                                handle.wait_ge(sem_handles[k], rtmp)
                                w_same[k] = v
                            ins = op["fn"](handle)
                            k, v = sem_of[i]
                            ins.then_inc(sem_handles[k], 16 if op["dma"] else 1)
                        self.loopvar[en] = None
                waited = dict(waited)
                for i in mine:
                    op = ops[i]
                    if op["seg"] != "post":
                        continue
                    need = {}
                    for d in op["deps"]:
                        dop = ops[d]
                        if dop["seg"] == "pre":
                            continue
                        if dop["eng"] == "pe" and en == "pe" and not dop["dma"]:
                            continue
                        k, v = sem_of[d]
                        if dop["seg"] == "B":
                            v = n_iter * delta[k] + v
                        if waited.get(k, 0) >= v:
                            continue
                        need[k] = max(need.get(k, 0), v)
                    for k, v in need.items():
                        handle.wait_ge(sem_handles[k], v)
                        waited[k] = v
                    ins = op["fn"](handle)
                    k, v = sem_of[i]
                    ins.then_inc(sem_handles[k], 16 if op["dma"] else 1)
                if en == final_eng:
                    for k, v in all_final.items():
                        if waited.get(k, 0) < v:
                            handle.wait_ge(sem_handles[k], v)

            with nc.Block() as block:
                @block.tensor
                def _(h):
                    run("pe", h)

                @block.scalar
                def _(h):
                    run("act", h)

                @block.vector
                def _(h):
                    run("dve", h)

                @block.gpsimd
                def _(h):
                    run("pool", h)

                @block.sync
                def _(h):
                    run("sp", h)


def make_consts():
    lg = np.log1p(-(2.0 ** (-5.0 - np.arange(4, dtype=np.float64))))
    pos = np.arange(128, dtype=np.float64)
    c = {}
    c["ident"] = np.eye(128, dtype=np.float32)
    diff = pos[None, :] - pos[:, None]
    dec = np.zeros((128, 4, 128), np.float64)
    for h in range(4):
        dec[:, h, :] = np.where(diff >= 0, np.exp(np.maximum(diff, 0) * lg[h]), 0.0) * (128 ** -0.5)
    c["decT"] = dec.astype(np.float32)
    gq = np.zeros((128, 4, 128), np.float64)
    for h in range(4):
        gq[:, h, :] = np.exp((pos + 1.0) * lg[h])[None, :]
    c["gq"] = gq.astype(np.float32)
    ks = np.zeros((128, 4), np.float64)
    for h in range(4):
        ks[:, h] = np.exp((127.0 - pos) * lg[h]) * (128 ** -0.5)
    c["kscale"] = ks.astype(np.float32)
    slopes = 2.0 ** (-8.0 * np.arange(1, 9, dtype=np.float64) / 8)
    i = np.arange(128)[:, None]
    j = np.arange(256)[None, :]
    dist = i + 128 - j
    valid = (dist >= 0) & (dist <= 128)
    b = np.zeros((128, 8, 256), np.float64)
    for h in range(8):
        b[:, h, :] = np.where(valid, -slopes[h] * dist, -1e30)
    c["biasR"] = b.astype(np.float32)
    bs = np.zeros((128, 132), np.float64)
    for p in range(128):
        h = p % 8
        bs[p, :129] = -slopes[h] * (128 - np.arange(129))
    c["biasS"] = bs.astype(np.float32)
    e16 = np.zeros((128, 16), np.float32)
    e16[:16, :] = np.eye(16, dtype=np.float32)
    c["eye16"] = e16
    c["eyeb"] = np.tile(np.eye(16, dtype=np.float32).reshape(1, 256), (128, 1))
    gam = np.exp(lg)
    c["gamma"] = [float(g) for g in gam]
    c["gamma128"] = [float(np.exp(128.0 * l)) for l in lg]
    return c


CONST_LAYOUT = [("ident", 128), ("decT", 512), ("gq", 512), ("kscale", 4), ("biasR", 2048),
                ("biasS", 132), ("eye16", 16), ("eyeb", 256)]


def pack_consts(c):
    cols = sum(n for _, n in CONST_LAYOUT)
    out = np.zeros((128, cols), np.float32)
    o = 0
    offs = {}
    for name, n in CONST_LAYOUT:
        out[:, o:o + n] = c[name].reshape(128, n)
        offs[name] = o
        o += n
    return out, offs


def build(T, L, with_sample=True):
    NT = T // TT
    consts = make_consts()
    cpack, coff = pack_consts(consts)
    NCC = cpack.shape[1]
    nc = bass.Bass("TRN2", target_bir_lowering=False)
    S = Sched()

    def din(name, shape, dt=F32):
        return nc.dram_tensor(name, list(shape), dt, kind="ExternalInput").ap()

    def dout(name, shape, dt=F32):
        return nc.dram_tensor(name, list(shape), dt, kind="ExternalOutput").ap()

    def dscr(name, shape, dt=BF16):
        return nc.dram_tensor(name, list(shape), dt).ap()

    xp = din("xp", [T, D])
    xs = din("xs", [NSAMP, D])
    NSLOT = 52
    SLOT_BASE = {"gu1": 0, "dn1": 11, "win": 17, "bra": 29, "brr": 31, "wo": 33, "gu2": 35, "dn2": 46}
    wall = din("wall", [L, NSLOT, 128, 4096])
    wall_bf = dscr("wall_bf", [L, NSLOT, 128, 4096])
    lng = din("lng", [L, 3, 128, D])
    lnb = din("lnb", [L, 3, 128, D])
    gng = din("gng", [L, 128, D])
    sinkb = din("sinkb", [L, 128, 8])
    sinkbh = din("sinkbh", [L, 128, 1])
    cst = din("cst", [128, NCC])
    bias0 = din("bias0", [NT * 128, 1024])
    ckr = din("ckr", [L, 128, 128, 64])
    cvr = din("cvr", [L, 128, 128, 64])
    ck = din("ck", [L, NSAMP, 128, 128])
    cv = din("cv", [L, NSAMP, 128, 128])
    stt = din("stt", [L, NSAMP, 4, 128, 256])
    yp = dout("yp", [T, D])
    ys = dout("ys", [NSAMP, D])
    nkp = dout("nkp", [L, 128, 128])
    nvp = dout("nvp", [L, 128, 128])
    nrp = dout("nrp", [L, 4, 128, 256])
    nks = dout("nks", [L, NSAMP, 128, 128])
    nvs = dout("nvs", [L, NSAMP, 128, 128])
    nrs = dout("nrs", [L, NSAMP, 4, 128, 256])
    zq = dscr("zq", [L, NSAMP, 512], F32)
    zk = dscr("zk", [L, NSAMP, 128], F32)
    zv = dscr("zv", [L, NSAMP, 128], F32)
    oscr = dscr("oscr", [L, 128, 64], F32)

    import contextlib
    es = contextlib.ExitStack()

    def sb(name, shape, dt=F32):
        return es.enter_context(nc.sbuf_tensor(name, list(shape), dt))

    def pst(name, shape, dt=F32):
        return es.enter_context(nc.psum_tensor(name, list(shape), dt))

    with es:
        cs = sb("cs", [128, NCC])
        identb = sb("identb", [128, 128], BF16)
        x32 = sb("x32", [128, NB, D])
        xT = sb("xT", [128, KD, TT], BF16)
        xb = sb("xb", [128, D], BF16)
        hbuf = sb("hbuf", [128, KF, TT], BF16)
        sabuf = [sb("sa%d" % i, [128, TT], BF16) for i in range(2)]
        NWS = 3
        wslot = [sb("wslot%d" % i, [128, KD, 512], BF16) for i in range(NWS)]
        lngs = sb("lngs", [128, D])
        lnbs = sb("lnbs", [128, D])
        gngs = lngs
        stats = sb("stats", [128, 4, 6])
        mv = sb("mv", [128, 4, 2])
        rstd = sb("rstd", [128, 4])
        gstats = sb("gstats", [128, 4, 6])
        gmv = sb("gmv", [128, 4, 2])
        grstd = sb("grstd", [128, 4])
        qz = sb("qz", [128, 2, 4, TT], BF16)
        kaT = sb("kaT", [128, 2, 128 + TT], BF16)
        qrT = sb("qrT", [128, 4, TT], BF16)
        krT = sb("krT", [128, 4, TT], BF16)
        gT = sb("gT", [128, 16, TT], BF16)
        vA = sb("vA", [128, NB + 1, 128], BF16)
        kdec = sb("kdec", [128, NB, 512], BF16)
        vR = sb("vR", [128, NB, D], BF16)
        gR = sb("gR", [128, NB, D], BF16)
        kv32 = sb("kv32", [128, 256])
        sc = sb("sc", [128, 8, 256])
        pb = sb("pb", [128, 8, 256], BF16)
        pT = sb("pT", [128, 16, 128], BF16)
        mx = sb("mx", [128, 8])
        negm = sb("negm", [128, 8])
        rsum = sb("rsum", [128, 8])
        esk = sb("esk", [128, 8])
        rden = sb("rden", [128, 8])
        sinks = sb("sinks", [128, 8])
        oab = sb("oab", [128, 512], BF16)
        oaT = sb("oaT", [128, 4, TT], BF16)
        scTb = sb("scTb", [128, 4, 128], BF16)
        qdec = sb("qdec", [128, 4, 128], BF16)
        S32 = sb("S32", [128, L, 4 * 256])
        Sb = sb("Sb", [128, L, 4 * 256], BF16)
        haloK = sb("haloK", [128, L, 2, 128], BF16)
        haloV = sb("haloV", [128, L, 128], BF16)
        orn = sb("orn", [128, D])
        orb = sb("orb", [128, D], BF16)
        orT = sb("orT", [128, KD, TT], BF16)
        mT = sb("mT", [128, KD, TT], BF16)
        tmpm = sb("tmpm", [128, TT])
        bias0s = sb("bias0s", [128, 1024])
        if with_sample:
            KQ = 16
            Kt = sb("Kt", [128, KQ, 64], BF16)
            Vt = sb("Vt", [128, KQ, 64], BF16)
            tmpS = sb("tmpS", [128, KQ, 64], BF16)
            qs = sb("qs", [128, 64])
            qsb = sb("qsb", [128, 64], BF16)
            kn = sb("kn", [128, 64])
            vn = sb("vn", [128, 64])
            scS = sb("scS", [128, 132])
            pS = sb("pS", [128, 132])
            pSb = sb("pSb", [128, 132], BF16)
            oS = sb("oS", [128, 64])
            oS2 = sb("oS2", [128, 64])
            smS = sb("smS", [128, 8])
            sinkS = sb("sinkS", [128, 1])
            zs32 = bias0s[0:NSAMP, 0:768]
            oa32 = sb("oa32", [NSAMP, 512])
            qr32 = sb("qr32", [NSAMP, 512])
            kr32 = sb("kr32", [NSAMP, 512])
            vr32 = sc[0:NSAMP, 0:4, :].rearrange("p a s -> p (a s)")
            osam = sc[0:NSAMP, 4:8, :].rearrange("p a s -> p (a s)")
            qsel = sb("qsel", [128, 4, NSAMP, NSAMP])
            qrT32 = sb("qrT32", [128, 4, NSAMP])
            vdiag = sb("vdiag", [NSAMP, 4, 256])
            S0 = sb("S0", [128, 4, 256])
            zt = sb("zt", [128, NSAMP])
            S1 = sb("S1", [128, 4, 256])
            qk = sb("qk", [NSAMP, 4])
            tmq = sb("tmq", [NSAMP, 512])
        ps = [pst("ps%d" % i, [128, 512]) for i in range(6)]
        pt = [pst("pt%d" % i, [128, 1024], BF16) for i in range(2)]

        def PS(i):
            return ("ps", i)

        def PT(i):
            return ("pt", i)

        def WS(i):
            return [("ws", i, 0), ("ws", i, 1)]

        NPC = 4
        SPP = NSLOT // NPC

        def precast_layer(l):
            for j in range(NPC):
                src = wall[l, j * SPP:(j + 1) * SPP].rearrange("s p c -> (s p) c")
                dst = wall_bf[l, j * SPP:(j + 1) * SPP].rearrange("s p c -> (s p) c")
                S.add("pool", (lambda e, s_=src, d_=dst: e.dma_start(out=d_, in_=s_)), writes=[("wb", l, j)], dma=True)

        S.add("sp", lambda e: e.dma_start(out=cs[:], in_=cst), writes=["cs"], dma=True)
        S.add("pool", lambda e: e.dma_start(out=identb[:], in_=cst[:, coff["ident"]:coff["ident"] + 128]),
              writes=["identb"], dma=True)
        STAGE = int(os.environ.get("KSTAGE", "99"))
        SUB = int(os.environ.get("KSUB", "99"))
        KATT = int(os.environ.get("KATT", "99"))
        for l in range(L):
            precast_layer(l)

        decT = cs[:, coff["decT"]:coff["decT"] + 512].rearrange("p (h i) -> p h i", h=4)
        gqc = cs[:, coff["gq"]:coff["gq"] + 512].rearrange("p (h i) -> p h i", h=4)
        kscale = cs[:, coff["kscale"]:coff["kscale"] + 4]
        biasR = cs[:, coff["biasR"]:coff["biasR"] + 2048].rearrange("p (h s) -> p h s", h=8)
        biasS = cs[:, coff["biasS"]:coff["biasS"] + 132]
        eyeb = cs[:, coff["eyeb"]:coff["eyeb"] + 256].rearrange("p (a b) -> p a b", a=NSAMP)

        S.add("pool", lambda e: e.memset(S32[:], 0.0), writes=[("S32", l) for l in range(L)])
        S.add("pool", lambda e: e.memset(Sb[:], 0.0), writes=[("Sb", l) for l in range(L)])
        S.add("pool", lambda e: e.memset(haloK[:], 0.0), writes=[("haloK", l) for l in range(L)])
        S.add("pool", lambda e: e.memset(haloV[:], 0.0), writes=[("haloV", l) for l in range(L)])
        S.add("pool", lambda e: e.memset(qz[:], 0.0), writes=[("qaT", i) for i in range(4)] + [("qaT", i, 1) for i in range(4)])

        if with_sample:
            S.add("pool", lambda e: e.memset(zt[:], 0.0), writes=["zt"])
        rr = {"ws": 0, "ps_a": 0, "ps_b": 0, "pt": 0}

        def load_slot(name, l, sidx):
            i = rr["ws"] % NWS
            rr["ws"] += 1
            sl = SLOT_BASE[name] + sidx
            src = wall_bf[l, sl].rearrange("p (k c) -> p k c", k=KD)
            S.add("sp", lambda e: e.dma_start(out=wslot[i][:, :, :], in_=src), reads=[("wb", l, sl // SPP)],
                  writes=WS(i), dma=True)
            return i

        def transpose_to(src_bf, np_, nk, dst_fn, src_res, dst_res):
            ti = rr["pt"] % 2
            rr["pt"] += 1

            def tr(e):
                ins = None
                for k in range(nk):
                    ins = e.transpose(out=pt[ti][:, k * 128:k * 128 + np_], in_=src_bf[0:np_, k * 128:(k + 1) * 128],
                                      identity=identb[0:np_, 0:np_])
                return ins
            S.add("pe", tr, reads=src_res + ["identb"], writes=[PT(ti)])
            S.add("act", lambda e: e.copy(
                out=dst_fn(), in_=pt[ti][:, 0:nk * 128].rearrange("p (k t) -> p k t", k=nk)[:, :, 0:np_]),
                reads=[PT(ti)], writes=dst_res)

        def layer_norm_blocks(nb, np_, l, which, last_out=None):
            eps = LN_EPS / (ALPHA * ALPHA)
            S.add("sp", lambda e: e.dma_start(out=lngs[:], in_=lng[l, which]), writes=["lng"], dma=True)
            S.add("sp", lambda e: e.dma_start(out=lnbs[:], in_=lnb[l, which]), writes=["lnb"], dma=True)
            for b in range(nb):
                xr = x32[0:np_, b, :]
                S.add("dve", lambda e, xr=xr: e.bn_stats(out=stats[0:np_, 0, :], in_=xr[:, 0:512]),
                      reads=[("x32", b)], writes=["stats"])
                S.add("dve", lambda e, xr=xr: e.bn_stats(out=stats[0:np_, 1, :], in_=xr[:, 512:1024]),
                      reads=[("x32", b)], writes=["stats1"])
                S.add("dve", lambda e: e.bn_aggr(out=mv[0:np_, 0, :],
                                                 in_=stats[0:np_, 0:2, :].rearrange("p a s -> p (a s)")),
                      reads=["stats", "stats1"], writes=["mv"])
                S.add("dve", lambda e: e.tensor_scalar(out=rstd[0:np_, 0:1], in0=mv[0:np_, 0, 1:2], scalar1=eps,
                                                       scalar2=None, op0=ALU.add),
                      reads=["mv"], writes=["rstd"])
                S.add("act", lambda e: e.activation(out=rstd[0:np_, 0:1], in_=rstd[0:np_, 0:1], func=AF.Sqrt),
                      reads=["rstd"], writes=["rstd"])
                S.add("dve", lambda e: e.reciprocal(out=rstd[0:np_, 0:1], in_=rstd[0:np_, 0:1]),
                      reads=["rstd"], writes=["rstd"])
                S.add("dve", lambda e, xr=xr: e.tensor_scalar(out=xr, in0=xr, scalar1=mv[0:np_, 0, 0:1],
                                                              scalar2=rstd[0:np_, 0:1], op0=ALU.subtract, op1=ALU.mult),
                      reads=[("x32", b), "mv", "rstd"], writes=[("x32", b)])
                S.add("pool", lambda e, xr=xr: e.tensor_tensor(out=xr, in0=xr, in1=lngs[0:np_, :], op=ALU.mult),
                      reads=[("x32", b), "lng"], writes=[("x32", b)])
                S.add("pool", lambda e, xr=xr: e.tensor_tensor(out=xr, in0=xr, in1=lnbs[0:np_, :], op=ALU.add),
                      reads=[("x32", b), "lnb"], writes=[("x32", b)])
                if last_out is not None:
                    S.add("sp", lambda e, xr=xr, b=b: e.dma_start(out=last_out(b), in_=xr), reads=[("x32", b)],
                          writes=[("yout", b)], dma=True)
                S.add("act", lambda e, xr=xr: e.copy(out=xb[0:np_, :], in_=xr), reads=[("x32", b)], writes=["xb"])
                transpose_to(xb, np_, KD, (lambda b=b: xT[:, :, b * 128:b * 128 + np_]), ["xb"], [("xT", b)])

        def ffn(nb, np_, l, which, last_out=None):
            ntok = (nb - 1) * 128 + np_
            gname = "gu1" if which == 0 else "gu2"
            dname = "dn1" if which == 0 else "dn2"
            xTr = [("xT", b) for b in range(nb)]
            for fp in range(KF // 2):
                i = load_slot(gname, l, fp)
                for j in range(2):
                    f = fp * 2 + j
                    par = f % 2
                    pa, pu = ps[2 * par], ps[2 * par + 1]

                    def mm(e, i=i, j=j, pa=pa, pu=pu):
                        ins = None
                        for k in range(KD):
                            ins = e.matmul(pa[:, 0:ntok], lhsT=wslot[i][:, k, j * 128:(j + 1) * 128],
                                           rhs=xT[:, k, 0:ntok], start=(k == 0), stop=(k == KD - 1))
                        for k in range(KD):
                            ins = e.matmul(pu[:, 0:ntok], lhsT=wslot[i][:, k, 256 + j * 128:256 + (j + 1) * 128],
                                           rhs=xT[:, k, 0:ntok], start=(k == 0), stop=(k == KD - 1))
                        return ins
                    S.add("pe", mm, reads=WS(i) + xTr, writes=[PS(2 * par), PS(2 * par + 1)])
                    S.add("act", lambda e, pa=pa, par=par: e.activation(out=sabuf[par][:, 0:ntok], in_=pa[:, 0:ntok],
                                                                        func=AF.Silu),
                          reads=[PS(2 * par)], writes=[("sa", par)])
                    S.add("dve", lambda e, pu=pu, par=par, f=f: e.tensor_tensor(
                        out=hbuf[:, f, 0:ntok], in0=pu[:, 0:ntok], in1=sabuf[par][:, 0:ntok], op=ALU.mult),
                        reads=[PS(2 * par + 1), ("sa", par)], writes=[("h", f)])
            pieces = [(0, 8), (8, 16), (16, 22)]
            for half in range(2):
                for pc, (f0, f1) in enumerate(pieces):
                    si = load_slot(dname, l, half * 3 + pc)
                    for b in range(nb):
                        npb = 128 if b < nb - 1 else np_
                        pi = 4 + b

                        def mm2(e, si=si, b=b, npb=npb, pi=pi, f0=f0, f1=f1):
                            ins = None
                            for f in range(f0, f1):
                                ins = e.matmul(ps[pi][0:npb, :], lhsT=hbuf[:, f, b * 128:b * 128 + npb],
                                               rhs=wslot[si][:, f - f0, :], start=(f == 0), stop=(f == KF - 1))
                            return ins
                        S.add("pe", mm2, reads=[("h", f) for f in range(f0, f1)] + WS(si), writes=[PS(pi)])
                for b in range(nb):
                    npb = 128 if b < nb - 1 else np_
                    pi = 4 + b
                    xr = x32[0:npb, b, half * 512:(half + 1) * 512]
                    S.add("dve", lambda e, xr=xr, pi=pi, npb=npb: e.scalar_tensor_tensor(
                        out=xr, in0=ps[pi][0:npb, :], scalar=0.5 / ALPHA, in1=xr, op0=ALU.mult, op1=ALU.add),
                        reads=[PS(pi), ("x32", b)], writes=[("x32", b)])
            layer_norm_blocks(nb, np_, l, 0 if which == 0 else 2, last_out=last_out)

        VR_ATOMS = [0, 256, 768]

        def vr_atoms(name, b):
            return [(name, b, o) for o in VR_ATOMS]

        def mix_proj(nb, np_, l, sample):
            ntok = (nb - 1) * 128 + np_
            xTr = [("xT", b) for b in range(nb)]
            if sample:
                fm = [(C_QR + 128 * i, "qr", i) for i in range(4)] + [(C_GA + 128 * i, "g", i) for i in range(16)]
                tm = [(0, 768, "zs"), (C_QR, C_KR, "qr32"), (C_KR, C_VR, "kr32"), (C_VR, C_GR, "vr"), (C_GR, C_GA, "gr")]
            else:
                fm = [(C_QA + 128 * i, "qa", i) for i in range(4)] + [(C_KA, "ka", 0), (C_KAS, "ka", 1)] + \
                     [(C_QR + 128 * i, "qr", i) for i in range(4)] + [(C_KR + 128 * i, "kr", i) for i in range(4)] + \
                     [(C_GA + 128 * i, "g", i) for i in range(16)]
                tm = [(C_VA, C_QR, "va"), (C_KR, C_VR, "kr"), (C_VR, C_GR, "vr"), (C_GR, C_GA, "gr"), (C_KA, C_VA, "ka32")]
            KP = os.environ.get("KPROJ", "")
            if KP:
                fm = [x for x in fm if x[1] in KP.split(",")]
                tm = [x for x in tm if x[2] in KP.split(",")]
            for g in range(12):
                c0 = g * 512
                ncols = min(512, INC2 - c0)
                si = load_slot("win", l, g)
                for (cc, kind, idx) in fm:
                    if not (c0 <= cc < c0 + ncols):
                        continue
                    pi = rr["ps_a"] % 4
                    rr["ps_a"] += 1
                    lo = cc - c0

                    def mm(e, si=si, lo=lo, pi=pi):
                        ins = None
                        for k in range(KD):
                            ins = e.matmul(ps[pi][:, 0:ntok], lhsT=wslot[si][:, k, lo:lo + 128], rhs=xT[:, k, 0:ntok],
                                           start=(k == 0), stop=(k == KD - 1))
                        return ins
                    S.add("pe", mm, reads=WS(si) + xTr, writes=[PS(pi)])
                    src = ps[pi][:, 0:ntok]
                    if kind == "qa":
                        S.add("act", lambda e, pi=pi, idx=idx: e.copy(out=qz[0:64, 0, idx, 0:ntok],
                                                                       in_=ps[pi][0:64, 0:ntok]),
                              reads=[PS(pi)], writes=[("qaT", idx)])
                        S.add("act", lambda e, pi=pi, idx=idx: e.copy(out=qz[64:128, 1, idx, 0:ntok],
                                                                       in_=ps[pi][64:128, 0:ntok]),
                              reads=[PS(pi)], writes=[("qaT", idx, 1)])
                    elif kind == "ka":
                        S.add("act", lambda e, src=src, idx=idx: e.copy(out=kaT[:, idx, 128:128 + ntok], in_=src),
                              reads=[PS(pi)], writes=[("kaT", idx)])
                    elif kind == "qr":
                        if sample:
                            S.add("act", lambda e, src=src, idx=idx: e.copy(out=qrT32[:, idx, :], in_=src),
                                  reads=[PS(pi)], writes=[("qrT32", idx)])
                        else:
                            S.add("act", lambda e, src=src, idx=idx: e.copy(out=qrT[:, idx, 0:ntok], in_=src),
                                  reads=[PS(pi)], writes=[("qrT", idx)])
                    elif kind == "kr":
                        S.add("act", lambda e, src=src, idx=idx: e.copy(out=krT[:, idx, 0:ntok], in_=src),
                              reads=[PS(pi)], writes=[("krT", idx)])
                    else:
                        S.add("act", lambda e, src=src, idx=idx: e.activation(out=gT[:, idx, 0:ntok], in_=src,
                                                                              func=AF.Sigmoid),
                              reads=[PS(pi)], writes=[("gT", idx)])
                for (a0, a1, kind) in tm:
                    lo_c = max(a0, c0)
                    hi_c = min(a1, c0 + ncols)
                    if lo_c >= hi_c:
                        continue
                    w = hi_c - lo_c
                    for b in range(nb):
                        npb = 128 if b < nb - 1 else np_
                        if kind == "ka32" and not (b == nb - 1):
                            continue
                        pi = 4 + (rr["ps_b"] % 2)
                        rr["ps_b"] += 1

                        def mm(e, si=si, lo=lo_c - c0, w=w, b=b, npb=npb, pi=pi):
                            ins = None
                            for k in range(KD):
                                ins = e.matmul(ps[pi][0:npb, 0:w], lhsT=xT[:, k, b * 128:b * 128 + npb],
                                               rhs=wslot[si][:, k, lo:lo + w], start=(k == 0), stop=(k == KD - 1))
                            return ins
                        S.add("pe", mm, reads=WS(si) + [("xT", b)], writes=[PS(pi)])
                        src = ps[pi][0:npb, 0:w]
                        o = lo_c - a0
                        if kind == "va":
                            S.add("act", lambda e, src=src, b=b: e.copy(out=vA[:, b + 1, :], in_=src),
                                  reads=[PS(pi)], writes=[("vA", b + 1)])
                            if b == nb - 1:
                                S.add("dve", lambda e, src=src: e.tensor_copy(out=kv32[:, 128:256], in_=src),
                                      reads=[PS(pi)], writes=["kv32v"])
                        elif kind == "ka32":
                            S.add("dve", lambda e, src=src: e.tensor_copy(out=kv32[:, 0:128], in_=src),
                                  reads=[PS(pi)], writes=["kv32k"])
                        elif kind == "kr":
                            for hh in range(w // 128):
                                h = (o + hh * 128) // 128
                                S.add("dve", lambda e, pi=pi, hh=hh, h=h, b=b: e.tensor_scalar(
                                    out=kdec[:, b, h * 128:(h + 1) * 128], in0=ps[pi][:, hh * 128:(hh + 1) * 128],
                                    scalar1=kscale[:, h:h + 1], scalar2=None, op0=ALU.mult),
                                    reads=[PS(pi), "cs"], writes=[("kdec", b, h)])
                        elif kind == "vr":
                            if sample:
                                S.add("act", lambda e, src=src, o=o, w=w: e.copy(out=vr32[:, o:o + w], in_=src),
                                      reads=[PS(pi)], writes=[("vr32", o)])
                            else:
                                S.add("act", lambda e, src=src, o=o, w=w, b=b: e.copy(out=vR[:, b, o:o + w], in_=src),
                                      reads=[PS(pi)], writes=[("vR", b, o)])
                        elif kind == "gr":
                            S.add("act", lambda e, src=src, o=o, w=w, b=b, npb=npb: e.activation(
                                out=gR[0:npb, b, o:o + w], in_=src, func=AF.Silu),
                                reads=[PS(pi)], writes=[("gR", b, o)])
                        elif kind == "zs":
                            S.add("act", lambda e, src=src, o=o, w=w: e.copy(out=zs32[:, o:o + w], in_=src),
                                  reads=[PS(pi)], writes=[("zs32", o)])
                        elif kind == "qr32":
                            S.add("act", lambda e, src=src, o=o, w=w: e.copy(out=qr32[:, o:o + w], in_=src),
                                  reads=[PS(pi)], writes=[("qr32", o)])
                        elif kind == "kr32":
                            S.add("act", lambda e, src=src, o=o, w=w: e.copy(out=kr32[:, o:o + w], in_=src),
                                  reads=[PS(pi)], writes=[("kr32", o)])

        def mix_out(nb, np_, l):
            ntok = (nb - 1) * 128 + np_
            oaTr = [("oaT", b) for b in range(nb)]
            orTr = [("orT", b) for b in range(nb)]
            for half in range(2):
                sa_ = load_slot("bra", l, half)
                sr_ = load_slot("brr", l, half)
                for mm_ in range(4):
                    m = half * 4 + mm_
                    par = m % 2
                    pa, pr = ps[2 * par], ps[2 * par + 1]

                    def mm(e, mm_=mm_, pa=pa, pr=pr, sa_=sa_, sr_=sr_):
                        ins = None
                        for k in range(4):
                            ins = e.matmul(pa[:, 0:ntok], lhsT=wslot[sa_][:, k, mm_ * 128:(mm_ + 1) * 128],
                                           rhs=oaT[:, k, 0:ntok], start=(k == 0), stop=(k == 3))
                        for k in range(KD):
                            ins = e.matmul(pr[:, 0:ntok], lhsT=wslot[sr_][:, k, mm_ * 128:(mm_ + 1) * 128],
                                           rhs=orT[:, k, 0:ntok], start=(k == 0), stop=(k == KD - 1))
                        return ins
                    S.add("pe", mm, reads=WS(sa_) + WS(sr_) + oaTr + orTr, writes=[PS(2 * par), PS(2 * par + 1)])
                    S.add("dve", lambda e, pa=pa, m=m: e.tensor_tensor(out=tmpm[:, 0:ntok], in0=pa[:, 0:ntok],
                                                                       in1=gT[:, m, 0:ntok], op=ALU.mult),
                          reads=[PS(2 * par), ("gT", m)], writes=["tmpm"])
                    S.add("dve", lambda e, pr=pr, m=m: e.tensor_tensor(out=mT[:, m, 0:ntok], in0=pr[:, 0:ntok],
                                                                       in1=gT[:, 8 + m, 0:ntok], op=ALU.mult),
                          reads=[PS(2 * par + 1), ("gT", 8 + m)], writes=[("mT", m)])
                    S.add("pool", lambda e, m=m: e.tensor_tensor(out=mT[:, m, 0:ntok], in0=mT[:, m, 0:ntok],
                                                                 in1=tmpm[:, 0:ntok], op=ALU.add),
                          reads=["tmpm", ("mT", m)], writes=[("mT", m)])
            for half in range(2):
                si = load_slot("wo", l, half)
                for b in range(nb):
                    npb = 128 if b < nb - 1 else np_
                    pi = 4 + (rr["ps_b"] % 2)
                    rr["ps_b"] += 1

                    def mm2(e, si=si, b=b, npb=npb, pi=pi):
                        ins = None
                        for k in range(KD):
                            ins = e.matmul(ps[pi][0:npb, :], lhsT=mT[:, k, b * 128:b * 128 + npb], rhs=wslot[si][:, k, :],
                                           start=(k == 0), stop=(k == KD - 1))
                        return ins
                    S.add("pe", mm2, reads=[("mT", m) for m in range(KD)] + WS(si), writes=[PS(pi)])
                    xr = x32[0:npb, b, half * 512:(half + 1) * 512]
                    S.add("dve", lambda e, xr=xr, pi=pi, npb=npb: e.scalar_tensor_tensor(
                        out=xr, in0=ps[pi][0:npb, :], scalar=1.0 / ALPHA, in1=xr, op0=ALU.mult, op1=ALU.add),
                        reads=[PS(pi), ("x32", b)], writes=[("x32", b)])
            layer_norm_blocks(nb, np_, l, 1)

        def gn_block(np_, l, b, src2_fn, src_fn, src_res):
            for h in range(4):
                S.add("dve", lambda e, h=h: e.bn_stats(out=gstats[0:np_, h, :], in_=src_fn(h)),
                      reads=src_res, writes=[("gst", h // 2)] if h % 2 else [("gstx", h // 2)])
            for h in range(4):
                S.add("dve", lambda e, h=h: e.bn_aggr(out=gmv[0:np_, h, :], in_=gstats[0:np_, h, :]),
                      reads=[("gst", 0), ("gst", 1), ("gstx", 0), ("gstx", 1)], writes=[("mvh", h)])
            S.add("dve", lambda e: e.tensor_scalar(out=grstd[0:np_, 0:4], in0=gmv[0:np_, :, 1], scalar1=GN_EPS,
                                                   scalar2=None, op0=ALU.add),
                  reads=[("mvh", h) for h in range(4)], writes=["grstd"])
            S.add("act", lambda e: e.activation(out=grstd[0:np_, 0:4], in_=grstd[0:np_, 0:4], func=AF.Sqrt),
                  reads=["grstd"], writes=["grstd"])
            S.add("dve", lambda e: e.reciprocal(out=grstd[0:np_, 0:4], in_=grstd[0:np_, 0:4]),
                  reads=["grstd"], writes=["grstd"])
            for h in range(4):
                S.add("dve", lambda e, h=h: e.tensor_scalar(out=orn[0:np_, h * 256:(h + 1) * 256], in0=src_fn(h),
                                                            scalar1=gmv[0:np_, h, 0:1], scalar2=grstd[0:np_, h:h + 1],
                                                            op0=ALU.subtract, op1=ALU.mult),
                      reads=src_res + [("mvh", h), "grstd"], writes=[("orn", h)])
            S.add("pool", lambda e: e.tensor_tensor(out=orn[0:np_, :], in0=orn[0:np_, :], in1=gngs[0:np_, :], op=ALU.mult),
                  reads=[("orn", h) for h in range(4)] + ["lng"], writes=[("orn", h) for h in range(4)])
            S.add("pool", lambda e: e.tensor_tensor(out=orb[0:np_, :], in0=orn[0:np_, :], in1=gR[0:np_, b, :], op=ALU.mult),
                  reads=[("orn", h) for h in range(4)] + vr_atoms("gR", b), writes=["orb"])
            transpose_to(orb, np_, KD, (lambda: orT[:, :, b * 128:b * 128 + np_]), ["orb"], [("orT", b)])

        def mix_prompt(l, tile_idx, is_last_tile):
            nb = NB
            S.add("pool", lambda e: e.tensor_copy(out=kaT[:, :, 0:128], in_=haloK[:, l, :, :]), reads=[("haloK", l)],
                  writes=["kaTh"])
            S.add("pool", lambda e: e.tensor_copy(out=vA[:, 0, :], in_=haloV[:, l, :]), reads=[("haloV", l)],
                  writes=[("vA", 0)])
            S.add("sp", lambda e: e.dma_start(out=sinks[:], in_=sinkb[l]), writes=["sinks"], dma=True)
            S.add("sp", lambda e: e.dma_start(out=gngs[:], in_=gng[l]), writes=["lng"], dma=True)
            mix_proj(nb, 128, l, False)
            if SUB < 2:
                return
            S.add("pool", lambda e: e.tensor_copy(out=haloK[:, l, :, :], in_=kaT[:, :, TT:TT + 128]),
                  reads=[("kaT", 0), ("kaT", 1)], writes=[("haloK", l)])
            S.add("pool", lambda e: e.tensor_copy(out=haloV[:, l, :], in_=vA[:, NB, :]), reads=[("vA", NB)],
                  writes=[("haloV", l)])
            if is_last_tile:
                S.add("sp", lambda e: e.dma_start(out=nkp[l], in_=kv32[:, 0:128]), reads=["kv32k"], writes=[("nkp", l)],
                      dma=True)
                S.add("sp", lambda e: e.dma_start(out=nvp[l], in_=kv32[:, 128:256]), reads=["kv32v"], writes=[("nvp", l)],
                      dma=True)
            def blk(b, first, s0, ns, kcol0, nsb):
                if KATT < 1:
                    return
                for hp in range(4):
                    def mm(e, hp=hp, b=b):
                        ins = None
                        for j in range(2):
                            h = 2 * hp + j
                            kv = h // 4
                            which = 0 if kv == j else 1
                            ins = e.matmul(ps[hp][:, j * 256 + s0:(j + 1) * 256],
                                           lhsT=qz[:, j, hp, b * 128:(b + 1) * 128],
                                           rhs=kaT[:, which, kcol0:kcol0 + ns], start=True, stop=True)
                        return ins
                    S.add("pe", mm, reads=[("qaT", hp), ("qaT", hp, 1), ("kaT", 0), ("kaT", 1), "kaTh"], writes=[PS(hp)])
                    if b == 0:
                        b0v = bias0s[:, :].rearrange("p (h s) -> p h s", h=8)
                        S.add("dve", lambda e, hp=hp: e.scalar_tensor_tensor(
                            out=sc[:, 2 * hp:2 * hp + 2, 0:128],
                            in0=ps[hp][:, :].rearrange("p (j s) -> p j s", j=2)[:, :, 0:128], scalar=0.125,
                            in1=b0v[:, 2 * hp:2 * hp + 2, :], op0=ALU.mult, op1=ALU.add),
                            reads=[PS(hp), "bias0s"], writes=[("sc", hp, 0)])
                        S.add("dve", lambda e, hp=hp: e.scalar_tensor_tensor(
                            out=sc[:, 2 * hp:2 * hp + 2, 128:256],
                            in0=ps[hp][:, :].rearrange("p (j s) -> p j s", j=2)[:, :, 128:256], scalar=0.125,
                            in1=biasR[:, 2 * hp:2 * hp + 2, 128:256], op0=ALU.mult, op1=ALU.add),
                            reads=[PS(hp), "cs"], writes=[("sc", hp)])
                    else:
                        S.add("dve", lambda e, hp=hp: e.scalar_tensor_tensor(
                            out=sc[:, 2 * hp:2 * hp + 2, s0:256],
                            in0=ps[hp][:, :].rearrange("p (j s) -> p j s", j=2)[:, :, s0:256], scalar=0.125,
                            in1=biasR[:, 2 * hp:2 * hp + 2, s0:256], op0=ALU.mult, op1=ALU.add),
                            reads=[PS(hp), "cs"], writes=[("sc", hp), ("sc", hp, 0)])
                scr = [("sc", hp) for hp in range(4)] + [("sc", hp, 0) for hp in range(4)]
                if KATT < 2:
                    return
                S.add("dve", lambda e: e.tensor_reduce(out=mx[:], in_=sc[:, :, s0:256], axis=AX.X, op=ALU.max),
                      reads=scr, writes=["mx"])
                S.add("dve", lambda e: e.tensor_tensor(out=mx[:], in0=mx[:], in1=sinks[:], op=ALU.max),
                      reads=["mx", "sinks"], writes=["mx"])
                S.add("dve", lambda e: e.tensor_scalar(out=negm[:], in0=mx[:], scalar1=-1.0, scalar2=None, op0=ALU.mult),
                      reads=["mx"], writes=["negm"])
                S.add("dve", lambda e: e.tensor_tensor(out=esk[:], in0=sinks[:], in1=mx[:], op=ALU.subtract),
                      reads=["mx", "sinks"], writes=["esk"])
                S.add("act", lambda e: e.activation(out=esk[:], in_=esk[:], func=AF.Exp), reads=["esk"], writes=["esk"])
                if KATT < 3:
                    return
                for h in range(8):
                    S.add("act", lambda e, h=h: e.activation(out=pb[:, h, s0:256], in_=sc[:, h, s0:256], func=AF.Exp,
                                                             bias=negm[:, h:h + 1], scale=1.0),
                          reads=scr + ["negm"], writes=[("pb", h)])
                if KATT < 4:
                    return
                S.add("dve", lambda e: e.tensor_reduce(out=rsum[:], in_=pb[:, :, s0:256], axis=AX.X, op=ALU.add),
                      reads=[("pb", h) for h in range(8)], writes=["rsum"])
                S.add("dve", lambda e: e.tensor_tensor(out=rden[:], in0=rsum[:], in1=esk[:], op=ALU.add),
                      reads=["rsum", "esk"], writes=["rden"])
                S.add("dve", lambda e: e.reciprocal(out=rden[:], in_=rden[:]), reads=["rden"], writes=["rden"])
                if SUB < 3:
                    return
                for half in range(2):
                    ti = rr["pt"] % 2
                    rr["pt"] += 1

                    def tr(e, half=half, ti=ti):
                        ins = None
                        for hh in range(4):
                            h = half * 4 + hh
                            for sbk in range(nsb):
                                c0 = s0 + sbk * 128
                                ins = e.transpose(out=pt[ti][:, (hh * 2 + sbk) * 128:(hh * 2 + sbk + 1) * 128],
                                                  in_=pb[:, h, c0:c0 + 128], identity=identb[:])
                        return ins
                    S.add("pe", tr, reads=[("pb", half * 4 + hh) for hh in range(4)] + ["identb"], writes=[PT(ti)])
                    S.add("act", lambda e, half=half, ti=ti: e.copy(out=pT[:, half * 8:(half + 1) * 8, :],
                                                                    in_=pt[ti][:, :].rearrange("p (a t) -> p a t", a=8)),
                          reads=[PT(ti)], writes=[("pT", half)])

                def pv(e, b=b):
                    ins = None
                    for h in range(8):
                        kv = h // 4
                        for sbk in range(nsb):
                            vb = b + sbk + (1 if first else 0)
                            ins = e.matmul(ps[4][:, h * 64:(h + 1) * 64], lhsT=pT[:, h * 2 + sbk, :],
                                           rhs=vA[:, vb, kv * 64:(kv + 1) * 64], start=(sbk == 0), stop=(sbk == nsb - 1))
                    return ins
                S.add("pe", pv, reads=[("pT", 0), ("pT", 1), ("vA", b), ("vA", b + 1)], writes=[PS(4)])
                S.add("dve", lambda e: e.tensor_tensor(
                    out=oab[:, :].rearrange("p (h d) -> p h d", h=8),
                    in0=ps[4][:, :].rearrange("p (h d) -> p h d", h=8),
                    in1=rden[:, :].unsqueeze(2).broadcast_to([128, 8, 64]), op=ALU.mult),
                    reads=[PS(4), "rden"], writes=["oab"])
                transpose_to(oab, 128, 4, (lambda b=b: oaT[:, :, b * 128:(b + 1) * 128]), ["oab"], [("oaT", b)])

                if SUB < 4:
                    return

                def mmsc(e, b=b):
                    ins = None
                    for h in range(4):
                        ins = e.matmul(ps[5][:, h * 128:(h + 1) * 128], lhsT=krT[:, h, b * 128:(b + 1) * 128],
                                       rhs=qrT[:, h, b * 128:(b + 1) * 128], start=True, stop=True)
                    return ins
                S.add("pe", mmsc, reads=[("krT", h) for h in range(4)] + [("qrT", h) for h in range(4)], writes=[PS(5)])
                S.add("dve", lambda e: e.tensor_tensor(out=scTb[:], in0=ps[5][:, :].rearrange("p (h i) -> p h i", h=4),
                                                       in1=decT, op=ALU.mult), reads=[PS(5), "cs"], writes=["scTb"])
                S.add("pool", lambda e, b=b: e.tensor_tensor(out=qdec[:], in0=qrT[:, :, b * 128:(b + 1) * 128], in1=gqc,
                                                             op=ALU.mult),
                      reads=[("qrT", h) for h in range(4)] + ["cs"], writes=["qdec"])

                def mmo(e, b=b):
                    ins = None
                    for h in range(4):
                        dst = ps[h // 2][:, (h % 2) * 256:(h % 2 + 1) * 256]
                        e.matmul(dst, lhsT=scTb[:, h, :], rhs=vR[:, b, h * 256:(h + 1) * 256], start=True, stop=False)
                        ins = e.matmul(dst, lhsT=qdec[:, h, :], rhs=Sb[:, l, h * 256:(h + 1) * 256], start=False, stop=True)
                    return ins
                S.add("pe", mmo, reads=["scTb", "qdec", ("Sb", l)] + vr_atoms("vR", b), writes=[PS(0), PS(1)])

                def mmu(e, b=b):
                    ins = None
                    for h in range(4):
                        dst = ps[2 + h // 2][:, (h % 2) * 256:(h % 2 + 1) * 256]
                        ins = e.matmul(dst, lhsT=kdec[:, b, h * 128:(h + 1) * 128], rhs=vR[:, b, h * 256:(h + 1) * 256],
                                       start=True, stop=True)
                    return ins
                S.add("pe", mmu, reads=[("kdec", b, h) for h in range(4)] + vr_atoms("vR", b), writes=[PS(2), PS(3)])
                for h in range(4):
                    S.add("dve", lambda e, h=h: e.scalar_tensor_tensor(
                        out=S32[:, l, h * 256:(h + 1) * 256], in0=S32[:, l, h * 256:(h + 1) * 256],
                        scalar=consts["gamma128"][h], in1=ps[2 + h // 2][:, (h % 2) * 256:(h % 2 + 1) * 256],
                        op0=ALU.mult, op1=ALU.add), reads=[("S32", l), PS(2 + h // 2)], writes=[("S32", l)])
                S.add("act", lambda e: e.copy(out=Sb[:, l, :], in_=S32[:, l, :]), reads=[("S32", l)], writes=[("Sb", l)])
                if SUB < 5:
                    return
                gn_block(128, l, b, lambda j: ps[j][:, :].rearrange("p (a v) -> p a v", a=2),
                         lambda h: ps[h // 2][:, (h % 2) * 256:(h % 2 + 1) * 256], [PS(0), PS(1)])

            for b in range(nb):
                first = (tile_idx == 0 and b == 0)
                s0 = 128 if first else 0
                blk(b, first, s0, 256 - s0, b * 128 + s0, (256 - s0) // 128)
            if is_last_tile:
                S.add("sp", lambda e: e.dma_start(out=nrp[l].rearrange("h p v -> p h v"),
                                                  in_=S32[:, l, :].rearrange("p (h v) -> p h v", h=4)),
                      reads=[("S32", l)], writes=[("nrp", l)], dma=True)
            if SUB < 6:
                return
            mix_out(nb, 128, l)

        def mix_sample(l):
            np_ = NSAMP
            S.add("sp", lambda e: e.dma_start(out=gngs[:], in_=gng[l]), writes=["lng"], dma=True)
            S.add("sp", lambda e: e.dma_start(out=sinkS[:], in_=sinkbh[l]), writes=["sinkS"], dma=True)
            mix_proj(1, np_, l, True)
            zsr = [("zs32", 0), ("zs32", 512)]
            S.add("sp", lambda e: e.dma_start(out=zq[l], in_=zs32[:, 0:512]), reads=zsr, writes=[("zq", l)], dma=True)
            S.add("sp", lambda e: e.dma_start(out=zk[l], in_=zs32[:, 512:640]), reads=zsr, writes=[("zk", l)], dma=True)
            S.add("sp", lambda e: e.dma_start(out=zv[l], in_=zs32[:, 640:768]), reads=zsr, writes=[("zv", l)], dma=True)
            S.add("sp", lambda e: e.dma_start(out=nks[l][:, 0:127, :], in_=ck[l][:, 1:128, :]), writes=[("nks", l, 0)],
                  dma=True)
            S.add("sp", lambda e: e.dma_start(out=nvs[l][:, 0:127, :], in_=cv[l][:, 1:128, :]), writes=[("nvs", l, 0)],
                  dma=True)
            S.add("sp", lambda e: e.dma_start(out=nks[l][:, 127, :], in_=zs32[:, 512:640]), reads=zsr,
                  writes=[("nks", l, 1)], dma=True)
            S.add("sp", lambda e: e.dma_start(out=nvs[l][:, 127, :], in_=zs32[:, 640:768]), reads=zsr,
                  writes=[("nvs", l, 1)], dma=True)
            S.add("sp", lambda e: e.dma_start(out=qs[:], in_=zq[l].rearrange("b (h d) -> (b h) d", h=8)),
                  reads=[("zq", l)], writes=["qs"], dma=True)
            kview = zk[l].rearrange("b (k d) -> b k d", k=2).unsqueeze(2).broadcast_to([NSAMP, 2, 4, 64])
            vview = zv[l].rearrange("b (k d) -> b k d", k=2).unsqueeze(2).broadcast_to([NSAMP, 2, 4, 64])
            S.add("sp", lambda e: e.dma_start(out=kn[:], in_=kview), reads=[("zk", l)], writes=["kn"], dma=True)
            S.add("sp", lambda e: e.dma_start(out=vn[:], in_=vview), reads=[("zv", l)], writes=["vn"], dma=True)
            S.add("act", lambda e: e.copy(out=qsb[:], in_=qs[:]), reads=["qs"], writes=["qsb"])
            for qi in range(128 // KQ):
                S.add("pool", lambda e, qi=qi: e.dma_start(out=Kt[:], in_=ckr[l][:, qi * KQ:(qi + 1) * KQ, :]),
                      writes=["Kt"], dma=True)
                S.add("dve", lambda e: e.tensor_tensor(out=tmpS[:], in0=Kt[:],
                                                       in1=qsb[:, :].unsqueeze(1).broadcast_to([128, KQ, 64]),
                                                       op=ALU.mult), reads=["Kt", "qsb"], writes=["tmpS"])
                S.add("dve", lambda e, qi=qi: e.tensor_reduce(out=scS[:, qi * KQ:(qi + 1) * KQ], in_=tmpS[:], axis=AX.X,
                                                              op=ALU.add), reads=["tmpS"], writes=[("scS", qi)])
            S.add("dve", lambda e: e.tensor_tensor(out=oS2[:], in0=kn[:], in1=qs[:], op=ALU.mult),
                  reads=["kn", "qs"], writes=["oS2"])
            S.add("dve", lambda e: e.tensor_reduce(out=scS[:, 128:129], in_=oS2[:], axis=AX.X, op=ALU.add),
                  reads=["oS2"], writes=[("scS", 128 // KQ)])
            scr_ = [("scS", i) for i in range(128 // KQ + 1)]
            S.add("dve", lambda e: e.scalar_tensor_tensor(out=scS[:, 0:129], in0=scS[:, 0:129], scalar=0.125,
                                                          in1=biasS[:, 0:129], op0=ALU.mult, op1=ALU.add),
                  reads=scr_ + ["cs"], writes=scr_)
            S.add("dve", lambda e: e.tensor_reduce(out=smS[:, 0:1], in_=scS[:, 0:129], axis=AX.X, op=ALU.max),
                  reads=scr_, writes=["smS0"])
            S.add("dve", lambda e: e.tensor_tensor(out=smS[:, 0:1], in0=smS[:, 0:1], in1=sinkS[:], op=ALU.max),
                  reads=["smS0", "sinkS"], writes=["smS0"])
            S.add("dve", lambda e: e.tensor_scalar(out=smS[:, 1:2], in0=smS[:, 0:1], scalar1=-1.0, scalar2=None,
                                                   op0=ALU.mult), reads=["smS0"], writes=["smS1"])
            S.add("dve", lambda e: e.tensor_tensor(out=smS[:, 2:3], in0=sinkS[:], in1=smS[:, 0:1], op=ALU.subtract),
                  reads=["smS0", "sinkS"], writes=["smS2"])
            S.add("act", lambda e: e.activation(out=smS[:, 2:3], in_=smS[:, 2:3], func=AF.Exp), reads=["smS2"],
                  writes=["smS2"])
            S.add("act", lambda e: e.activation(out=pS[:, 0:129], in_=scS[:, 0:129], func=AF.Exp, bias=smS[:, 1:2],
                                                scale=1.0), reads=scr_ + ["smS1"], writes=["pS"])
            S.add("dve", lambda e: e.tensor_reduce(out=smS[:, 3:4], in_=pS[:, 0:129], axis=AX.X, op=ALU.add),
                  reads=["pS"], writes=["smS3"])
            S.add("dve", lambda e: e.tensor_tensor(out=smS[:, 4:5], in0=smS[:, 3:4], in1=smS[:, 2:3], op=ALU.add),
                  reads=["smS3", "smS2"], writes=["smS4"])
            S.add("dve", lambda e: e.reciprocal(out=smS[:, 4:5], in_=smS[:, 4:5]), reads=["smS4"], writes=["smS4"])
            S.add("act", lambda e: e.copy(out=pSb[:, 0:128], in_=pS[:, 0:128]), reads=["pS"], writes=["pSb"])
            S.add("dve", lambda e: e.tensor_scalar(out=oS[:], in0=vn[:], scalar1=pS[:, 128:129], scalar2=None,
                                                   op0=ALU.mult), reads=["vn", "pS"], writes=["oS"])
            for qi in range(128 // KQ):
                S.add("pool", lambda e, qi=qi: e.dma_start(out=Vt[:], in_=cvr[l][:, qi * KQ:(qi + 1) * KQ, :]),
                      writes=["Vt"], dma=True)
                S.add("dve", lambda e, qi=qi: e.tensor_tensor(
                    out=tmpS[:], in0=Vt[:], in1=pSb[:, qi * KQ:(qi + 1) * KQ].unsqueeze(2).broadcast_to([128, KQ, 64]),
                    op=ALU.mult), reads=["Vt", "pSb"], writes=["tmpS"])
                S.add("dve", lambda e: e.tensor_reduce(out=oS2[:], in_=tmpS[:, :, :].rearrange("p s d -> p d s"),
                                                       axis=AX.X, op=ALU.add), reads=["tmpS"], writes=["oS2"])
                S.add("dve", lambda e: e.tensor_tensor(out=oS[:], in0=oS[:], in1=oS2[:], op=ALU.add),
                      reads=["oS", "oS2"], writes=["oS"])
            S.add("dve", lambda e: e.tensor_scalar(out=oS[:], in0=oS[:], scalar1=smS[:, 4:5], scalar2=None, op0=ALU.mult),
                  reads=["oS", "smS4"], writes=["oS"])
            S.add("sp", lambda e: e.dma_start(out=oscr[l], in_=oS[:]), reads=["oS"], writes=[("oscr", l)], dma=True)
            S.add("sp", lambda e: e.dma_start(out=oa32[:], in_=oscr[l].rearrange("(b h) d -> b (h d)", h=8)),
                  reads=[("oscr", l)], writes=["oa32"], dma=True)
            S.add("act", lambda e: e.copy(out=oab[0:np_, :], in_=oa32[:]), reads=["oa32"], writes=["oab"])
            transpose_to(oab, np_, 4, (lambda: oaT[:, :, 0:np_]), ["oab"], [("oaT", 0)])
            qrr = [("qr32", 0), ("qr32", 256)]
            krr = [("kr32", 0), ("kr32", 256)]
            vrr = [("vr32", o) for o in VR_ATOMS]
            S.add("dve", lambda e: e.tensor_tensor(out=tmq[:], in0=qr32[:], in1=kr32[:], op=ALU.mult), reads=qrr + krr,
                  writes=["tmq"])
            S.add("dve", lambda e: e.tensor_reduce(out=qk[:], in_=tmq[:, :].rearrange("b (h d) -> b h d", h=4), axis=AX.X,
                                                   op=ALU.add), reads=["tmq"], writes=["qk"])
            S.add("dve", lambda e: e.tensor_scalar(out=qk[:], in0=qk[:], scalar1=128 ** -0.5, scalar2=None, op0=ALU.mult),
                  reads=["qk"], writes=["qk"])
            for h in range(4):
                S.add("pool", lambda e, h=h: e.tensor_tensor(
                    out=qsel[:, h, :, :], in0=eyeb,
                    in1=qrT32[:, h, :].unsqueeze(1).broadcast_to([128, NSAMP, NSAMP]), op=ALU.mult),
                    reads=[("qrT32", h), "cs"], writes=[("qsel", h)])
            for b in range(NSAMP):
                S.add("sp", lambda e, b=b: e.dma_start(out=S0[:], in_=stt[l, b].rearrange("h p v -> p h v")),
                      writes=["S0"], dma=True)

                def mmc(e, b=b):
                    ins = None
                    if b == 0:
                        for j in range(2):
                            e.matmul(ps[j][0:np_, :], lhsT=zt[:, :], rhs=S0[:, 2 * j:2 * j + 2, :], start=True,
                                     stop=False, skip_group_check=True)
                    for h in range(4):
                        dst = ps[h // 2][0:np_, (h % 2) * 256:(h % 2 + 1) * 256]
                        ins = e.matmul(dst, lhsT=qsel[:, h, b, :], rhs=S0[:, h, :], start=False,
                                       stop=(b == NSAMP - 1), skip_group_check=True)
                    return ins
                S.add("pe", mmc, reads=["S0", "zt"] + [("qsel", h) for h in range(4)], writes=[PS(0), PS(1)])
                S.add("pool", lambda e, b=b: e.tensor_scalar(
                    out=vdiag[:, :, :], in0=vr32[:, :].rearrange("b (h v) -> b h v", h=4),
                    scalar1=cs[0:NSAMP, coff["eye16"] + b:coff["eye16"] + b + 1], scalar2=128 ** -0.5, op0=ALU.mult,
                    op1=ALU.mult), reads=vrr + ["cs"], writes=["vdiag"])

                def mmu(e):
                    ins = None
                    for h in range(4):
                        dst = ps[2 + h // 2][:, (h % 2) * 256:(h % 2 + 1) * 256]
                        ins = e.matmul(dst, lhsT=kr32[:, h * 128:(h + 1) * 128], rhs=vdiag[:, h, :], start=True,
                                       stop=True)
                    return ins
                S.add("pe", mmu, reads=krr + ["vdiag"], writes=[PS(2), PS(3)])
                for h in range(4):
                    S.add("dve", lambda e, h=h: e.scalar_tensor_tensor(
                        out=S1[:, h, :], in0=S0[:, h, :], scalar=consts["gamma"][h],
                        in1=ps[2 + h // 2][:, (h % 2) * 256:(h % 2 + 1) * 256], op0=ALU.mult, op1=ALU.add),
                        reads=["S0", PS(2 + h // 2)], writes=["S1"])
                S.add("sp", lambda e, b=b: e.dma_start(out=nrs[l, b].rearrange("h p v -> p h v"), in_=S1[:]),
                      reads=["S1"], writes=[("nrs", l, b)], dma=True)
            for h in range(4):
                S.add("dve", lambda e, h=h: e.tensor_scalar(
                    out=osam[:, h * 256:(h + 1) * 256], in0=ps[h // 2][0:np_, (h % 2) * 256:(h % 2 + 1) * 256],
                    scalar1=consts["gamma"][h], scalar2=None, op0=ALU.mult),
                    reads=[PS(h // 2)], writes=[("osam", h)])
                S.add("dve", lambda e, h=h: e.scalar_tensor_tensor(
                    out=osam[:, h * 256:(h + 1) * 256], in0=vr32[:, h * 256:(h + 1) * 256], scalar=qk[:, h:h + 1],
                    in1=osam[:, h * 256:(h + 1) * 256], op0=ALU.mult, op1=ALU.add),
                    reads=vrr + ["qk", ("osam", h)], writes=[("osam", h)])
            if STAGE == 15:
                S.add("sp", lambda e: e.dma_start(out=ys[0:NSAMP, :], in_=osam[:, :]),
                      reads=[("osam", h) for h in range(4)], writes=[("yout", 0)], dma=True)
                return
            gn_block(np_, l, 0, lambda j: osam[:, j * 512:(j + 1) * 512].rearrange("p (a v) -> p a v", a=2),
                     lambda h: osam[:, h * 256:(h + 1) * 256], [("osam", h) for h in range(4)])
            mix_out(1, np_, l)

        def load_x(src_fn, nb, np_):
            for b in range(nb):
                npb = 128 if b < nb - 1 else np_
                S.add("sp", lambda e, b=b, npb=npb: e.dma_start(out=x32[0:npb, b, :], in_=src_fn(b, npb)),
                      writes=[("x32", b)], dma=True)
                S.add("act", lambda e, b=b, npb=npb: e.copy(out=xb[0:npb, :], in_=x32[0:npb, b, :]),
                      reads=[("x32", b)], writes=["xb"])
                transpose_to(xb, npb, KD, (lambda b=b, npb=npb: xT[:, :, b * 128:b * 128 + npb]), ["xb"], [("xT", b)])

        def tile_body():
            for k_ in rr:
                rr[k_] = 0
            load_x(lambda b, npb: xp[bass.ds(S.loopvar["sp"] * TT + b * 128, 128), :], NB, 128)
            S.add("sp", lambda e: e.dma_start(out=bias0s[:], in_=bias0[bass.ds(S.loopvar["sp"] * 128, 128), :]),
                  writes=["bias0s"], dma=True)
            for l in range(L):
                ffn(NB, 128, l, 0)
                mix_prompt(l, 1, True)
                lo = None
                if l == L - 1:
                    lo = (lambda b: yp[bass.ds(S.loopvar["sp"] * TT + b * 128, 128), :])
                ffn(NB, 128, l, 1, last_out=lo)

        S.seg = "A"
        tile_body()
        S.seg = "B"
        tile_body()
        S.seg = "post"
        for k_ in rr:
            rr[k_] = 0
        if with_sample:
            load_x(lambda b, npb: xs[0:npb, :], 1, NSAMP)
            for l in range(L):
                ffn(1, NSAMP, l, 0)
                mix_sample(l)
                lo = None
                if l == L - 1:
                    lo = (lambda b: ys[0:NSAMP, :])
                ffn(1, NSAMP, l, 1, last_out=lo)

        S.emit(nc, NT)
    return nc, cpack


def prep_core_inputs(c, inp, cpack, T, L):
    f = np.float32
    seq = c % 2
    b0 = c * NSAMP
    m = {}
    m["xp"] = np.ascontiguousarray(inp["x_prompt"][seq, :T])
    m["xs"] = np.ascontiguousarray(inp["x_sample"][b0:b0 + NSAMP, 0])
    m["wall"] = WALL_CACHE["wall"]
    g = np.stack([inp["ln1_g"][:L], inp["ln2_g"][:L], inp["ln3_g"][:L]], axis=1)
    b = np.stack([inp["ln1_b"][:L], inp["ln2_b"][:L], inp["ln3_b"][:L]], axis=1)
    m["lng"] = np.ascontiguousarray(np.broadcast_to(g[:, :, None, :], (L, 3, 128, D))).astype(f)
    m["lnb"] = np.ascontiguousarray(np.broadcast_to(b[:, :, None, :], (L, 3, 128, D))).astype(f)
    m["gng"] = np.ascontiguousarray(np.broadcast_to(inp["ret_gn_g"][:L, None, :], (L, 128, D))).astype(f)
    m["sinkb"] = np.ascontiguousarray(np.broadcast_to(inp["attn_sinks"][:L, None, :], (L, 128, 8))).astype(f)
    m["sinkbh"] = np.ascontiguousarray(np.tile(inp["attn_sinks"][:L], (1, NSAMP)).reshape(L, 128, 1)).astype(f)
    m["cst"] = cpack
    cc = make_consts()
    nt = T // TT
    bz = np.tile(cc["biasR"][:, :, 0:128].reshape(1, 128, 1024), (nt, 1, 1))
    bz[0] = -1e30
    m["bias0"] = bz.reshape(nt * 128, 1024)
    ckc = inp["cache_win_k"][:L, b0:b0 + NSAMP]
    cvc = inp["cache_win_v"][:L, b0:b0 + NSAMP]
    m["ckr"] = np.ascontiguousarray(np.repeat(np.transpose(ckc, (0, 1, 3, 2, 4)), 4, axis=2).reshape(L, 128, 128, 64))
    m["cvr"] = np.ascontiguousarray(np.repeat(np.transpose(cvc, (0, 1, 3, 2, 4)), 4, axis=2).reshape(L, 128, 128, 64))
    m["ck"] = np.ascontiguousarray(ckc.reshape(L, NSAMP, 128, 128))
    m["cv"] = np.ascontiguousarray(cvc.reshape(L, NSAMP, 128, 128))
    m["stt"] = np.ascontiguousarray(inp["state_ret"][:L, b0:b0 + NSAMP])
    return {k: np.ascontiguousarray(v, dtype=f) for k, v in m.items()}


WALL_CACHE = {}


def tile_weights(inp, L):
    f = np.float32
    out = np.zeros((L, 52, 128, 4096), f)

    def img(sub):
        nk = sub.shape[0] // 128
        nc_ = sub.shape[1]
        a = np.zeros((128, 8, 512), f)
        a[:, :nk, :nc_] = sub.reshape(nk, 128, nc_).transpose(1, 0, 2)
        return a.reshape(128, 4096)

    pieces = [(0, 8), (8, 16), (16, 22)]
    for l in range(L):
        wi = inp["w_in"][l]
        win = np.concatenate([wi, wi[:, 576:640], wi[:, 512:576]], axis=1)
        for nm, base in (("w_ff1_gu", 0), ("w_ff2_gu", 35)):
            w = inp[nm][l]
            for fp in range(11):
                out[l, base + fp] = img(np.concatenate([w[:, fp * 256:(fp + 1) * 256],
                                                        w[:, DFF + fp * 256:DFF + (fp + 1) * 256]], axis=1))
        for nm, base in (("w_ff1_dn", 11), ("w_ff2_dn", 46)):
            w = inp[nm][l]
            for half in range(2):
                for pc, (f0, f1) in enumerate(pieces):
                    out[l, base + half * 3 + pc] = img(w[f0 * 128:f1 * 128, half * 512:(half + 1) * 512])
        for g in range(12):
            out[l, 17 + g] = img(win[:, g * 512:min((g + 1) * 512, INC2)])
        for half in range(2):
            out[l, 29 + half] = img(inp["w_br_a"][l][:, half * 512:(half + 1) * 512])
            out[l, 31 + half] = img(inp["w_br_r"][l][:, half * 512:(half + 1) * 512])
            out[l, 33 + half] = img(inp["w_o"][l][:, half * 512:(half + 1) * 512])
    return out


def run(inp, T, L, with_sample=True):
    nc, cpack = build(T, L, with_sample)
    WALL_CACHE["wall"] = tile_weights(inp, L)
    in_maps = [prep_core_inputs(c, inp, cpack, T, L) for c in range(8)]
    res = run_bass_kernel_spmd(nc, in_maps, core_ids=list(range(8)))
    r = res.results
    y_p = np.stack([r[0]["yp"], r[1]["yp"]], 0)
    y_s = np.concatenate([r[c]["ys"] for c in range(8)], 0)[:, None, :]
    nk_p = np.stack([r[0]["nkp"], r[1]["nkp"]], 1).reshape(L, 2, 128, 2, 64)
    nv_p = np.stack([r[0]["nvp"], r[1]["nvp"]], 1).reshape(L, 2, 128, 2, 64)
    nr_p = np.stack([r[0]["nrp"], r[1]["nrp"]], 1)
    nk_s = np.concatenate([r[c]["nks"] for c in range(8)], 1).reshape(L, 128, 128, 2, 64)
    nv_s = np.concatenate([r[c]["nvs"] for c in range(8)], 1).reshape(L, 128, 128, 2, 64)
    nr_s = np.concatenate([r[c]["nrs"] for c in range(8)], 1)
    return tuple(np.ascontiguousarray(a, dtype=np.float32) for a in (y_p, y_s, nk_p, nv_p, nr_p, nk_s, nv_s, nr_s))


def kernel(**inputs):
    inp = {k: np.asarray(v) for k, v in inputs.items()}
    return run(inp, 8192, 4, True)
```

```python
import os
import numpy as np
import concourse.bass as bass
import concourse.mybir as mybir
from concourse.bass_utils import run_bass_kernel_spmd

F32 = mybir.dt.float32
BF16 = mybir.dt.bfloat16
AF = mybir.ActivationFunctionType
ALU = mybir.AluOpType
AX = mybir.AxisListType

D = 1024
DFF = 2816
KD = 8
KF = 22
INC = 5888
INC2 = 6016
C_KAS = 5888
DEPTH_FULL = 4
ALPHA = (2.0 * DEPTH_FULL) ** 0.25
LN_EPS = 1e-5
GN_EPS = 1e-6
NH_A = 8
NH_R = 4
NSAMP = 16
TT = 256
NB = 2
C_QA, C_KA, C_VA, C_QR, C_KR, C_VR, C_GR, C_GA, C_GB = 0, 512, 640, 768, 1280, 1792, 2816, 3840, 4864
EPOCH = int(os.environ.get('KEPOCH', '8000'))
DEPOCH = int(os.environ.get('KDEPOCH', '1500'))
CINC = int(os.environ.get('KINC', '1'))


class Sched:
    def __init__(self):
        self.ops = []
        self.last_w = {}
        self.readers = {}
        self.seg = "pre"
        self.loopvar = {}

    def add(self, eng, fn, reads=(), writes=(), dma=False):
        deps = set()
        for r in reads:
            w = self.last_w.get(r)
            if w is not None:
                deps.add(w)
            if isinstance(r, tuple) and r[0] in ("ps", "pt"):
                deps.update(self.readers.get(r, ()))
        for r in writes:
            w = self.last_w.get(r)
            if w is not None:
                deps.add(w)
            deps.update(self.readers.get(r, ()))
        idx = len(self.ops)
        self.ops.append(dict(eng=eng, fn=fn, deps=deps, dma=dma, seg=self.seg))
        for r in reads:
            self.readers.setdefault(r, []).append(idx)
        for r in writes:
            self.last_w[r] = idx
            self.readers[r] = []
        return idx

    def emit(self, nc, n_iter, final_eng="sp"):
        import contextlib
        ops = self.ops
        engs = ["pe", "act", "dve", "pool", "sp"]
        segA = [i for i, op in enumerate(ops) if op["seg"] == "A"]
        segB = [i for i, op in enumerate(ops) if op["seg"] == "B"]
        assert len(segA) == len(segB)
        a2b = dict(zip(segA, segB))
        for ia, ib in a2b.items():
            assert ops[ia]["eng"] == ops[ib]["eng"] and ops[ia]["dma"] == ops[ib]["dma"]
        NBODY_C = 2
        NBODY_D = 12
        sem_of = {}
        counters = {}
        body_cnt = {e: sum(1 for i in segB if ops[i]["eng"] == e and not ops[i]["dma"]) for e in engs}
        body_seen = {}
        dma_rr = {}
        for i, op in enumerate(ops):
            e, sg = op["eng"], op["seg"]
            if sg == "A":
                continue
            if sg == "B":
                if op["dma"]:
                    k = dma_rr.get(("B", e), 0)
                    dma_rr[("B", e)] = k + 1
                    key = ("bd", e, k % NBODY_D)
                    c = counters.get(key, 0) + 16
                else:
                    n = body_seen.get(e, 0)
                    body_seen[e] = n + 1
                    half = (n * NBODY_C) // max(body_cnt[e], 1)
                    key = ("bc", e, half)
                    c = counters.get(key, 0) + 1
                counters[key] = c
                sem_of[i] = (key, c)
                continue
            if op["dma"]:
                k = dma_rr.get((sg, e), 0)
                dma_rr[(sg, e)] = k + 1
                key = ("d", sg, e, k % 6, (k // 6) // DEPOCH)
                c = counters.get(key, 0) + 16
            else:
                tot = counters.get(("n", sg, e), 0)
                counters[("n", sg, e)] = tot + 1
                key = ("c", sg, e, tot // EPOCH)
                c = counters.get(key, 0) + 1
            counters[key] = c
            sem_of[i] = (key, c)
        delta = {k: v for k, v in counters.items() if k[0] in ("bc", "bd")}
        keys = sorted({k for k, _ in sem_of.values()}, key=str)
        if os.environ.get("KSTATS"):
            print("SCHED ops:", {e: sum(1 for op in ops if op["eng"] == e and op["seg"] != "A") for e in engs},
                  "body:", body_cnt, "nsems", len(keys), "delta", delta, flush=True)
        sem_handles = {}
        with contextlib.ExitStack() as st:
            for k in keys:
                sem_handles[k] = st.enter_context(nc.semaphore("s_" + "_".join(str(x) for x in k)))
            pre_final = {}
            all_final = {}
            for i, op in enumerate(ops):
                if op["seg"] == "A":
                    continue
                k, v = sem_of[i]
                if op["seg"] == "pre":
                    pre_final[k] = max(pre_final.get(k, 0), v)
                if op["seg"] == "B":
                    v = (n_iter) * delta[k] + v
                all_final[k] = max(all_final.get(k, 0), v)
            owner = {}
            for i, op in enumerate(ops):
                if op["seg"] == "B":
                    owner[sem_of[i][0]] = op["eng"]

            def run(en, handle):
                mine = [i for i, op in enumerate(ops) if op["eng"] == en]
                waited = {}
                for i in mine:
                    op = ops[i]
                    if op["seg"] != "pre":
                        continue
                    need = {}
                    for d in op["deps"]:
                        dop = ops[d]
                        if dop["eng"] == "pe" and en == "pe" and not dop["dma"]:
                            continue
                        k, v = sem_of[d]
                        if waited.get(k, 0) >= v:
                            continue
                        need[k] = max(need.get(k, 0), v)
                    for k, v in need.items():
                        handle.wait_ge(sem_handles[k], v)
                        waited[k] = v
                    ins = op["fn"](handle)
                    k, v = sem_of[i]
                    ins.then_inc(sem_handles[k], 16 if op["dma"] else 1)
                for k, eo in owner.items():
                    if eo != en:
                        continue
                    rem = delta[k]
                    while rem > 0:
                        stp = min(rem, 240)
                        handle.sem_inc(sem_handles[k], stp)
                        rem -= stp
                for k, v in pre_final.items():
                    if waited.get(k, 0) < v:
                        handle.wait_ge(sem_handles[k], v)
                        waited[k] = v
                body = [i for i in mine if ops[i]["seg"] == "B"]
                if body:
                    rtmp = nc.alloc_register(handle.engine, "wtmp_" + en)
                    with handle.Fori(0, n_iter) as it:
                        self.loopvar[en] = it
                        w_same = {}
                        w_prev = {}
                        for i in body:
                            op = ops[i]
                            need_same = {}
                            need_prev = {}
                            for d in op["deps"]:
                                dop = ops[d]
                                if dop["seg"] == "pre":
                                    continue
                                if dop["eng"] == "pe" and en == "pe" and not dop["dma"]:
                                    continue
                                if dop["seg"] == "A":
                                    k, v = sem_of[a2b[d]]
                                    if w_same.get(k, 0) >= 1 or w_prev.get(k, 0) >= v:
                                        continue
                                    need_prev[k] = max(need_prev.get(k, 0), v)
                                else:
                                    k, v = sem_of[d]
                                    if w_same.get(k, 0) >= v:
                                        continue
                                    need_same[k] = max(need_same.get(k, 0), v)
                            for k, v in need_prev.items():
                                if k in need_same:
                                    continue
                                handle.reg_mul(rtmp, it, delta[k])
                                handle.reg_add(rtmp, rtmp, v)
                                handle.wait_ge(sem_handles[k], rtmp)
                                w_prev[k] = v
                            for k, v in need_same.items():
                                handle.reg_mul(rtmp, it, delta[k])
                                handle.reg_add(rtmp, rtmp, delta[k] + v)
                                handle.wait_ge(sem_handles[k], rtmp)
                                w_same[k] = v
                            ins = op["fn"](handle)
                            k, v = sem_of[i]
                            ins.then_inc(sem_handles[k], 16 if op["dma"] else 1)
                        self.loopvar[en] = None
                waited = dict(waited)
                for i in mine:
                    op = ops[i]
                    if op["seg"] != "post":
                        continue
                    need = {}
                    for d in op["deps"]:
                        dop = ops[d]
                        if dop["seg"] == "pre":
                            continue
                        if dop["eng"] == "pe" and en == "pe" and not dop["dma"]:
                            continue
                        k, v = sem_of[d]
                        if dop["seg"] == "B":
                            v = n_iter * delta[k] + v
                        if waited.get(k, 0) >= v:
                            continue
                        need[k] = max(need.get(k, 0), v)
                    for k, v in need.items():
                        handle.wait_ge(sem_handles[k], v)
                        waited[k] = v
                    ins = op["fn"](handle)
                    k, v = sem_of[i]
                    ins.then_inc(sem_handles[k], 16 if op["dma"] else 1)
                if en == final_eng:
                    for k, v in all_final.items():
                        if waited.get(k, 0) < v:
                            handle.wait_ge(sem_handles[k], v)

            with nc.Block() as block:
                @block.tensor
                def _(h):
                    run("pe", h)

                @block.scalar
                def _(h):
                    run("act", h)

                @block.vector
                def _(h):
                    run("dve", h)

                @block.gpsimd
                def _(h):
                    run("pool", h)

                @block.sync
                def _(h):
                    run("sp", h)


def make_consts():
    lg = np.log1p(-(2.0 ** (-5.0 - np.arange(4, dtype=np.float64))))
    pos = np.arange(128, dtype=np.float64)
    c = {}
    c["ident"] = np.eye(128, dtype=np.float32)
    diff = pos[None, :] - pos[:, None]
    dec = np.zeros((128, 4, 128), np.float64)
    for h in range(4):
        dec[:, h, :] = np.where(diff >= 0, np.exp(np.maximum(diff, 0) * lg[h]), 0.0) * (128 ** -0.5)
    c["decT"] = dec.astype(np.float32)
    gq = np.zeros((128, 4, 128), np.float64)
    for h in range(4):
        gq[:, h, :] = np.exp((pos + 1.0) * lg[h])[None, :]
    c["gq"] = gq.astype(np.float32)
    ks = np.zeros((128, 4), np.float64)
    for h in range(4):
        ks[:, h] = np.exp((127.0 - pos) * lg[h]) * (128 ** -0.5)
    c["kscale"] = ks.astype(np.float32)
    slopes = 2.0 ** (-8.0 * np.arange(1, 9, dtype=np.float64) / 8)
    i = np.arange(128)[:, None]
    j = np.arange(256)[None, :]
    dist = i + 128 - j
    valid = (dist >= 0) & (dist <= 128)
    b = np.zeros((128, 8, 256), np.float64)
    for h in range(8):
        b[:, h, :] = np.where(valid, -slopes[h] * dist, -1e30)
    c["biasR"] = b.astype(np.float32)
    bs = np.zeros((128, 132), np.float64)
    for p in range(128):
        h = p % 8
        bs[p, :129] = -slopes[h] * (128 - np.arange(129))
    c["biasS"] = bs.astype(np.float32)
    e16 = np.zeros((128, 16), np.float32)
    e16[:16, :] = np.eye(16, dtype=np.float32)
    c["eye16"] = e16
    c["eyeb"] = np.tile(np.eye(16, dtype=np.float32).reshape(1, 256), (128, 1))
    gam = np.exp(lg)
    c["gamma"] = [float(g) for g in gam]
    c["gamma128"] = [float(np.exp(128.0 * l)) for l in lg]
    return c


CONST_LAYOUT = [("ident", 128), ("decT", 512), ("gq", 512), ("kscale", 4), ("biasR", 2048),
                ("biasS", 132), ("eye16", 16), ("eyeb", 256)]


def pack_consts(c):
    cols = sum(n for _, n in CONST_LAYOUT)
    out = np.zeros((128, cols), np.float32)
    o = 0
    offs = {}
    for name, n in CONST_LAYOUT:
        out[:, o:o + n] = c[name].reshape(128, n)
        offs[name] = o
        o += n
    return out, offs


def build(T, L, with_sample=True):
    NT = T // TT
    consts = make_consts()
    cpack, coff = pack_consts(consts)
    NCC = cpack.shape[1]
    nc = bass.Bass("TRN2", target_bir_lowering=False)
    S = Sched()

    def din(name, shape, dt=F32):
        return nc.dram_tensor(name, list(shape), dt, kind="ExternalInput").ap()

    def dout(name, shape, dt=F32):
        return nc.dram_tensor(name, list(shape), dt, kind="ExternalOutput").ap()

    def dscr(name, shape, dt=BF16):
        return nc.dram_tensor(name, list(shape), dt).ap()

    xp = din("xp", [T, D])
    xs = din("xs", [NSAMP, D])
    NSLOT = 52
    SLOT_BASE = {"gu1": 0, "dn1": 11, "win": 17, "bra": 29, "brr": 31, "wo": 33, "gu2": 35, "dn2": 46}
    wall = din("wall", [L, NSLOT, 128, 4096])
    wall_bf = dscr("wall_bf", [L, NSLOT, 128, 4096])
    lng = din("lng", [L, 3, 128, D])
    lnb = din("lnb", [L, 3, 128, D])
    gng = din("gng", [L, 128, D])
    sinkb = din("sinkb", [L, 128, 8])
    sinkbh = din("sinkbh", [L, 128, 1])
    cst = din("cst", [128, NCC])
    bias0 = din("bias0", [NT * 128, 1024])
    ckr = din("ckr", [L, 128, 128, 64])
    cvr = din("cvr", [L, 128, 128, 64])
    ck = din("ck", [L, NSAMP, 128, 128])
    cv = din("cv", [L, NSAMP, 128, 128])
    stt = din("stt", [L, NSAMP, 4, 128, 256])
    yp = dout("yp", [T, D])
    ys = dout("ys", [NSAMP, D])
    nkp = dout("nkp", [L, 128, 128])
    nvp = dout("nvp", [L, 128, 128])
    nrp = dout("nrp", [L, 4, 128, 256])
    nks = dout("nks", [L, NSAMP, 128, 128])
    nvs = dout("nvs", [L, NSAMP, 128, 128])
    nrs = dout("nrs", [L, NSAMP, 4, 128, 256])
    zq = dscr("zq", [L, NSAMP, 512], F32)
    zk = dscr("zk", [L, NSAMP, 128], F32)
    zv = dscr("zv", [L, NSAMP, 128], F32)
    oscr = dscr("oscr", [L, 128, 64], F32)

    import contextlib
    es = contextlib.ExitStack()

    def sb(name, shape, dt=F32):
        return es.enter_context(nc.sbuf_tensor(name, list(shape), dt))

    def pst(name, shape, dt=F32):
        return es.enter_context(nc.psum_tensor(name, list(shape), dt))

    with es:
        cs = sb("cs", [128, NCC])
        identb = sb("identb", [128, 128], BF16)
        x32 = sb("x32", [128, NB, D])
        xT = sb("xT", [128, KD, TT], BF16)
        xb = sb("xb", [128, D], BF16)
        hbuf = sb("hbuf", [128, KF, TT], BF16)
        sabuf = [sb("sa%d" % i, [128, TT], BF16) for i in range(2)]
        NWS = 4
        wslot = [sb("wslot%d" % i, [128, KD, 512], BF16) for i in range(NWS)]
        lngs = sb("lngs", [128, D])
        lnbs = sb("lnbs", [128, D])
        gngs = lngs
        stats = sb("stats", [128, 4, 6])
        mv = sb("mv", [128, 4, 2])
        rstd = sb("rstd", [128, 4])
        gstats = sb("gstats", [128, 4, 6])
        gmv = sb("gmv", [128, 4, 2])
        grstd = sb("grstd", [128, 4])
        qz = sb("qz", [128, 2, 4, TT], BF16)
        kaT = sb("kaT", [128, 2, 128 + TT], BF16)
        qrT = sb("qrT", [128, 4, TT], BF16)
        krT = sb("krT", [128, 4, TT], BF16)
        gT = sb("gT", [128, 16, TT], BF16)
        vA = sb("vA", [128, NB + 1, 128], BF16)
        kdec = sb("kdec", [128, NB, 512], BF16)
        vR = sb("vR", [128, NB, D], BF16)
        gR = sb("gR", [128, NB, D], BF16)
        kv32 = sb("kv32", [128, 256])
        sc = sb("sc", [128, 8, 256])
        pb = sb("pb", [128, 8, 256], BF16)
        pT = sb("pT", [128, 16, 128], BF16)
        mx = sb("mx", [128, 8])
        negm = sb("negm", [128, 8])
        rsum = sb("rsum", [128, 8])
        esk = sb("esk", [128, 8])
        rden = sb("rden", [128, 8])
        sinks = sb("sinks", [128, 8])
        oab = sb("oab", [128, 512], BF16)
        oaT = sb("oaT", [128, 4, TT], BF16)
        scTb = sb("scTb", [128, 4, 128], BF16)
        qdec = sb("qdec", [128, 4, 128], BF16)
        S32 = sb("S32", [128, L, 4 * 256])
        Sb = sb("Sb", [128, L, 4 * 256], BF16)
        haloK = sb("haloK", [128, L, 2, 128], BF16)
        haloV = sb("haloV", [128, L, 128], BF16)
        orn = sb("orn", [128, D])
        orb = sb("orb", [128, D], BF16)
        orT = sb("orT", [128, KD, TT], BF16)
        mT = sb("mT", [128, KD, TT], BF16)
        tmpm = sb("tmpm", [128, TT])
        bias0s = sb("bias0s", [128, 1024])
        if with_sample:
            KQ = 16
            Kt = sb("Kt", [128, KQ, 64], BF16)
            Vt = sb("Vt", [128, KQ, 64], BF16)
            tmpS = sb("tmpS", [128, KQ, 64], BF16)
            qs = sb("qs", [128, 64])
            qsb = sb("qsb", [128, 64], BF16)
            kn = sb("kn", [128, 64])
            vn = sb("vn", [128, 64])
            scS = sb("scS", [128, 132])
            pS = sb("pS", [128, 132])
            pSb = sb("pSb", [128, 132], BF16)
            oS = sb("oS", [128, 64])
            oS2 = sb("oS2", [128, 64])
            smS = sb("smS", [128, 8])
            sinkS = sb("sinkS", [128, 1])
            zs32 = bias0s[0:NSAMP, 0:768]
            oa32 = sb("oa32", [NSAMP, 512])
            qr32 = sb("qr32", [NSAMP, 512])
            kr32 = sb("kr32", [NSAMP, 512])
            vr32 = sc[0:NSAMP, 0:4, :].rearrange("p a s -> p (a s)")
            osam = sc[0:NSAMP, 4:8, :].rearrange("p a s -> p (a s)")
            qsel = sb("qsel", [128, 4, NSAMP, NSAMP])
            qrT32 = sb("qrT32", [128, 4, NSAMP])
            vdiag = sb("vdiag", [NSAMP, 4, 256])
            S0 = S32[:, 0, :].rearrange("p (h v) -> p h v", h=4)
            S1 = S32[:, 1, :].rearrange("p (h v) -> p h v", h=4)
            zt = sb("zt", [128, NSAMP])
            qk = sb("qk", [NSAMP, 4])
            tmq = sb("tmq", [NSAMP, 512])
        ps = [pst("ps%d" % i, [128, 512]) for i in range(6)]
        pt = [pst("pt%d" % i, [128, 1024], BF16) for i in range(2)]

        def PS(i):
            return ("ps", i)

        def PT(i):
            return ("pt", i)

        def WS(i):
            return [("ws", i, 0), ("ws", i, 1)]

        NPC = 4
        SPP = NSLOT // NPC

        def precast_layer(l):
            for j in range(NPC):
                src = wall[l, j * SPP:(j + 1) * SPP].rearrange("s p c -> (s p) c")
                dst = wall_bf[l, j * SPP:(j + 1) * SPP].rearrange("s p c -> (s p) c")
                S.add("pool", (lambda e, s_=src, d_=dst: e.dma_start(out=d_, in_=s_)), writes=[("wb", l, j)], dma=True)

        S.add("sp", lambda e: e.dma_start(out=cs[:], in_=cst), writes=["cs"], dma=True)
        S.add("pool", lambda e: e.dma_start(out=identb[:], in_=cst[:, coff["ident"]:coff["ident"] + 128]),
              writes=["identb"], dma=True)
        STAGE = int(os.environ.get("KSTAGE", "99"))
        SUB = int(os.environ.get("KSUB", "99"))
        KATT = int(os.environ.get("KATT", "99"))
        for l in range(L):
            precast_layer(l)

        decT = cs[:, coff["decT"]:coff["decT"] + 512].rearrange("p (h i) -> p h i", h=4)
        gqc = cs[:, coff["gq"]:coff["gq"] + 512].rearrange("p (h i) -> p h i", h=4)
        kscale = cs[:, coff["kscale"]:coff["kscale"] + 4]
        biasR = cs[:, coff["biasR"]:coff["biasR"] + 2048].rearrange("p (h s) -> p h s", h=8)
        biasS = cs[:, coff["biasS"]:coff["biasS"] + 132]
        eyeb = cs[:, coff["eyeb"]:coff["eyeb"] + 256].rearrange("p (a b) -> p a b", a=NSAMP)

        S.add("pool", lambda e: e.memset(S32[:], 0.0), writes=[("S32", l) for l in range(L)])
        S.add("pool", lambda e: e.memset(Sb[:], 0.0), writes=[("Sb", l) for l in range(L)])
        S.add("pool", lambda e: e.memset(haloK[:], 0.0), writes=[("haloK", l) for l in range(L)])
        S.add("pool", lambda e: e.memset(haloV[:], 0.0), writes=[("haloV", l) for l in range(L)])
        S.add("pool", lambda e: e.memset(qz[:], 0.0), writes=[("qaT", i) for i in range(4)] + [("qaT", i, 1) for i in range(4)])

        if with_sample:
            S.add("pool", lambda e: e.memset(zt[:], 0.0), writes=["zt"])
        rr = {"ws": 0, "ps_a": 0, "ps_b": 0, "pt": 0}

        def load_slot(name, l, sidx):
            i = rr["ws"] % NWS
            rr["ws"] += 1
            sl = SLOT_BASE[name] + sidx
            src = wall_bf[l, sl].rearrange("p (k c) -> p k c", k=KD)
            S.add("sp", lambda e: e.dma_start(out=wslot[i][:, :, :], in_=src), reads=[("wb", l, sl // SPP)],
                  writes=WS(i), dma=True)
            return i

        def transpose_to(src_bf, np_, nk, dst_fn, src_res, dst_res):
            ti = rr["pt"] % 2
            rr["pt"] += 1

            def tr(e):
                ins = None
                for k in range(nk):
                    ins = e.transpose(out=pt[ti][:, k * 128:k * 128 + np_], in_=src_bf[0:np_, k * 128:(k + 1) * 128],
                                      identity=identb[0:np_, 0:np_])
                return ins
            S.add("pe", tr, reads=src_res + ["identb"], writes=[PT(ti)])
            S.add("act", lambda e: e.copy(
                out=dst_fn(), in_=pt[ti][:, 0:nk * 128].rearrange("p (k t) -> p k t", k=nk)[:, :, 0:np_]),
                reads=[PT(ti)], writes=dst_res)

        def layer_norm_blocks(nb, np_, l, which, last_out=None):
            eps = LN_EPS / (ALPHA * ALPHA)
            S.add("sp", lambda e: e.dma_start(out=lngs[:], in_=lng[l, which]), writes=["lng"], dma=True)
            S.add("sp", lambda e: e.dma_start(out=lnbs[:], in_=lnb[l, which]), writes=["lnb"], dma=True)
            for b in range(nb):
                xr = x32[0:np_, b, :]
                S.add("dve", lambda e, xr=xr: e.bn_stats(out=stats[0:np_, 0, :], in_=xr[:, 0:512]),
                      reads=[("x32", b)], writes=["stats"])
                S.add("dve", lambda e, xr=xr: e.bn_stats(out=stats[0:np_, 1, :], in_=xr[:, 512:1024]),
                      reads=[("x32", b)], writes=["stats1"])
                S.add("dve", lambda e: e.bn_aggr(out=mv[0:np_, 0, :],
                                                 in_=stats[0:np_, 0:2, :].rearrange("p a s -> p (a s)")),
                      reads=["stats", "stats1"], writes=["mv"])
                S.add("dve", lambda e: e.tensor_scalar(out=rstd[0:np_, 0:1], in0=mv[0:np_, 0, 1:2], scalar1=eps,
                                                       scalar2=None, op0=ALU.add),
                      reads=["mv"], writes=["rstd"])
                S.add("act", lambda e: e.activation(out=rstd[0:np_, 0:1], in_=rstd[0:np_, 0:1], func=AF.Sqrt),
                      reads=["rstd"], writes=["rstd"])
                S.add("dve", lambda e: e.reciprocal(out=rstd[0:np_, 0:1], in_=rstd[0:np_, 0:1]),
                      reads=["rstd"], writes=["rstd"])
                S.add("dve", lambda e, xr=xr: e.tensor_scalar(out=xr, in0=xr, scalar1=mv[0:np_, 0, 0:1],
                                                              scalar2=rstd[0:np_, 0:1], op0=ALU.subtract, op1=ALU.mult),
                      reads=[("x32", b), "mv", "rstd"], writes=[("x32", b)])
                S.add("pool", lambda e, xr=xr: e.tensor_tensor(out=xr, in0=xr, in1=lngs[0:np_, :], op=ALU.mult),
                      reads=[("x32", b), "lng"], writes=[("x32", b)])
                S.add("pool", lambda e, xr=xr: e.tensor_tensor(out=xr, in0=xr, in1=lnbs[0:np_, :], op=ALU.add),
                      reads=[("x32", b), "lnb"], writes=[("x32", b)])
                if last_out is not None:
                    S.add("sp", lambda e, xr=xr, b=b: e.dma_start(out=last_out(b), in_=xr), reads=[("x32", b)],
                          writes=[("yout", b)], dma=True)
                S.add("act", lambda e, xr=xr: e.copy(out=xb[0:np_, :], in_=xr), reads=[("x32", b)], writes=["xb"])
                transpose_to(xb, np_, KD, (lambda b=b: xT[:, :, b * 128:b * 128 + np_]), ["xb"], [("xT", b)])

        def ffn(nb, np_, l, which, last_out=None):
            ntok = (nb - 1) * 128 + np_
            gname = "gu1" if which == 0 else "gu2"
            dname = "dn1" if which == 0 else "dn2"
            xTr = [("xT", b) for b in range(nb)]
            for fp in range(KF // 2):
                i = load_slot(gname, l, fp)
                for j in range(2):
                    f = fp * 2 + j
                    par = f % 2
                    pa, pu = ps[2 * par], ps[2 * par + 1]

                    def mm(e, i=i, j=j, pa=pa, pu=pu):
                        ins = None
                        for k in range(KD):
                            ins = e.matmul(pa[:, 0:ntok], lhsT=wslot[i][:, k, j * 128:(j + 1) * 128],
                                           rhs=xT[:, k, 0:ntok], start=(k == 0), stop=(k == KD - 1))
                        for k in range(KD):
                            ins = e.matmul(pu[:, 0:ntok], lhsT=wslot[i][:, k, 256 + j * 128:256 + (j + 1) * 128],
                                           rhs=xT[:, k, 0:ntok], start=(k == 0), stop=(k == KD - 1))
                        return ins
                    S.add("pe", mm, reads=WS(i) + xTr, writes=[PS(2 * par), PS(2 * par + 1)])
                    S.add("act", lambda e, pa=pa, par=par: e.activation(out=sabuf[par][:, 0:ntok], in_=pa[:, 0:ntok],
                                                                        func=AF.Silu),
                          reads=[PS(2 * par)], writes=[("sa", par)])
                    S.add("dve", lambda e, pu=pu, par=par, f=f: e.tensor_tensor(
                        out=hbuf[:, f, 0:ntok], in0=pu[:, 0:ntok], in1=sabuf[par][:, 0:ntok], op=ALU.mult),
                        reads=[PS(2 * par + 1), ("sa", par)], writes=[("h", f)])
            pieces = [(0, 8), (8, 16), (16, 22)]
            for half in range(2):
                for pc, (f0, f1) in enumerate(pieces):
                    si = load_slot(dname, l, half * 3 + pc)
                    for b in range(nb):
                        npb = 128 if b < nb - 1 else np_
                        pi = 4 + b

                        def mm2(e, si=si, b=b, npb=npb, pi=pi, f0=f0, f1=f1):
                            ins = None
                            for f in range(f0, f1):
                                ins = e.matmul(ps[pi][0:npb, :], lhsT=hbuf[:, f, b * 128:b * 128 + npb],
                                               rhs=wslot[si][:, f - f0, :], start=(f == 0), stop=(f == KF - 1))
                            return ins
                        S.add("pe", mm2, reads=[("h", f) for f in range(f0, f1)] + WS(si), writes=[PS(pi)])
                for b in range(nb):
                    npb = 128 if b < nb - 1 else np_
                    pi = 4 + b
                    xr = x32[0:npb, b, half * 512:(half + 1) * 512]
                    S.add("dve", lambda e, xr=xr, pi=pi, npb=npb: e.scalar_tensor_tensor(
                        out=xr, in0=ps[pi][0:npb, :], scalar=0.5 / ALPHA, in1=xr, op0=ALU.mult, op1=ALU.add),
                        reads=[PS(pi), ("x32", b)], writes=[("x32", b)])
            layer_norm_blocks(nb, np_, l, 0 if which == 0 else 2, last_out=last_out)

        VR_ATOMS = [0, 256, 768]

        def vr_atoms(name, b):
            return [(name, b, o) for o in VR_ATOMS]

        def mix_proj(nb, np_, l, sample):
            ntok = (nb - 1) * 128 + np_
            xTr = [("xT", b) for b in range(nb)]
            if sample:
                fm = [(C_QR + 128 * i, "qr", i) for i in range(4)] + [(C_GA + 128 * i, "g", i) for i in range(16)]
                tm = [(0, 768, "zs"), (C_QR, C_KR, "qr32"), (C_KR, C_VR, "kr32"), (C_VR, C_GR, "vr"), (C_GR, C_GA, "gr")]
            else:
                fm = [(C_QA + 128 * i, "qa", i) for i in range(4)] + [(C_KA, "ka", 0), (C_KAS, "ka", 1)] + \
                     [(C_QR + 128 * i, "qr", i) for i in range(4)] + [(C_KR + 128 * i, "kr", i) for i in range(4)] + \
                     [(C_GA + 128 * i, "g", i) for i in range(16)]
                tm = [(C_VA, C_QR, "va"), (C_KR, C_VR, "kr"), (C_VR, C_GR, "vr"), (C_GR, C_GA, "gr"), (C_KA, C_VA, "ka32")]
            KP = os.environ.get("KPROJ", "")
            if KP:
                fm = [x for x in fm if x[1] in KP.split(",")]
                tm = [x for x in tm if x[2] in KP.split(",")]
            for g in range(12):
                c0 = g * 512
                ncols = min(512, INC2 - c0)
                si = load_slot("win", l, g)
                for (cc, kind, idx) in fm:
                    if not (c0 <= cc < c0 + ncols):
                        continue
                    pi = rr["ps_a"] % 4
                    rr["ps_a"] += 1
                    lo = cc - c0

                    def mm(e, si=si, lo=lo, pi=pi):
                        ins = None
                        for k in range(KD):
                            ins = e.matmul(ps[pi][:, 0:ntok], lhsT=wslot[si][:, k, lo:lo + 128], rhs=xT[:, k, 0:ntok],
                                           start=(k == 0), stop=(k == KD - 1))
                        return ins
                    S.add("pe", mm, reads=WS(si) + xTr, writes=[PS(pi)])
                    src = ps[pi][:, 0:ntok]
                    if kind == "qa":
                        S.add("act", lambda e, pi=pi, idx=idx: e.copy(out=qz[0:64, 0, idx, 0:ntok],
                                                                       in_=ps[pi][0:64, 0:ntok]),
                              reads=[PS(pi)], writes=[("qaT", idx)])
                        S.add("act", lambda e, pi=pi, idx=idx: e.copy(out=qz[64:128, 1, idx, 0:ntok],
                                                                       in_=ps[pi][64:128, 0:ntok]),
                              reads=[PS(pi)], writes=[("qaT", idx, 1)])
                    elif kind == "ka":
                        S.add("act", lambda e, src=src, idx=idx: e.copy(out=kaT[:, idx, 128:128 + ntok], in_=src),
                              reads=[PS(pi)], writes=[("kaT", idx)])
                    elif kind == "qr":
                        if sample:
                            S.add("act", lambda e, src=src, idx=idx: e.copy(out=qrT32[:, idx, :], in_=src),
                                  reads=[PS(pi)], writes=[("qrT32", idx)])
                        else:
                            S.add("act", lambda e, src=src, idx=idx: e.copy(out=qrT[:, idx, 0:ntok], in_=src),
                                  reads=[PS(pi)], writes=[("qrT", idx)])
                    elif kind == "kr":
                        S.add("act", lambda e, src=src, idx=idx: e.copy(out=krT[:, idx, 0:ntok], in_=src),
                              reads=[PS(pi)], writes=[("krT", idx)])
                    else:
                        S.add("act", lambda e, src=src, idx=idx: e.activation(out=gT[:, idx, 0:ntok], in_=src,
                                                                              func=AF.Sigmoid),
                              reads=[PS(pi)], writes=[("gT", idx)])
                for (a0, a1, kind) in tm:
                    lo_c = max(a0, c0)
                    hi_c = min(a1, c0 + ncols)
                    if lo_c >= hi_c:
                        continue
                    w = hi_c - lo_c
                    for b in range(nb):
                        npb = 128 if b < nb - 1 else np_
                        if kind == "ka32" and not (b == nb - 1):
                            continue
                        pi = 4 + (rr["ps_b"] % 2)
                        rr["ps_b"] += 1

                        def mm(e, si=si, lo=lo_c - c0, w=w, b=b, npb=npb, pi=pi):
                            ins = None
                            for k in range(KD):
                                ins = e.matmul(ps[pi][0:npb, 0:w], lhsT=xT[:, k, b * 128:b * 128 + npb],
                                               rhs=wslot[si][:, k, lo:lo + w], start=(k == 0), stop=(k == KD - 1))
                            return ins
                        S.add("pe", mm, reads=WS(si) + [("xT", b)], writes=[PS(pi)])
                        src = ps[pi][0:npb, 0:w]
                        o = lo_c - a0
                        if kind == "va":
                            S.add("act", lambda e, src=src, b=b: e.copy(out=vA[:, b + 1, :], in_=src),
                                  reads=[PS(pi)], writes=[("vA", b + 1)])
                            if b == nb - 1:
                                S.add("dve", lambda e, src=src: e.tensor_copy(out=kv32[:, 128:256], in_=src),
                                      reads=[PS(pi)], writes=["kv32v"])
                        elif kind == "ka32":
                            S.add("dve", lambda e, src=src: e.tensor_copy(out=kv32[:, 0:128], in_=src),
                                  reads=[PS(pi)], writes=["kv32k"])
                        elif kind == "kr":
                            for hh in range(w // 128):
                                h = (o + hh * 128) // 128
                                S.add("dve", lambda e, pi=pi, hh=hh, h=h, b=b: e.tensor_scalar(
                                    out=kdec[:, b, h * 128:(h + 1) * 128], in0=ps[pi][:, hh * 128:(hh + 1) * 128],
                                    scalar1=kscale[:, h:h + 1], scalar2=None, op0=ALU.mult),
                                    reads=[PS(pi), "cs"], writes=[("kdec", b, h)])
                        elif kind == "vr":
                            if sample:
                                S.add("act", lambda e, src=src, o=o, w=w: e.copy(out=vr32[:, o:o + w], in_=src),
                                      reads=[PS(pi)], writes=[("vr32", o)])
                            else:
                                S.add("act", lambda e, src=src, o=o, w=w, b=b: e.copy(out=vR[:, b, o:o + w], in_=src),
                                      reads=[PS(pi)], writes=[("vR", b, o)])
                        elif kind == "gr":
                            S.add("act", lambda e, src=src, o=o, w=w, b=b, npb=npb: e.activation(
                                out=gR[0:npb, b, o:o + w], in_=src, func=AF.Silu),
                                reads=[PS(pi)], writes=[("gR", b, o)])
                        elif kind == "zs":
                            S.add("act", lambda e, src=src, o=o, w=w: e.copy(out=zs32[:, o:o + w], in_=src),
                                  reads=[PS(pi)], writes=[("zs32", o)])
                        elif kind == "qr32":
                            S.add("act", lambda e, src=src, o=o, w=w: e.copy(out=qr32[:, o:o + w], in_=src),
                                  reads=[PS(pi)], writes=[("qr32", o)])
                        elif kind == "kr32":
                            S.add("act", lambda e, src=src, o=o, w=w: e.copy(out=kr32[:, o:o + w], in_=src),
                                  reads=[PS(pi)], writes=[("kr32", o)])

        def mix_out(nb, np_, l):
            ntok = (nb - 1) * 128 + np_
            oaTr = [("oaT", b) for b in range(nb)]
            orTr = [("orT", b) for b in range(nb)]
            for half in range(2):
                sa_ = load_slot("bra", l, half)
                sr_ = load_slot("brr", l, half)
                for mm_ in range(4):
                    m = half * 4 + mm_
                    par = m % 2
                    pa, pr = ps[2 * par], ps[2 * par + 1]

                    def mm(e, mm_=mm_, pa=pa, pr=pr, sa_=sa_, sr_=sr_):
                        ins = None
                        for k in range(4):
                            ins = e.matmul(pa[:, 0:ntok], lhsT=wslot[sa_][:, k, mm_ * 128:(mm_ + 1) * 128],
                                           rhs=oaT[:, k, 0:ntok], start=(k == 0), stop=(k == 3))
                        for k in range(KD):
                            ins = e.matmul(pr[:, 0:ntok], lhsT=wslot[sr_][:, k, mm_ * 128:(mm_ + 1) * 128],
                                           rhs=orT[:, k, 0:ntok], start=(k == 0), stop=(k == KD - 1))
                        return ins
                    S.add("pe", mm, reads=WS(sa_) + WS(sr_) + oaTr + orTr, writes=[PS(2 * par), PS(2 * par + 1)])
                    S.add("dve", lambda e, pa=pa, m=m: e.tensor_tensor(out=tmpm[:, 0:ntok], in0=pa[:, 0:ntok],
                                                                       in1=gT[:, m, 0:ntok], op=ALU.mult),
                          reads=[PS(2 * par), ("gT", m)], writes=["tmpm"])
                    S.add("dve", lambda e, pr=pr, m=m: e.tensor_tensor(out=mT[:, m, 0:ntok], in0=pr[:, 0:ntok],
                                                                       in1=gT[:, 8 + m, 0:ntok], op=ALU.mult),
                          reads=[PS(2 * par + 1), ("gT", 8 + m)], writes=[("mT", m)])
                    S.add("pool", lambda e, m=m: e.tensor_tensor(out=mT[:, m, 0:ntok], in0=mT[:, m, 0:ntok],
                                                                 in1=tmpm[:, 0:ntok], op=ALU.add),
                          reads=["tmpm", ("mT", m)], writes=[("mT", m)])
            for half in range(2):
                si = load_slot("wo", l, half)
                for b in range(nb):
                    npb = 128 if b < nb - 1 else np_
                    pi = 4 + (rr["ps_b"] % 2)
                    rr["ps_b"] += 1

                    def mm2(e, si=si, b=b, npb=npb, pi=pi):
                        ins = None
                        for k in range(KD):
                            ins = e.matmul(ps[pi][0:npb, :], lhsT=mT[:, k, b * 128:b * 128 + npb], rhs=wslot[si][:, k, :],
                                           start=(k == 0), stop=(k == KD - 1))
                        return ins
                    S.add("pe", mm2, reads=[("mT", m) for m in range(KD)] + WS(si), writes=[PS(pi)])
                    xr = x32[0:npb, b, half * 512:(half + 1) * 512]
                    S.add("dve", lambda e, xr=xr, pi=pi, npb=npb: e.scalar_tensor_tensor(
                        out=xr, in0=ps[pi][0:npb, :], scalar=1.0 / ALPHA, in1=xr, op0=ALU.mult, op1=ALU.add),
                        reads=[PS(pi), ("x32", b)], writes=[("x32", b)])
            layer_norm_blocks(nb, np_, l, 1)

        def gn_block(np_, l, b, src2_fn, src_fn, src_res):
            for h in range(4):
                S.add("dve", lambda e, h=h: e.bn_stats(out=gstats[0:np_, h, :], in_=src_fn(h)),
                      reads=src_res, writes=[("gst", h // 2)] if h % 2 else [("gstx", h // 2)])
            for h in range(4):
                S.add("dve", lambda e, h=h: e.bn_aggr(out=gmv[0:np_, h, :], in_=gstats[0:np_, h, :]),
                      reads=[("gst", 0), ("gst", 1), ("gstx", 0), ("gstx", 1)], writes=[("mvh", h)])
            S.add("dve", lambda e: e.tensor_scalar(out=grstd[0:np_, 0:4], in0=gmv[0:np_, :, 1], scalar1=GN_EPS,
                                                   scalar2=None, op0=ALU.add),
                  reads=[("mvh", h) for h in range(4)], writes=["grstd"])
            S.add("act", lambda e: e.activation(out=grstd[0:np_, 0:4], in_=grstd[0:np_, 0:4], func=AF.Sqrt),
                  reads=["grstd"], writes=["grstd"])
            S.add("dve", lambda e: e.reciprocal(out=grstd[0:np_, 0:4], in_=grstd[0:np_, 0:4]),
                  reads=["grstd"], writes=["grstd"])
            for h in range(4):
                S.add("dve", lambda e, h=h: e.tensor_scalar(out=orn[0:np_, h * 256:(h + 1) * 256], in0=src_fn(h),
                                                            scalar1=gmv[0:np_, h, 0:1], scalar2=grstd[0:np_, h:h + 1],
                                                            op0=ALU.subtract, op1=ALU.mult),
                      reads=src_res + [("mvh", h), "grstd"], writes=[("orn", h)])
            S.add("pool", lambda e: e.tensor_tensor(out=orn[0:np_, :], in0=orn[0:np_, :], in1=gngs[0:np_, :], op=ALU.mult),
                  reads=[("orn", h) for h in range(4)] + ["lng"], writes=[("orn", h) for h in range(4)])
            S.add("pool", lambda e: e.tensor_tensor(out=orb[0:np_, :], in0=orn[0:np_, :], in1=gR[0:np_, b, :], op=ALU.mult),
                  reads=[("orn", h) for h in range(4)] + vr_atoms("gR", b), writes=["orb"])
            transpose_to(orb, np_, KD, (lambda: orT[:, :, b * 128:b * 128 + np_]), ["orb"], [("orT", b)])

        def mix_prompt(l, tile_idx, is_last_tile):
            nb = NB
            S.add("pool", lambda e: e.tensor_copy(out=kaT[:, :, 0:128], in_=haloK[:, l, :, :]), reads=[("haloK", l)],
                  writes=["kaTh"])
            S.add("pool", lambda e: e.tensor_copy(out=vA[:, 0, :], in_=haloV[:, l, :]), reads=[("haloV", l)],
                  writes=[("vA", 0)])
            S.add("sp", lambda e: e.dma_start(out=sinks[:], in_=sinkb[l]), writes=["sinks"], dma=True)
            S.add("sp", lambda e: e.dma_start(out=gngs[:], in_=gng[l]), writes=["lng"], dma=True)
            mix_proj(nb, 128, l, False)
            if SUB < 2:
                return
            S.add("pool", lambda e: e.tensor_copy(out=haloK[:, l, :, :], in_=kaT[:, :, TT:TT + 128]),
                  reads=[("kaT", 0), ("kaT", 1)], writes=[("haloK", l)])
            S.add("pool", lambda e: e.tensor_copy(out=haloV[:, l, :], in_=vA[:, NB, :]), reads=[("vA", NB)],
                  writes=[("haloV", l)])
            if is_last_tile:
                S.add("sp", lambda e: e.dma_start(out=nkp[l], in_=kv32[:, 0:128]), reads=["kv32k"], writes=[("nkp", l)],
                      dma=True)
                S.add("sp", lambda e: e.dma_start(out=nvp[l], in_=kv32[:, 128:256]), reads=["kv32v"], writes=[("nvp", l)],
                      dma=True)
            def blk(b, first, s0, ns, kcol0, nsb):
                if KATT < 1:
                    return
                for hp in range(4):
                    def mm(e, hp=hp, b=b):
                        ins = None
                        for j in range(2):
                            h = 2 * hp + j
                            kv = h // 4
                            which = 0 if kv == j else 1
                            ins = e.matmul(ps[hp][:, j * 256 + s0:(j + 1) * 256],
                                           lhsT=qz[:, j, hp, b * 128:(b + 1) * 128],
                                           rhs=kaT[:, which, kcol0:kcol0 + ns], start=True, stop=True)
                        return ins
                    S.add("pe", mm, reads=[("qaT", hp), ("qaT", hp, 1), ("kaT", 0), ("kaT", 1), "kaTh"], writes=[PS(hp)])
                    if b == 0:
                        b0v = bias0s[:, :].rearrange("p (h s) -> p h s", h=8)
                        S.add("dve", lambda e, hp=hp: e.scalar_tensor_tensor(
                            out=sc[:, 2 * hp:2 * hp + 2, 0:128],
                            in0=ps[hp][:, :].rearrange("p (j s) -> p j s", j=2)[:, :, 0:128], scalar=0.125,
                            in1=b0v[:, 2 * hp:2 * hp + 2, :], op0=ALU.mult, op1=ALU.add),
                            reads=[PS(hp), "bias0s"], writes=[("sc", hp, 0)])
                        S.add("dve", lambda e, hp=hp: e.scalar_tensor_tensor(
                            out=sc[:, 2 * hp:2 * hp + 2, 128:256],
                            in0=ps[hp][:, :].rearrange("p (j s) -> p j s", j=2)[:, :, 128:256], scalar=0.125,
                            in1=biasR[:, 2 * hp:2 * hp + 2, 128:256], op0=ALU.mult, op1=ALU.add),
                            reads=[PS(hp), "cs"], writes=[("sc", hp)])
                    else:
                        S.add("dve", lambda e, hp=hp: e.scalar_tensor_tensor(
                            out=sc[:, 2 * hp:2 * hp + 2, s0:256],
                            in0=ps[hp][:, :].rearrange("p (j s) -> p j s", j=2)[:, :, s0:256], scalar=0.125,
                            in1=biasR[:, 2 * hp:2 * hp + 2, s0:256], op0=ALU.mult, op1=ALU.add),
                            reads=[PS(hp), "cs"], writes=[("sc", hp), ("sc", hp, 0)])
                scr = [("sc", hp) for hp in range(4)] + [("sc", hp, 0) for hp in range(4)]
                if KATT < 2:
                    return
                S.add("dve", lambda e: e.tensor_reduce(out=mx[:], in_=sc[:, :, s0:256], axis=AX.X, op=ALU.max),
                      reads=scr, writes=["mx"])
                S.add("dve", lambda e: e.tensor_tensor(out=mx[:], in0=mx[:], in1=sinks[:], op=ALU.max),
                      reads=["mx", "sinks"], writes=["mx"])
                S.add("dve", lambda e: e.tensor_scalar(out=negm[:], in0=mx[:], scalar1=-1.0, scalar2=None, op0=ALU.mult),
                      reads=["mx"], writes=["negm"])
                S.add("dve", lambda e: e.tensor_tensor(out=esk[:], in0=sinks[:], in1=mx[:], op=ALU.subtract),
                      reads=["mx", "sinks"], writes=["esk"])
                S.add("act", lambda e: e.activation(out=esk[:], in_=esk[:], func=AF.Exp), reads=["esk"], writes=["esk"])
                if KATT < 3:
                    return
                for h in range(8):
                    S.add("act", lambda e, h=h: e.activation(out=pb[:, h, s0:256], in_=sc[:, h, s0:256], func=AF.Exp,
                                                             bias=negm[:, h:h + 1], scale=1.0),
                          reads=scr + ["negm"], writes=[("pb", h)])
                if KATT < 4:
                    return
                S.add("dve", lambda e: e.tensor_reduce(out=rsum[:], in_=pb[:, :, s0:256], axis=AX.X, op=ALU.add),
                      reads=[("pb", h) for h in range(8)], writes=["rsum"])
                S.add("dve", lambda e: e.tensor_tensor(out=rden[:], in0=rsum[:], in1=esk[:], op=ALU.add),
                      reads=["rsum", "esk"], writes=["rden"])
                S.add("dve", lambda e: e.reciprocal(out=rden[:], in_=rden[:]), reads=["rden"], writes=["rden"])
                if SUB < 3:
                    return
                for half in range(2):
                    ti = rr["pt"] % 2
                    rr["pt"] += 1

                    def tr(e, half=half, ti=ti):
                        ins = None
                        for hh in range(4):
                            h = half * 4 + hh
                            for sbk in range(nsb):
                                c0 = s0 + sbk * 128
                                ins = e.transpose(out=pt[ti][:, (hh * 2 + sbk) * 128:(hh * 2 + sbk + 1) * 128],
                                                  in_=pb[:, h, c0:c0 + 128], identity=identb[:])
                        return ins
                    S.add("pe", tr, reads=[("pb", half * 4 + hh) for hh in range(4)] + ["identb"], writes=[PT(ti)])
                    S.add("act", lambda e, half=half, ti=ti: e.copy(out=pT[:, half * 8:(half + 1) * 8, :],
                                                                    in_=pt[ti][:, :].rearrange("p (a t) -> p a t", a=8)),
                          reads=[PT(ti)], writes=[("pT", half)])

                def pv(e, b=b):
                    ins = None
                    for h in range(8):
                        kv = h // 4
                        for sbk in range(nsb):
                            vb = b + sbk + (1 if first else 0)
                            ins = e.matmul(ps[4][:, h * 64:(h + 1) * 64], lhsT=pT[:, h * 2 + sbk, :],
                                           rhs=vA[:, vb, kv * 64:(kv + 1) * 64], start=(sbk == 0), stop=(sbk == nsb - 1))
                    return ins
                S.add("pe", pv, reads=[("pT", 0), ("pT", 1), ("vA", b), ("vA", b + 1)], writes=[PS(4)])
                S.add("dve", lambda e: e.tensor_tensor(
                    out=oab[:, :].rearrange("p (h d) -> p h d", h=8),
                    in0=ps[4][:, :].rearrange("p (h d) -> p h d", h=8),
                    in1=rden[:, :].unsqueeze(2).broadcast_to([128, 8, 64]), op=ALU.mult),
                    reads=[PS(4), "rden"], writes=["oab"])
                transpose_to(oab, 128, 4, (lambda b=b: oaT[:, :, b * 128:(b + 1) * 128]), ["oab"], [("oaT", b)])

                if SUB < 4:
                    return

                def mmsc(e, b=b):
                    ins = None
                    for h in range(4):
                        ins = e.matmul(ps[5][:, h * 128:(h + 1) * 128], lhsT=krT[:, h, b * 128:(b + 1) * 128],
                                       rhs=qrT[:, h, b * 128:(b + 1) * 128], start=True, stop=True)
                    return ins
                S.add("pe", mmsc, reads=[("krT", h) for h in range(4)] + [("qrT", h) for h in range(4)], writes=[PS(5)])
                S.add("dve", lambda e: e.tensor_tensor(out=scTb[:], in0=ps[5][:, :].rearrange("p (h i) -> p h i", h=4),
                                                       in1=decT, op=ALU.mult), reads=[PS(5), "cs"], writes=["scTb"])
                S.add("pool", lambda e, b=b: e.tensor_tensor(out=qdec[:], in0=qrT[:, :, b * 128:(b + 1) * 128], in1=gqc,
                                                             op=ALU.mult),
                      reads=[("qrT", h) for h in range(4)] + ["cs"], writes=["qdec"])

                def mmo(e, b=b):
                    ins = None
                    for h in range(4):
                        dst = ps[h // 2][:, (h % 2) * 256:(h % 2 + 1) * 256]
                        e.matmul(dst, lhsT=scTb[:, h, :], rhs=vR[:, b, h * 256:(h + 1) * 256], start=True, stop=False)
                        ins = e.matmul(dst, lhsT=qdec[:, h, :], rhs=Sb[:, l, h * 256:(h + 1) * 256], start=False, stop=True)
                    return ins
                S.add("pe", mmo, reads=["scTb", "qdec", ("Sb", l)] + vr_atoms("vR", b), writes=[PS(0), PS(1)])

                def mmu(e, b=b):
                    ins = None
                    for h in range(4):
                        dst = ps[2 + h // 2][:, (h % 2) * 256:(h % 2 + 1) * 256]
                        ins = e.matmul(dst, lhsT=kdec[:, b, h * 128:(h + 1) * 128], rhs=vR[:, b, h * 256:(h + 1) * 256],
                                       start=True, stop=True)
                    return ins
                S.add("pe", mmu, reads=[("kdec", b, h) for h in range(4)] + vr_atoms("vR", b), writes=[PS(2), PS(3)])
                for h in range(4):
                    S.add("dve", lambda e, h=h: e.scalar_tensor_tensor(
                        out=S32[:, l, h * 256:(h + 1) * 256], in0=S32[:, l, h * 256:(h + 1) * 256],
                        scalar=consts["gamma128"][h], in1=ps[2 + h // 2][:, (h % 2) * 256:(h % 2 + 1) * 256],
                        op0=ALU.mult, op1=ALU.add), reads=[("S32", l), PS(2 + h // 2)], writes=[("S32", l)])
                S.add("act", lambda e: e.copy(out=Sb[:, l, :], in_=S32[:, l, :]), reads=[("S32", l)], writes=[("Sb", l)])
                if SUB < 5:
                    return
                gn_block(128, l, b, lambda j: ps[j][:, :].rearrange("p (a v) -> p a v", a=2),
                         lambda h: ps[h // 2][:, (h % 2) * 256:(h % 2 + 1) * 256], [PS(0), PS(1)])

            for b in range(nb):
                first = (tile_idx == 0 and b == 0)
                s0 = 128 if first else 0
                blk(b, first, s0, 256 - s0, b * 128 + s0, (256 - s0) // 128)
            if is_last_tile:
                S.add("sp", lambda e: e.dma_start(out=nrp[l].rearrange("h p v -> p h v"),
                                                  in_=S32[:, l, :].rearrange("p (h v) -> p h v", h=4)),
                      reads=[("S32", l)], writes=[("nrp", l)], dma=True)
            if SUB < 6:
                return
            mix_out(nb, 128, l)

        def mix_sample(l):
            np_ = NSAMP
            S.add("sp", lambda e: e.dma_start(out=gngs[:], in_=gng[l]), writes=["lng"], dma=True)
            S.add("sp", lambda e: e.dma_start(out=sinkS[:], in_=sinkbh[l]), writes=["sinkS"], dma=True)
            mix_proj(1, np_, l, True)
            zsr = [("zs32", 0), ("zs32", 512)]
            S.add("sp", lambda e: e.dma_start(out=zq[l], in_=zs32[:, 0:512]), reads=zsr, writes=[("zq", l)], dma=True)
            S.add("sp", lambda e: e.dma_start(out=zk[l], in_=zs32[:, 512:640]), reads=zsr, writes=[("zk", l)], dma=True)
            S.add("sp", lambda e: e.dma_start(out=zv[l], in_=zs32[:, 640:768]), reads=zsr, writes=[("zv", l)], dma=True)
            S.add("sp", lambda e: e.dma_start(out=nks[l][:, 0:127, :], in_=ck[l][:, 1:128, :]), writes=[("nks", l, 0)],
                  dma=True)
            S.add("sp", lambda e: e.dma_start(out=nvs[l][:, 0:127, :], in_=cv[l][:, 1:128, :]), writes=[("nvs", l, 0)],
                  dma=True)
            S.add("sp", lambda e: e.dma_start(out=nks[l][:, 127, :], in_=zs32[:, 512:640]), reads=zsr,
                  writes=[("nks", l, 1)], dma=True)
            S.add("sp", lambda e: e.dma_start(out=nvs[l][:, 127, :], in_=zs32[:, 640:768]), reads=zsr,
                  writes=[("nvs", l, 1)], dma=True)
            S.add("sp", lambda e: e.dma_start(out=qs[:], in_=zq[l].rearrange("b (h d) -> (b h) d", h=8)),
                  reads=[("zq", l)], writes=["qs"], dma=True)
            kview = zk[l].rearrange("b (k d) -> b k d", k=2).unsqueeze(2).broadcast_to([NSAMP, 2, 4, 64])
            vview = zv[l].rearrange("b (k d) -> b k d", k=2).unsqueeze(2).broadcast_to([NSAMP, 2, 4, 64])
            S.add("sp", lambda e: e.dma_start(out=kn[:], in_=kview), reads=[("zk", l)], writes=["kn"], dma=True)
            S.add("sp", lambda e: e.dma_start(out=vn[:], in_=vview), reads=[("zv", l)], writes=["vn"], dma=True)
            S.add("act", lambda e: e.copy(out=qsb[:], in_=qs[:]), reads=["qs"], writes=["qsb"])
            for qi in range(128 // KQ):
                S.add("pool", lambda e, qi=qi: e.dma_start(out=Kt[:], in_=ckr[l][:, qi * KQ:(qi + 1) * KQ, :]),
                      writes=["Kt"], dma=True)
                S.add("dve", lambda e: e.tensor_tensor(out=tmpS[:], in0=Kt[:],
                                                       in1=qsb[:, :].unsqueeze(1).broadcast_to([128, KQ, 64]),
                                                       op=ALU.mult), reads=["Kt", "qsb"], writes=["tmpS"])
                S.add("dve", lambda e, qi=qi: e.tensor_reduce(out=scS[:, qi * KQ:(qi + 1) * KQ], in_=tmpS[:], axis=AX.X,
                                                              op=ALU.add), reads=["tmpS"], writes=[("scS", qi)])
            S.add("dve", lambda e: e.tensor_tensor(out=oS2[:], in0=kn[:], in1=qs[:], op=ALU.mult),
                  reads=["kn", "qs"], writes=["oS2"])
            S.add("dve", lambda e: e.tensor_reduce(out=scS[:, 128:129], in_=oS2[:], axis=AX.X, op=ALU.add),
                  reads=["oS2"], writes=[("scS", 128 // KQ)])
            scr_ = [("scS", i) for i in range(128 // KQ + 1)]
            S.add("dve", lambda e: e.scalar_tensor_tensor(out=scS[:, 0:129], in0=scS[:, 0:129], scalar=0.125,
                                                          in1=biasS[:, 0:129], op0=ALU.mult, op1=ALU.add),
                  reads=scr_ + ["cs"], writes=scr_)
            S.add("dve", lambda e: e.tensor_reduce(out=smS[:, 0:1], in_=scS[:, 0:129], axis=AX.X, op=ALU.max),
                  reads=scr_, writes=["smS0"])
            S.add("dve", lambda e: e.tensor_tensor(out=smS[:, 0:1], in0=smS[:, 0:1], in1=sinkS[:], op=ALU.max),
                  reads=["smS0", "sinkS"], writes=["smS0"])
            S.add("dve", lambda e: e.tensor_scalar(out=smS[:, 1:2], in0=smS[:, 0:1], scalar1=-1.0, scalar2=None,
                                                   op0=ALU.mult), reads=["smS0"], writes=["smS1"])
            S.add("dve", lambda e: e.tensor_tensor(out=smS[:, 2:3], in0=sinkS[:], in1=smS[:, 0:1], op=ALU.subtract),
                  reads=["smS0", "sinkS"], writes=["smS2"])
            S.add("act", lambda e: e.activation(out=smS[:, 2:3], in_=smS[:, 2:3], func=AF.Exp), reads=["smS2"],
                  writes=["smS2"])
            S.add("act", lambda e: e.activation(out=pS[:, 0:129], in_=scS[:, 0:129], func=AF.Exp, bias=smS[:, 1:2],
                                                scale=1.0), reads=scr_ + ["smS1"], writes=["pS"])
            S.add("dve", lambda e: e.tensor_reduce(out=smS[:, 3:4], in_=pS[:, 0:129], axis=AX.X, op=ALU.add),
                  reads=["pS"], writes=["smS3"])
            S.add("dve", lambda e: e.tensor_tensor(out=smS[:, 4:5], in0=smS[:, 3:4], in1=smS[:, 2:3], op=ALU.add),
                  reads=["smS3", "smS2"], writes=["smS4"])
            S.add("dve", lambda e: e.reciprocal(out=smS[:, 4:5], in_=smS[:, 4:5]), reads=["smS4"], writes=["smS4"])
            S.add("act", lambda e: e.copy(out=pSb[:, 0:128], in_=pS[:, 0:128]), reads=["pS"], writes=["pSb"])
            S.add("dve", lambda e: e.tensor_scalar(out=oS[:], in0=vn[:], scalar1=pS[:, 128:129], scalar2=None,
                                                   op0=ALU.mult), reads=["vn", "pS"], writes=["oS"])
            for qi in range(128 // KQ):
                S.add("pool", lambda e, qi=qi: e.dma_start(out=Vt[:], in_=cvr[l][:, qi * KQ:(qi + 1) * KQ, :]),
                      writes=["Vt"], dma=True)
                S.add("dve", lambda e, qi=qi: e.tensor_tensor(
                    out=tmpS[:], in0=Vt[:], in1=pSb[:, qi * KQ:(qi + 1) * KQ].unsqueeze(2).broadcast_to([128, KQ, 64]),
                    op=ALU.mult), reads=["Vt", "pSb"], writes=["tmpS"])
                S.add("dve", lambda e: e.tensor_reduce(out=oS2[:], in_=tmpS[:, :, :].rearrange("p s d -> p d s"),
                                                       axis=AX.X, op=ALU.add), reads=["tmpS"], writes=["oS2"])
                S.add("dve", lambda e: e.tensor_tensor(out=oS[:], in0=oS[:], in1=oS2[:], op=ALU.add),
                      reads=["oS", "oS2"], writes=["oS"])
            S.add("dve", lambda e: e.tensor_scalar(out=oS[:], in0=oS[:], scalar1=smS[:, 4:5], scalar2=None, op0=ALU.mult),
                  reads=["oS", "smS4"], writes=["oS"])
            S.add("sp", lambda e: e.dma_start(out=oscr[l], in_=oS[:]), reads=["oS"], writes=[("oscr", l)], dma=True)
            S.add("sp", lambda e: e.dma_start(out=oa32[:], in_=oscr[l].rearrange("(b h) d -> b (h d)", h=8)),
                  reads=[("oscr", l)], writes=["oa32"], dma=True)
            S.add("act", lambda e: e.copy(out=oab[0:np_, :], in_=oa32[:]), reads=["oa32"], writes=["oab"])
            transpose_to(oab, np_, 4, (lambda: oaT[:, :, 0:np_]), ["oab"], [("oaT", 0)])
            qrr = [("qr32", 0), ("qr32", 256)]
            krr = [("kr32", 0), ("kr32", 256)]
            vrr = [("vr32", o) for o in VR_ATOMS]
            S.add("dve", lambda e: e.tensor_tensor(out=tmq[:], in0=qr32[:], in1=kr32[:], op=ALU.mult), reads=qrr + krr,
                  writes=["tmq"])
            S.add("dve", lambda e: e.tensor_reduce(out=qk[:], in_=tmq[:, :].rearrange("b (h d) -> b h d", h=4), axis=AX.X,
                                                   op=ALU.add), reads=["tmq"], writes=["qk"])
            S.add("dve", lambda e: e.tensor_scalar(out=qk[:], in0=qk[:], scalar1=128 ** -0.5, scalar2=None, op0=ALU.mult),
                  reads=["qk"], writes=["qk"])
            for h in range(4):
                S.add("pool", lambda e, h=h: e.tensor_tensor(
                    out=qsel[:, h, :, :], in0=eyeb,
                    in1=qrT32[:, h, :].unsqueeze(1).broadcast_to([128, NSAMP, NSAMP]), op=ALU.mult),
                    reads=[("qrT32", h), "cs"], writes=[("qsel", h)])
            for b in range(NSAMP):
                S.add("sp", lambda e, b=b: e.dma_start(out=S0, in_=stt[l, b].rearrange("h p v -> p h v")),
                      writes=["S0", ("S32", 0)], dma=True)

                def mmc(e, b=b):
                    ins = None
                    if b == 0:
                        for j in range(2):
                            e.matmul(ps[j][0:np_, :], lhsT=zt[:, :], rhs=S0[:, 2 * j:2 * j + 2, :], start=True,
                                     stop=False, skip_group_check=True)
                    for h in range(4):
                        dst = ps[h // 2][0:np_, (h % 2) * 256:(h % 2 + 1) * 256]
                        ins = e.matmul(dst, lhsT=qsel[:, h, b, :], rhs=S0[:, h, :], start=False,
                                       stop=(b == NSAMP - 1), skip_group_check=True)
                    return ins
                S.add("pe", mmc, reads=["S0", "zt"] + [("qsel", h) for h in range(4)], writes=[PS(0), PS(1)])
                S.add("pool", lambda e, b=b: e.tensor_scalar(
                    out=vdiag[:, :, :], in0=vr32[:, :].rearrange("b (h v) -> b h v", h=4),
                    scalar1=cs[0:NSAMP, coff["eye16"] + b:coff["eye16"] + b + 1], scalar2=128 ** -0.5, op0=ALU.mult,
                    op1=ALU.mult), reads=vrr + ["cs"], writes=["vdiag"])

                def mmu(e):
                    ins = None
                    for h in range(4):
                        dst = ps[2 + h // 2][:, (h % 2) * 256:(h % 2 + 1) * 256]
                        ins = e.matmul(dst, lhsT=kr32[:, h * 128:(h + 1) * 128], rhs=vdiag[:, h, :], start=True,
                                       stop=True)
                    return ins
                S.add("pe", mmu, reads=krr + ["vdiag"], writes=[PS(2), PS(3)])
                for h in range(4):
                    S.add("dve", lambda e, h=h: e.scalar_tensor_tensor(
                        out=S1[:, h, :], in0=S0[:, h, :], scalar=consts["gamma"][h],
                        in1=ps[2 + h // 2][:, (h % 2) * 256:(h % 2 + 1) * 256], op0=ALU.mult, op1=ALU.add),
                        reads=["S0", PS(2 + h // 2)], writes=["S1", ("S32", 1)])
                S.add("sp", lambda e, b=b: e.dma_start(out=nrs[l, b].rearrange("h p v -> p h v"), in_=S1),
                      reads=["S1"], writes=[("nrs", l, b)], dma=True)
            for h in range(4):
                S.add("dve", lambda e, h=h: e.tensor_scalar(
                    out=osam[:, h * 256:(h + 1) * 256], in0=ps[h // 2][0:np_, (h % 2) * 256:(h % 2 + 1) * 256],
                    scalar1=consts["gamma"][h], scalar2=None, op0=ALU.mult),
                    reads=[PS(h // 2)], writes=[("osam", h)])
                S.add("dve", lambda e, h=h: e.scalar_tensor_tensor(
                    out=osam[:, h * 256:(h + 1) * 256], in0=vr32[:, h * 256:(h + 1) * 256], scalar=qk[:, h:h + 1],
                    in1=osam[:, h * 256:(h + 1) * 256], op0=ALU.mult, op1=ALU.add),
                    reads=vrr + ["qk", ("osam", h)], writes=[("osam", h)])
            if STAGE == 15:
                S.add("sp", lambda e: e.dma_start(out=ys[0:NSAMP, :], in_=osam[:, :]),
                      reads=[("osam", h) for h in range(4)], writes=[("yout", 0)], dma=True)
                return
            gn_block(np_, l, 0, lambda j: osam[:, j * 512:(j + 1) * 512].rearrange("p (a v) -> p a v", a=2),
                     lambda h: osam[:, h * 256:(h + 1) * 256], [("osam", h) for h in range(4)])
            mix_out(1, np_, l)

        def load_x(src_fn, nb, np_):
            for b in range(nb):
                npb = 128 if b < nb - 1 else np_
                S.add("sp", lambda e, b=b, npb=npb: e.dma_start(out=x32[0:npb, b, :], in_=src_fn(b, npb)),
                      writes=[("x32", b)], dma=True)
                S.add("act", lambda e, b=b, npb=npb: e.copy(out=xb[0:npb, :], in_=x32[0:npb, b, :]),
                      reads=[("x32", b)], writes=["xb"])
                transpose_to(xb, npb, KD, (lambda b=b, npb=npb: xT[:, :, b * 128:b * 128 + npb]), ["xb"], [("xT", b)])

        def tile_body():
            for k_ in rr:
                rr[k_] = 0
            load_x(lambda b, npb: xp[bass.ds(S.loopvar["sp"] * TT + b * 128, 128), :], NB, 128)
            S.add("sp", lambda e: e.dma_start(out=bias0s[:], in_=bias0[bass.ds(S.loopvar["sp"] * 128, 128), :]),
                  writes=["bias0s"], dma=True)
            for l in range(L):
                ffn(NB, 128, l, 0)
                mix_prompt(l, 1, True)
                lo = None
                if l == L - 1:
                    lo = (lambda b: yp[bass.ds(S.loopvar["sp"] * TT + b * 128, 128), :])
                ffn(NB, 128, l, 1, last_out=lo)

        S.seg = "A"
        tile_body()
        S.seg = "B"
        tile_body()
        S.seg = "post"
        for k_ in rr:
            rr[k_] = 0
        if with_sample:
            load_x(lambda b, npb: xs[0:npb, :], 1, NSAMP)
            for l in range(L):
                ffn(1, NSAMP, l, 0)
                mix_sample(l)
                lo = None
                if l == L - 1:
                    lo = (lambda b: ys[0:NSAMP, :])
                ffn(1, NSAMP, l, 1, last_out=lo)

        S.emit(nc, NT)
    return nc, cpack


def prep_core_inputs(c, inp, cpack, T, L):
    f = np.float32
    seq = c % 2
    b0 = c * NSAMP
    m = {}
    m["xp"] = np.ascontiguousarray(inp["x_prompt"][seq, :T])
    m["xs"] = np.ascontiguousarray(inp["x_sample"][b0:b0 + NSAMP, 0])
    m["wall"] = WALL_CACHE["wall"]
    g = np.stack([inp["ln1_g"][:L], inp["ln2_g"][:L], inp["ln3_g"][:L]], axis=1)
    b = np.stack([inp["ln1_b"][:L], inp["ln2_b"][:L], inp["ln3_b"][:L]], axis=1)
    m["lng"] = np.ascontiguousarray(np.broadcast_to(g[:, :, None, :], (L, 3, 128, D))).astype(f)
    m["lnb"] = np.ascontiguousarray(np.broadcast_to(b[:, :, None, :], (L, 3, 128, D))).astype(f)
    m["gng"] = np.ascontiguousarray(np.broadcast_to(inp["ret_gn_g"][:L, None, :], (L, 128, D))).astype(f)
    m["sinkb"] = np.ascontiguousarray(np.broadcast_to(inp["attn_sinks"][:L, None, :], (L, 128, 8))).astype(f)
    m["sinkbh"] = np.ascontiguousarray(np.tile(inp["attn_sinks"][:L], (1, NSAMP)).reshape(L, 128, 1)).astype(f)
    m["cst"] = cpack
    cc = make_consts()
    nt = T // TT
    bz = np.tile(cc["biasR"][:, :, 0:128].reshape(1, 128, 1024), (nt, 1, 1))
    bz[0] = -1e30
    m["bias0"] = bz.reshape(nt * 128, 1024)
    ckc = inp["cache_win_k"][:L, b0:b0 + NSAMP]
    cvc = inp["cache_win_v"][:L, b0:b0 + NSAMP]
    m["ckr"] = np.ascontiguousarray(np.repeat(np.transpose(ckc, (0, 1, 3, 2, 4)), 4, axis=2).reshape(L, 128, 128, 64))
    m["cvr"] = np.ascontiguousarray(np.repeat(np.transpose(cvc, (0, 1, 3, 2, 4)), 4, axis=2).reshape(L, 128, 128, 64))
    m["ck"] = np.ascontiguousarray(ckc.reshape(L, NSAMP, 128, 128))
    m["cv"] = np.ascontiguousarray(cvc.reshape(L, NSAMP, 128, 128))
    m["stt"] = np.ascontiguousarray(inp["state_ret"][:L, b0:b0 + NSAMP])
    return {k: np.ascontiguousarray(v, dtype=f) for k, v in m.items()}


WALL_CACHE = {}


def tile_weights(inp, L):
    f = np.float32
    out = np.zeros((L, 52, 128, 4096), f)

    def img(sub):
        nk = sub.shape[0] // 128
        nc_ = sub.shape[1]
        a = np.zeros((128, 8, 512), f)
        a[:, :nk, :nc_] = sub.reshape(nk, 128, nc_).transpose(1, 0, 2)
        return a.reshape(128, 4096)

    pieces = [(0, 8), (8, 16), (16, 22)]
    for l in range(L):
        wi = inp["w_in"][l]
        win = np.concatenate([wi, wi[:, 576:640], wi[:, 512:576]], axis=1)
        for nm, base in (("w_ff1_gu", 0), ("w_ff2_gu", 35)):
            w = inp[nm][l]
            for fp in range(11):
                out[l, base + fp] = img(np.concatenate([w[:, fp * 256:(fp + 1) * 256],
                                                        w[:, DFF + fp * 256:DFF + (fp + 1) * 256]], axis=1))
        for nm, base in (("w_ff1_dn", 11), ("w_ff2_dn", 46)):
            w = inp[nm][l]
            for half in range(2):
                for pc, (f0, f1) in enumerate(pieces):
                    out[l, base + half * 3 + pc] = img(w[f0 * 128:f1 * 128, half * 512:(half + 1) * 512])
        for g in range(12):
            out[l, 17 + g] = img(win[:, g * 512:min((g + 1) * 512, INC2)])
        for half in range(2):
            out[l, 29 + half] = img(inp["w_br_a"][l][:, half * 512:(half + 1) * 512])
            out[l, 31 + half] = img(inp["w_br_r"][l][:, half * 512:(half + 1) * 512])
            out[l, 33 + half] = img(inp["w_o"][l][:, half * 512:(half + 1) * 512])
    return out


def run(inp, T, L, with_sample=True):
    nc, cpack = build(T, L, with_sample)
    WALL_CACHE["wall"] = tile_weights(inp, L)
    in_maps = [prep_core_inputs(c, inp, cpack, T, L) for c in range(8)]
    res = run_bass_kernel_spmd(nc, in_maps, core_ids=list(range(8)))
    r = res.results
    y_p = np.stack([r[0]["yp"], r[1]["yp"]], 0)
    y_s = np.concatenate([r[c]["ys"] for c in range(8)], 0)[:, None, :]
    nk_p = np.stack([r[0]["nkp"], r[1]["nkp"]], 1).reshape(L, 2, 128, 2, 64)
    nv_p = np.stack([r[0]["nvp"], r[1]["nvp"]], 1).reshape(L, 2, 128, 2, 64)
    nr_p = np.stack([r[0]["nrp"], r[1]["nrp"]], 1)
    nk_s = np.concatenate([r[c]["nks"] for c in range(8)], 1).reshape(L, 128, 128, 2, 64)
    nv_s = np.concatenate([r[c]["nvs"] for c in range(8)], 1).reshape(L, 128, 128, 2, 64)
    nr_s = np.concatenate([r[c]["nrs"] for c in range(8)], 1)
    return tuple(np.ascontiguousarray(a, dtype=np.float32) for a in (y_p, y_s, nk_p, nv_p, nr_p, nk_s, nv_s, nr_s))


def kernel(**inputs):
    inp = {k: np.asarray(v) for k, v in inputs.items()}
    return run(inp, 8192, 4, True)
```
